# Optimizing a Trainium2 kernel written in Bass

```python
import math
import jax, jax.numpy as jnp
from jax import lax
import numpy as np

D_MODEL = 1024
BATCH = 32
SEQ = 256
DEPTH = 2
DEC_BATCH = 2
DEC_SEQ = 2048
PAST_LEN = 256

GRID_W = 64
N_HEADS = 16
HEAD_DIM = D_MODEL // N_HEADS
WIN_ROWS = 8
WIN_COLS = 16
Q_COLS = 16
K_COLS = 32
CTX_Q_BLOCK = 128
D_FF = 2816
N_MIXERS = 2
N_ATTN_LAYERS = (DEPTH + 1) // 2
N_HYENA_LAYERS = DEPTH // 2
N_MOD = 9
HY_ORDER = 2
HY_DIRS = 2
HY_SHORT = 3
HY_BANDS = 16
HY_EMB = 1 + 2 * HY_BANDS
HY_FILTER_W = 64
HY_FAST_DECAY = 0.3
HY_SLOW_DECAY = 1.5
HY_TARGET = 1e-2
ALPHA = (2 * DEPTH) ** 0.25
BETA = (8 * DEPTH) ** -0.25
LN_EPS = 1e-5

kernel_name = "hybrid_natten_hyena_diffusion_step"


def layer_norm(x, g, b):
    xf = x.astype(jnp.float32)
    mu = xf.mean(-1, keepdims=True)
    var = jnp.square(xf - mu).mean(-1, keepdims=True)
    return ((xf - mu) * lax.rsqrt(var + LN_EPS) * g + b).astype(x.dtype)


def modulate(x, shift, scale):
    return x * (1.0 + scale) + shift


def swiglu(h, w_in, w_out):
    g, u = jnp.split(h @ w_in, 2, axis=-1)
    return (jax.nn.silu(g) * u) @ w_out


def attention_context(h, w_qkv, w_o):
    B, S, _ = h.shape
    q, k, v = [t.reshape(B, S, N_HEADS, HEAD_DIM) for t in jnp.split(h @ w_qkv, 3, axis=-1)]
    qb = jnp.moveaxis((q * HEAD_DIM ** -0.5).reshape(B, S // CTX_Q_BLOCK, CTX_Q_BLOCK, N_HEADS, HEAD_DIM), 1, 0)

    def block(q_blk):
        s = jnp.einsum('bqhd,bkhd->bhqk', q_blk, k).astype(jnp.float32)
        p = jax.nn.softmax(s, axis=-1).astype(v.dtype)
        return jnp.einsum('bhqk,bkhd->bqhd', p, v)

    o = jnp.moveaxis(lax.map(block, qb), 0, 1).reshape(B, S, D_MODEL)
    return o @ w_o, k, v


def attention_latent(h, k_ctx, v_ctx, w_qkv, w_o, rpb):
    B, L, _ = h.shape
    rows = L // GRID_W
    kr = min(WIN_ROWS, rows)
    n_cb = GRID_W // Q_COLS
    q, k, v = [t.reshape(B, rows, GRID_W, N_HEADS, HEAD_DIM) for t in jnp.split(h @ w_qkv, 3, axis=-1)]
    q = q * HEAD_DIM ** -0.5
    qcol = np.arange(GRID_W).reshape(n_cb, Q_COLS)
    cstart = np.clip(qcol - WIN_COLS // 2, 0, GRID_W - WIN_COLS)
    kstart = np.clip(cstart[:, 0], 0, GRID_W - K_COLS)
    kcol = kstart[:, None] + np.arange(K_COLS)
    col_ok = (kcol[:, None, :] >= cstart[:, :, None]) & (kcol[:, None, :] < cstart[:, :, None] + WIN_COLS)
    dcol = np.clip(kcol[:, None, :] - qcol[:, :, None], 1 - WIN_COLS, WIN_COLS - 1) + WIN_COLS - 1
    bias_col = rpb[:, :, dcol]
    mask = jnp.asarray(col_ok)[None, None, :, :, None, :]
    n_lat = kr * K_COLS

    def one_row(args):
        r, q_r = args
        rs = jnp.clip(r - kr // 2, 0, rows - kr)
        k_blk = lax.dynamic_slice_in_dim(k, rs, kr, axis=1)[:, :, kcol]
        v_blk = lax.dynamic_slice_in_dim(v, rs, kr, axis=1)[:, :, kcol]
        drow = rs + jnp.arange(kr) - r + WIN_ROWS - 1
        bias = bias_col[:, drow].transpose(0, 2, 3, 1, 4)
        q_b = q_r.reshape(B, n_cb, Q_COLS, N_HEADS, HEAD_DIM)
        s_lat = jnp.einsum('bnqhd,brnkhd->bhnqrk', q_b, k_blk).astype(jnp.float32) + bias
        s_lat = jnp.where(mask, s_lat, -jnp.inf).reshape(B, N_HEADS, n_cb, Q_COLS, n_lat)
        s_ctx = jnp.einsum('bnqhd,bshd->bhnqs', q_b, k_ctx).astype(jnp.float32)
        p = jax.nn.softmax(jnp.concatenate([s_lat, s_ctx], axis=-1), axis=-1).astype(v.dtype)
        p_lat = p[..., :n_lat].reshape(B, N_HEADS, n_cb, Q_COLS, kr, K_COLS)
        o = (jnp.einsum('bhnqrk,brnkhd->bnqhd', p_lat, v_blk)
             + jnp.einsum('bhnqs,bshd->bnqhd', p[..., n_lat:], v_ctx))
        return o.reshape(B, GRID_W, D_MODEL)

    o = lax.map(one_row, (jnp.arange(rows), jnp.moveaxis(q, 1, 0)))
    return jnp.moveaxis(o, 0, 1).reshape(B, L, D_MODEL) @ w_o


def short_conv(u, w, b):
    up = jnp.pad(u, ((0, 0), (1, 1), (0, 0)))
    return up[:, :-2] * w[0] + up[:, 1:-1] * w[1] + up[:, 2:] * w[2] + b


def hyena_filters(L, f_w1, f_b1, f_freq1, f_w2, f_b2, f_freq2, f_w3, f_b3):
    pos = jnp.arange(L, dtype=jnp.float32)[:, None]
    t = pos / max(L - 1, 1)
    bands = jnp.linspace(1e-4, HY_BANDS - 1, HY_BANDS, dtype=jnp.float32)
    ang = 2.0 * math.pi * pos / L * bands
    z = jnp.concatenate([t, jnp.cos(ang), -jnp.sin(ang)], axis=-1)
    hid = jnp.sin(f_freq1 * (z @ f_w1 + f_b1))
    hid = jnp.sin(f_freq2 * (hid @ f_w2 + f_b2))
    filt = (hid @ f_w3 + f_b3).reshape(L, HY_ORDER, HY_DIRS, D_MODEL).astype(jnp.float32)
    deltas = jnp.abs(jnp.linspace(math.log(HY_TARGET) / HY_SLOW_DECAY, math.log(HY_TARGET) / HY_FAST_DECAY,
                                  D_MODEL, dtype=jnp.float32))
    return filt * jnp.exp(-t.reshape(L, 1, 1, 1) * deltas)


def long_conv(z, h_fwd, h_bwd, bias):
    L = z.shape[1]
    k = jnp.concatenate([h_fwd, jnp.zeros_like(h_fwd[:1]), h_bwd[:0:-1]], axis=0)
    zf = z.astype(jnp.float32)
    y = jnp.fft.irfft(jnp.fft.rfft(zf, n=2 * L, axis=1) * jnp.fft.rfft(k, n=2 * L, axis=0)[None],
                      n=2 * L, axis=1)[:, :L]
    return (y + zf * bias).astype(z.dtype)


def hyena(h, w_in, conv_w, conv_b, f_w1, f_b1, f_freq1, f_w2, f_b2, f_freq2, f_w3, f_b3, bias, w_out):
    L = h.shape[1]
    v, x1, x2 = jnp.split(short_conv(h @ w_in, conv_w, conv_b), 3, axis=-1)
    filt = hyena_filters(L, f_w1, f_b1, f_freq1, f_w2, f_b2, f_freq2, f_w3, f_b3)
    z = long_conv(v, filt[:, 0, 0], filt[:, 0, 1], bias[0]) * x1
    z = long_conv(z, filt[:, 1, 0], filt[:, 1, 1], bias[1]) * x2
    return z @ w_out


def trunk_layer(x, cond, mod_w_i, mod_b_i, ln_g_i, ln_b_i, ffn_w_in_i, ffn_w_out_i, mixer):
    sh1, sc1, g1, sh2, sc2, g2, sh3, sc3, g3 = jnp.split(jax.nn.silu(cond) @ mod_w_i + mod_b_i, N_MOD, axis=-1)
    x = layer_norm(ALPHA * x + 0.5 * g1 * swiglu(modulate(x, sh1, sc1), ffn_w_in_i[0], ffn_w_out_i[0]),
                   ln_g_i[0], ln_b_i[0])
    mix, extras = mixer(modulate(x, sh2, sc2))
    x = layer_norm(ALPHA * x + g2 * mix, ln_g_i[1], ln_b_i[1])
    x = layer_norm(ALPHA * x + 0.5 * g3 * swiglu(modulate(x, sh3, sc3), ffn_w_in_i[1], ffn_w_out_i[1]),
                   ln_g_i[2], ln_b_i[2])
    return x, extras


def setup_inputs(seed: int = 0) -> dict:
    key = jax.random.key(seed)
    ks = jax.random.split(key, 28)
    D = D_MODEL
    hd = HY_ORDER * HY_DIRS * D_MODEL

    def nrm(k, shape, s):
        return jax.random.normal(k, shape, jnp.float32) * s

    return {
        "x_prompt": nrm(ks[0], (BATCH, SEQ, D), 1.0),
        "x_sample": nrm(ks[1], (DEC_BATCH, DEC_SEQ, D), 1.0),
        "cache_k": nrm(ks[2], (DEC_BATCH, N_ATTN_LAYERS, PAST_LEN, N_HEADS, HEAD_DIM), 1.0),
        "cache_v": nrm(ks[3], (DEC_BATCH, N_ATTN_LAYERS, PAST_LEN, N_HEADS, HEAD_DIM), 1.0),
        "c": nrm(ks[4], (DEC_BATCH, D), 1.0),
        "c_ctx": nrm(ks[5], (D,), 1.0),
        "mod_w": nrm(ks[6], (DEPTH, D, N_MOD * D), 0.5 * D ** -0.5),
        "mod_b": nrm(ks[7], (DEPTH, N_MOD * D), 0.01),
        "ln_g": 1.0 + nrm(ks[8], (DEPTH, 3, D), 0.05),
        "ln_b": nrm(ks[9], (DEPTH, 3, D), 0.02),
        "ffn_w_in": nrm(ks[10], (DEPTH, 2, D, 2 * D_FF), D ** -0.5),
        "ffn_w_out": nrm(ks[11], (DEPTH, 2, D_FF, D), BETA * D_FF ** -0.5),
        "attn_w_qkv": nrm(ks[12], (N_ATTN_LAYERS, D, 3 * D), D ** -0.5),
        "attn_w_o": nrm(ks[13], (N_ATTN_LAYERS, D, D), BETA * D ** -0.5),
        "attn_rpb": nrm(ks[14], (N_ATTN_LAYERS, N_HEADS, 2 * WIN_ROWS - 1, 2 * WIN_COLS - 1), 0.1),
        "hy_w_in": nrm(ks[15], (N_HYENA_LAYERS, D, 3 * D), D ** -0.5),
        "hy_conv_w": nrm(ks[16], (N_HYENA_LAYERS, HY_SHORT, 3 * D), HY_SHORT ** -0.5),
        "hy_conv_b": nrm(ks[17], (N_HYENA_LAYERS, 3 * D), 0.02),
        "hy_f_w1": nrm(ks[18], (N_HYENA_LAYERS, HY_EMB, HY_FILTER_W), HY_EMB ** -0.5),
        "hy_f_b1": nrm(ks[19], (N_HYENA_LAYERS, HY_FILTER_W), 0.1),
        "hy_f_freq1": 1.0 + nrm(ks[20], (N_HYENA_LAYERS, HY_FILTER_W), 0.05),
        "hy_f_w2": nrm(ks[21], (N_HYENA_LAYERS, HY_FILTER_W, HY_FILTER_W), HY_FILTER_W ** -0.5),
        "hy_f_b2": nrm(ks[22], (N_HYENA_LAYERS, HY_FILTER_W), 0.1),
        "hy_f_freq2": 1.0 + nrm(ks[23], (N_HYENA_LAYERS, HY_FILTER_W), 0.05),
        "hy_f_w3": nrm(ks[24], (N_HYENA_LAYERS, HY_FILTER_W, hd), 0.1 * HY_FILTER_W ** -0.5),
        "hy_f_b3": nrm(ks[25], (N_HYENA_LAYERS, hd), 0.01),
        "hy_bias": nrm(ks[26], (N_HYENA_LAYERS, HY_ORDER, D), 0.1),
        "hy_w_out": nrm(ks[27], (N_HYENA_LAYERS, D, D), BETA * D ** -0.5),
    }


def reference(x_prompt, x_sample, cache_k, cache_v, c, c_ctx, mod_w, mod_b, ln_g, ln_b, ffn_w_in, ffn_w_out,
              attn_w_qkv, attn_w_o, attn_rpb, hy_w_in, hy_conv_w, hy_conv_b, hy_f_w1, hy_f_b1, hy_f_freq1,
              hy_f_w2, hy_f_b2, hy_f_freq2, hy_f_w3, hy_f_b3, hy_bias, hy_w_out):
    hy_params = (hy_w_in, hy_conv_w, hy_conv_b, hy_f_w1, hy_f_b1, hy_f_freq1, hy_f_w2, hy_f_b2, hy_f_freq2,
                 hy_f_w3, hy_f_b3, hy_bias, hy_w_out)

    cond_ctx = c_ctx[None, None, :]
    y_prompt = x_prompt
    k_new, v_new = [], []
    for i in range(DEPTH):
        j = i // N_MIXERS
        if i % N_MIXERS == 0:
            def mixer(h, j=j):
                out, k_c, v_c = attention_context(h, attn_w_qkv[j], attn_w_o[j])
                return out, (k_c, v_c)
        else:
            def mixer(h, j=j):
                return hyena(h, *[p[j] for p in hy_params]), ()
        y_prompt, extras = trunk_layer(y_prompt, cond_ctx, mod_w[i], mod_b[i], ln_g[i], ln_b[i],
                                       ffn_w_in[i], ffn_w_out[i], mixer)
        if i % N_MIXERS == 0:
            k_new.append(extras[0])
            v_new.append(extras[1])
    new_cache_k = jnp.stack(k_new, axis=1)
    new_cache_v = jnp.stack(v_new, axis=1)

    cond_lat = c[:, None, :]
    y_sample = x_sample
    for i in range(DEPTH):
        j = i // N_MIXERS
        if i % N_MIXERS == 0:
            def mixer(h, j=j):
                return attention_latent(h, cache_k[:, j], cache_v[:, j], attn_w_qkv[j], attn_w_o[j],
                                        attn_rpb[j]), ()
        else:
            def mixer(h, j=j):
                return hyena(h, *[p[j] for p in hy_params]), ()
        y_sample, _ = trunk_layer(y_sample, cond_lat, mod_w[i], mod_b[i], ln_g[i], ln_b[i],
                                  ffn_w_in[i], ffn_w_out[i], mixer)

    return (y_prompt, y_sample, new_cache_k, new_cache_v)
```

```python
import numpy as np
import concourse.bass as bass
import concourse.mybir as mybir
from contextlib import ExitStack

F32 = mybir.dt.float32
BF16 = mybir.dt.bfloat16
I32 = mybir.dt.int32
ALU = mybir.AluOpType
AF = mybir.ActivationFunctionType
AX = mybir.AxisListType

COMPUTE = ("pe", "act", "dve", "pool")
NDMA_SLOTS = 10


class Res:
    __slots__ = ("name", "last_w", "readers")

    def __init__(self, name=""):
        self.name = name
        self.last_w = None
        self.readers = []


class Op:
    __slots__ = ("eng", "idx", "fn", "waits", "needs_inc", "is_dma", "slot", "slot_val", "clock", "cnt")

    def __init__(self, eng, idx, fn, is_dma):
        self.eng = eng
        self.idx = idx
        self.fn = fn
        self.waits = []
        self.needs_inc = False
        self.is_dma = is_dma
        self.slot = None
        self.slot_val = 0
        self.clock = None
        self.cnt = 0


class Prog:
    def __init__(self, nc):
        self.nc = nc
        self.engs = {"pe": nc.tensor, "act": nc.scalar, "dve": nc.vector, "pool": nc.gpsimd, "sp": nc.sync}
        self.ops = {e: [] for e in self.engs}
        self.clock = {e: {f: -1 for f in COMPUTE} for e in self.engs}
        self.known_dma = {e: set() for e in self.engs}
        self.slot_last = {e: [None] * NDMA_SLOTS for e in self.engs}
        self.slot_cnt = {e: [0] * NDMA_SLOTS for e in self.engs}
        self.slot_rr = {e: 0 for e in self.engs}
        self.stack = ExitStack()

    def sb(self, name, shape, dt):
        return self.stack.enter_context(self.nc.sbuf_tensor(name, list(shape), dt))

    def ps(self, name, shape, dt=F32):
        return self.stack.enter_context(self.nc.psum_tensor(name, list(shape), dt))

    def add(self, eng, fn, reads=(), writes=(), dma=False):
        op = Op(eng, len(self.ops[eng]), fn, dma)
        deps = []
        for r in reads:
            if r.last_w is not None:
                deps.append(r.last_w)
        for w in writes:
            if w.last_w is not None:
                deps.append(w.last_w)
            deps.extend(w.readers)
        ck = self.clock[eng]
        kd = self.known_dma[eng]
        if dma:
            s = self.slot_rr[eng]
            self.slot_rr[eng] = (s + 1) % NDMA_SLOTS
            prev = self.slot_last[eng][s]
            if prev is not None:
                deps.append(prev)
            self.slot_cnt[eng][s] += 1
            op.slot = s
            op.slot_val = 16 * self.slot_cnt[eng][s]
            self.slot_last[eng][s] = op
        seen = set()
        for d in deps:
            if id(d) in seen:
                continue
            seen.add(id(d))
            if not d.is_dma and d.eng not in COMPUTE:
                continue
            if d.is_dma:
                if id(d) in kd:
                    continue
                kd.add(id(d))
                op.waits.append(d)
                for f in COMPUTE:
                    if d.clock[f] > ck[f]:
                        ck[f] = d.clock[f]
            else:
                if d.eng == eng and eng == "pe":
                    continue
                if ck[d.eng] >= d.idx:
                    continue
                d.needs_inc = True
                op.waits.append(d)
                for f in COMPUTE:
                    if d.clock[f] > ck[f]:
                        ck[f] = d.clock[f]
                if d.idx > ck[d.eng]:
                    ck[d.eng] = d.idx
        op.clock = dict(ck)
        if eng == "pe":
            pass
        self.ops[eng].append(op)
        for r in reads:
            r.readers.append(op)
        for w in writes:
            w.last_w = op
            w.readers = []
        return op


    def phase(self):
        ph = ExitStack()
        return ph

    def psb(self, ph, name, shape, dt):
        self._uid = getattr(self, "_uid", 0) + 1
        return ph.enter_context(self.nc.sbuf_tensor("%s_%d" % (name, self._uid), list(shape), dt))

    def fence(self, tiny):
        if not hasattr(self, "_fres"):
            self._fres = {e: Res("fence_" + e) for e in COMPUTE}
        outstanding = []
        for e in self.engs:
            for s in range(NDMA_SLOTS):
                if self.slot_last[e][s] is not None:
                    outstanding.append(self.slot_last[e][s])
        dres = Res("fence_dma")
        for e in COMPUTE:
            op = self.add(e, tiny[e], writes=[self._fres[e]])
            for d in outstanding:
                if id(d) not in self.known_dma[e]:
                    self.known_dma[e].add(id(d))
                    op.waits.append(d)
        for e in self.engs:
            if e in COMPUTE:
                self.add(e, tiny[e], reads=[self._fres[x] for x in COMPUTE if x != e], writes=[self._fres[e]])
            else:
                op = self.add(e, lambda eng: None, reads=[self._fres[x] for x in COMPUTE])
                for d in outstanding:
                    if id(d) not in self.known_dma[e]:
                        self.known_dma[e].add(id(d))
                        op.waits.append(d)

    def emit(self, final_waits=()):
        nc = self.nc
        st = self.stack
        sems = {e: st.enter_context(nc.semaphore("s_" + e)) for e in COMPUTE}
        dsems = {e: [st.enter_context(nc.semaphore("d_%s%d" % (e, i))) for i in range(NDMA_SLOTS)]
                 for e in self.engs}
        for e in COMPUTE:
            c = 0
            for op in self.ops[e]:
                if op.needs_inc:
                    c += 1
                op.cnt = c
        block = st.enter_context(nc.Block())
        self.n_waits = 0

        def run(ename, eng):
            for op in self.ops[ename]:
                for d in op.waits:
                    if d.is_dma:
                        eng.wait_ge(dsems[d.eng][d.slot], d.slot_val)
                    else:
                        eng.wait_ge(sems[d.eng], d.cnt)
                    self.n_waits += 1
                ins = op.fn(eng)
                if ins is None:
                    continue
                if op.is_dma:
                    ins.then_inc(dsems[ename][op.slot], 16)
                elif op.needs_inc:
                    ins.then_inc(sems[ename], 1)
            for s in range(NDMA_SLOTS):
                last = self.slot_last[ename][s]
                if last is not None:
                    eng.wait_ge(dsems[ename][s], last.slot_val)

        block.tensor(lambda eng: run("pe", eng))
        block.scalar(lambda eng: run("act", eng))
        block.vector(lambda eng: run("dve", eng))
        block.gpsimd(lambda eng: run("pool", eng))
        block.sync(lambda eng: run("sp", eng))
        st.close()

import ml_dtypes
from concourse.bass_utils import run_bass_kernel_spmd

D = 1024
DFF = 2816
NJ = 22
NH = 16
DH = 64
ALPHA_C = float(4.0 ** 0.25)
LN_EPS = 1e-5
NEG = -30000.0
PI = float(np.pi)


def build(stage=99):
    nc = bass.Bass("TRN2", target_bir_lowering=False)
    P = Prog(nc)

    def din(name, shape, dt=F32):
        return nc.dram_tensor(name, list(shape), dt, kind="ExternalInput").ap()

    def dout(name, shape):
        return nc.dram_tensor(name, list(shape), F32, kind="ExternalOutput").ap()

    def dscr(name, shape, dt=F32):
        return nc.dram_tensor(name, list(shape), dt).ap()

    xp = din("xp", [1024, D]); xs = din("xs", [2048, D])
    ck = din("ck", [256, D]); cv = din("cv", [256, D])
    condT = din("condT", [128, 2, 8])
    mod_w = din("mod_w", [2, 36, 128, 8 * 256]); mod_b = din("mod_b", [2, 9 * D])
    ln_g = din("ln_g", [2, 3, D]); ln_b = din("ln_b", [2, 3, D])
    w_in = din("ffn_w_in", [2, 2, NJ // 2, 128, 8 * 2 * 256]); w_out = din("ffn_w_out", [2, 2, DFF, D])
    w_qkv = din("attn_w_qkv", [D, 3 * D]); w_o = din("attn_w_o", [D, D])
    w_qkv_t = din("attn_w_qkv_t", [24, 128, 8 * 128]); hy_w_in_t = din("hy_w_in_t", [24, 128, 8 * 128])
    btab = din("btab", [NH, 4, 128, 8, 512])
    hy_w_in = din("hy_w_in", [D, 3 * D]); hy_w_out = din("hy_w_out", [D, D])
    hy_cw = din("hy_cw", [128, 24, 3]); hy_cb = din("hy_cb", [128, 24])
    hy_biasT = din("hy_biasT", [128, 8, 2])
    hy_w1 = din("hy_w1", [33, 64]); hy_w2 = din("hy_w2", [64, 64]); hy_w3a = din("hy_w3a", [65, 4096])
    hy_fb = din("hy_fb", [64, 4])
    zT = {256: din("zT256", [33, 256]), 2048: din("zT2048", [33, 2048])}
    decay = {256: din("decay256", [256, 2, D]), 2048: din("decay2048", [2048, 2, D])}
    FP = {256: 384, 2048: 2176}
    TBL = {256: 256, 2048: 512}
    fwd = {L: din("fwd%d" % L, [FP[L] // 128, 128, (L // 128) * 2 * 128], BF16) for L in (256, 2048)}
    inv = {L: din("inv%d" % L, [L // TBL[L], 128, 2 * (FP[L] // 128) * TBL[L]], BF16) for L in (256, 2048)}
    ident_d = din("ident", [128, 128], BF16)
    idx_tok_d = din("idx_tok", [128, 4], I32)
    idx_ch_d = din("idx_ch", [128, 8], I32)
    inv_own = din("inv_own", [128, 2 * (FP[2048] // 128) * 512], BF16)
    yso = dout("yso", [512, D])
    XDo = dscr("xdo", [512, D]); XDoR = [Res() for _ in range(4)]
    ZDb = dscr("zdb", [4 * D, 512]); ZDbR = Res()
    U2b = dscr("u2b", [4 * D, 512]); U2bR = Res()
    yp = dout("yp", [1024, D])
    ys = dout("ys", [2048, D]) if stage != 99 else None
    nk = dout("nk", [1024, D]); nv = dout("nv", [1024, D])
    modv = dscr("modv", [2, 3, 2, 3, D]); modvR = Res("modv")
    XDp = dscr("xdp", [1024, D]); XDs = dscr("xds", [2048, D])
    XDpR = [Res() for _ in range(8)]; XDsR = [Res() for _ in range(16)]
    KTD = dscr("ktd", [128, 8, 2304], BF16); KTDR = Res()
    VD = dscr("vd", [2304, D], BF16); VDR = Res()
    UD = dscr("ud", [3, D, 2048]); UDR = Res()
    ZD = dscr("zd", [D, 2048]); ZDR = Res()
    ZFD = dscr("zfd", [D, 2048], BF16); ZFDR = Res()
    KD = {L: dscr("kd%d" % L, [2, 2 * FP[L], D]) for L in (256, 2048)}; KDR = Res()

    MOD = P.sb("mod", [128, 3, D], F32); MODR = Res("mod")
    LNGB = P.sb("lngb", [128, 2, D], F32); LNGBR = Res("lngb")
    ident_bf = P.sb("ident_bf", [128, 128], BF16); identR = Res("ident")
    ones_bf = P.sb("ones_bf", [128, 128], BF16); onesR = Res("ones")
    EPS = P.sb("eps", [128, 1], F32); epsR = Res("eps")
    XB = [P.sb("xb%d" % i, [128, D], BF16) for i in range(2)]; XBR = [Res() for _ in range(2)]
    TMPF = [P.sb("tmpf%d" % i, [128, D], F32) for i in range(2)]; TMPFR = [Res() for _ in range(2)]
    ST = [P.sb("st%d" % i, [128, 2, 6], F32) for i in range(2)]
    MV = [P.sb("mv%d" % i, [128, 4], F32) for i in range(2)]
    STR = [Res() for _ in range(2)]
    SCR = {e: P.sb("scr_" + e, [128, 2], F32) for e in ("act", "dve", "pool")}
    PB = [P.ps("pb%d" % i, [128, 512], F32) for i in range(8)]
    PBR = [Res("pb%d" % i) for i in range(8)]
    rot = {"xb": 0, "st": 0, "pt": 0, "nb": 0}

    def nb():
        rot["nb"] = (rot["nb"] + 1) % 6
        return 2 + rot["nb"]

    P.add("sp", lambda e: e.dma_start(out=ident_bf[:], in_=ident_d), writes=[identR], dma=True)
    P.add("dve", lambda e: e.memset(EPS[:], LN_EPS), writes=[epsR])
    P.add("dve", lambda e: e.memset(ones_bf[:], 1.0), writes=[onesR])
    P.add("act", lambda e: e.activation(out=SCR["act"][:, 0:2], in_=EPS[:, 0:1].to_broadcast([128, 2]),
                                        func=AF.Identity), reads=[epsR])
    tiny = {
        "pe": lambda e: e.matmul(PB[7][0:1, 0:1], lhsT=ident_bf[:, 0:1], rhs=ident_bf[:, 0:1], start=True, stop=True),
        "act": lambda e: e.activation(out=SCR["act"][:, 0:1], in_=SCR["act"][:, 1:2], func=AF.Identity),
        "dve": lambda e: e.memset(SCR["dve"][:, 0:1], 0.0),
        "pool": lambda e: e.memset(SCR["pool"][:, 0:1], 0.0),
    }

    def fence():
        P.add("pe", tiny["pe"], reads=[identR], writes=[PBR[7]])
        P.fence(tiny)

    def endph(ph):
        fence(); ph.close()

    def ldx(XD, XDR, t, buf, bufR):
        P.add("sp", lambda e: e.dma_start(out=buf[:], in_=XD[t * 128:(t + 1) * 128, :]), reads=[XDR[t]],
              writes=[bufR], dma=True)

    def stx(XD, XDR, t, buf, bufR):
        P.add("act", lambda e: e.dma_start(out=XD[t * 128:(t + 1) * 128, :], in_=buf[:]), reads=[bufR],
              writes=[XDR[t]], dma=True)

    cT = P.sb("cT", [128, 2, 8], F32); cTR = Res()
    sT = P.sb("sT", [128, 2, 8], F32)
    s2 = P.sb("s2", [128, 8, 2], BF16); s2R = Res()
    WM = [P.sb("wm%d" % i, [128, 8, 256], BF16) for i in range(2)]; WMR = [Res() for _ in range(2)]
    MRS = [P.sb("mrs%d" % i, [2, 2, 256], F32) for i in range(2)]; MRSR = [Res() for _ in range(2)]
    P.add("sp", lambda e: e.dma_start(out=cT[:], in_=condT), writes=[cTR], dma=True)
    P.add("act", lambda e: e.activation(out=sT[:], in_=cT[:], func=AF.Silu), reads=[cTR], writes=[cTR])
    for r in range(2):
        P.add("dve", lambda e, r=r: e.tensor_copy(out=s2[:, :, r], in_=sT[:, r, :]), reads=[cTR], writes=[s2R])
    mod_tasks = [(l, sb_, ci) for l in range(2) for sb_ in range(3) for ci in range(12)]
    mod_pos = [0]

    def mod_upto(n):
        mod_chunks(max(0, n - mod_pos[0]))

    def mod_chunks(n):
        for _ in range(n):
            if mod_pos[0] >= len(mod_tasks):
                return
            layer, sub, ci = mod_tasks[mod_pos[0]]
            k = mod_pos[0] % 2
            mod_pos[0] += 1
            v, q = ci // 4, ci % 4
            wb, wbR, mr, mrR = WM[k], WMR[k], MRS[k], MRSR[k]
            c0 = (sub * 3 + v) * D + q * 256
            P.add("pool", lambda e, wb=wb, c0=c0, layer=layer: e.dma_start(
                out=wb[:].rearrange("p a b -> p (a b)"), in_=mod_w[layer, c0 // 256]), writes=[wbR], dma=True)
            P.add("sp", lambda e, mr=mr, c0=c0, layer=layer: e.dma_start(
                out=mr[:, 1, :], in_=mod_b[layer, c0:c0 + 256].partition_broadcast(2)), writes=[mrR], dma=True)
            bk = nb()
            for kc in range(8):
                P.add("pe", lambda e, wb=wb, kc=kc, bk=bk: e.matmul(
                    PB[bk][0:2, 0:256], lhsT=s2[:, kc, :], rhs=wb[:, kc, :], start=(kc == 0), stop=(kc == 7)),
                    reads=[s2R, wbR], writes=[PBR[bk]])
            addc = 1.0 if v == 1 else 0.0
            P.add("dve", lambda e, bk=bk, mr=mr, addc=addc: e.scalar_tensor_tensor(
                out=mr[:, 0, :], in0=PB[bk][0:2, 0:256], scalar=addc, in1=mr[:, 1, :], op0=ALU.add, op1=ALU.add),
                reads=[PBR[bk], mrR], writes=[mrR])
            P.add("act", lambda e, mr=mr, layer=layer, sub=sub, v=v, q=q: e.dma_start(
                out=modv[layer, sub, :, v, q * 256:(q + 1) * 256], in_=mr[:, 0, :]),
                reads=[mrR], writes=[modvR], dma=True)

    def load_mod(layer, sub, row):
        P.add("sp", lambda e: e.dma_start(out=MOD[:], in_=modv[layer, sub, row].partition_broadcast(128)),
              reads=[modvR], writes=[MODR], dma=True)
        P.add("sp", lambda e: e.dma_start(out=LNGB[:, 0, :], in_=ln_g[layer, sub].partition_broadcast(128)),
              writes=[LNGBR], dma=True)
        P.add("sp", lambda e: e.dma_start(out=LNGB[:, 1, :], in_=ln_b[layer, sub].partition_broadcast(128)),
              writes=[LNGBR], dma=True)

    def layer_norm(Xt, XtR):
        k = rot["st"]; rot["st"] ^= 1
        st, mv, sR = ST[k], MV[k], STR[k]
        for c in range(2):
            P.add("dve", lambda e, c=c: e.bn_stats(out=st[:, c, :], in_=Xt[:, c * 512:(c + 1) * 512]),
                  reads=[XtR], writes=[sR])
        P.add("dve", lambda e: e.bn_aggr(out=mv[:, 0:2], in_=st[:]), reads=[sR], writes=[sR])
        P.add("act", lambda e: e.activation(out=mv[:, 2:3], in_=mv[:, 1:2], func=AF.Sqrt, bias=EPS[:, 0:1],
                                            scale=1.0), reads=[sR, epsR], writes=[sR])
        P.add("dve", lambda e: e.reciprocal(out=mv[:, 3:4], in_=mv[:, 2:3]), reads=[sR], writes=[sR])
        P.add("dve", lambda e: e.scalar_tensor_tensor(out=mv[:, 2:3], in0=mv[:, 0:1], scalar=-1.0, in1=mv[:, 3:4],
                                                      op0=ALU.mult, op1=ALU.mult), reads=[sR], writes=[sR])
        P.add("act", lambda e: e.activation(out=Xt[:], in_=Xt[:], func=AF.Identity, scale=mv[:, 3:4],
                                            bias=mv[:, 2:3]), reads=[XtR, sR], writes=[XtR])
        P.add("dve", lambda e: e.tensor_tensor(out=Xt[:], in0=Xt[:], in1=LNGB[:, 0, :], op=ALU.mult),
              reads=[XtR, LNGBR], writes=[XtR])
        P.add("dve", lambda e: e.tensor_tensor(out=Xt[:], in0=Xt[:], in1=LNGB[:, 1, :], op=ALU.add),
              reads=[XtR, LNGBR], writes=[XtR])

    def residual_ln(Xt, XtR, bk, n, gscale):
        k = rot["xb"]; rot["xb"] ^= 1
        tf, tfR = TMPF[k], TMPFR[k]
        P.add("dve", lambda e: e.scalar_tensor_tensor(
            out=tf[:, 0:512], in0=PB[bk][:], scalar=gscale, in1=MOD[:, 2, n * 512:(n + 1) * 512],
            op0=ALU.mult, op1=ALU.mult), reads=[PBR[bk], MODR], writes=[tfR])
        P.add("dve", lambda e: e.scalar_tensor_tensor(
            out=Xt[:, n * 512:(n + 1) * 512], in0=Xt[:, n * 512:(n + 1) * 512], scalar=ALPHA_C,
            in1=tf[:, 0:512], op0=ALU.mult, op1=ALU.add), reads=[XtR, tfR], writes=[XtR])

    def transpose_tile(xb, xbR, dst3, dstR, nchunk=8):
        b = rot["pt"]; rot["pt"] ^= 1
        pt = PB[b][:].bitcast(BF16)
        for c in range(nchunk):
            P.add("pe", lambda e, c=c: e.transpose(pt[:, c * 128:(c + 1) * 128], xb[:, c * 128:(c + 1) * 128],
                                                   ident_bf[:]), reads=[xbR, identR], writes=[PBR[b]])
        P.add("act", lambda e: e.activation(out=dst3, in_=pt[:, 0:nchunk * 128].rearrange("p (c t) -> p c t", c=nchunk),
                                            func=AF.Identity), reads=[PBR[b]],
              writes=(dstR if isinstance(dstR, list) else [dstR]))

    def mod_transpose(Xt, XtR, dst3, dstR):
        k = rot["xb"]; rot["xb"] ^= 1
        tf, tfR, xb, xbR = TMPF[k], TMPFR[k], XB[k], XBR[k]
        P.add("dve", lambda e: e.tensor_tensor(out=tf[:], in0=Xt[:], in1=MOD[:, 1, :], op=ALU.mult),
              reads=[XtR, MODR], writes=[tfR])
        P.add("dve", lambda e: e.tensor_tensor(out=xb[:], in0=tf[:], in1=MOD[:, 0, :], op=ALU.add),
              reads=[tfR, MODR], writes=[xbR])
        transpose_tile(xb, xbR, dst3, dstR)

    def proj_fm(ph, tag, hT, hTR, ntok, wsrc, col0, nchunks, evac, wt=None, wbuf=None):
        if wbuf is not None:
            WB, WBR = wbuf
        else:
            WB = [P.psb(ph, "%s_w%d" % (tag, i), [128, 8, 128], BF16) for i in range(3)]
            WBR = [Res() for _ in range(3)]
        nblk = (ntok + 511) // 512
        for mc in range(nchunks):
            wb, wbR = WB[mc % 3], WBR[mc % 3]
            c0 = col0 + mc * 128
            if wt is not None:
                P.add("pool", lambda e, wb=wb, c0=c0: e.dma_start(
                    out=wb[:].rearrange("p a b -> p (a b)"), in_=wt[c0 // 128]), writes=[wbR], dma=True)
            else:
                P.add("pool", lambda e, wb=wb, c0=c0: e.dma_start(
                    out=wb[:], in_=wsrc[:, c0:c0 + 128].rearrange("(kc p) n -> p kc n", p=128)), writes=[wbR], dma=True)
            for blk in range(nblk):
                n = min(512, ntok - blk * 512)
                bk = nb()
                for kc in range(8):
                    P.add("pe", lambda e, wb=wb, kc=kc, bk=bk, blk=blk, n=n: e.matmul(
                        PB[bk][:, 0:n], lhsT=wb[:, kc, :], rhs=hT[:, kc, blk * 512:blk * 512 + n],
                        start=(kc == 0), stop=(kc == 7)), reads=[wbR, hTR], writes=[PBR[bk]])
                evac(mc, blk, n, bk)

    def proj_tm(ph, tag, hT, hTR, ntiles, wsrc, col0, nn, evac):
        WB = [P.psb(ph, "%s_w%d" % (tag, i), [128, 8, 512], BF16) for i in range(2)]; WBR = [Res() for _ in range(2)]
        for n in range(nn):
            wb, wbR = WB[n % 2], WBR[n % 2]
            c0 = col0 + n * 512
            P.add("pool", lambda e, wb=wb, c0=c0: e.dma_start(
                out=wb[:], in_=wsrc[:, c0:c0 + 512].rearrange("(kc p) n -> p kc n", p=128)), writes=[wbR], dma=True)
            for t in range(ntiles):
                bk = nb()
                for kc in range(8):
                    P.add("pe", lambda e, wb=wb, kc=kc, bk=bk, t=t: e.matmul(
                        PB[bk][:], lhsT=hT[:, kc, t * 128:(t + 1) * 128], rhs=wb[:, kc, :],
                        start=(kc == 0), stop=(kc == 7)), reads=[wbR, hTR], writes=[PBR[bk]])
                evac(n, t, bk)

    def ffn(XD, XDR, ntiles, layer, idx, sub, row, dst=None, modn=0, src=None):
        ph = P.phase()
        XMT = P.psb(ph, "xmt", [128, 8, 1024], BF16); XMTR = [Res(), Res()]
        AT = P.psb(ph, "at", [128, NJ, 1024], BF16); ATR = [Res() for _ in range(NJ)]
        WO = P.psb(ph, "wo", [128, NJ, D], BF16); WOR = [Res(), Res()]
        WI = [P.psb(ph, "wi%d" % i, [128, 8, 2, 256], BF16) for i in range(2)]; WIR = [Res() for _ in range(2)]
        SG = [P.psb(ph, "sg%d" % i, [128, 512], F32) for i in range(2)]; SGR = [Res() for _ in range(2)]
        XT = [P.psb(ph, "xt%d" % i, [128, D], F32) for i in range(8)]; XTR = [Res() for _ in range(8)]
        load_mod(layer, sub, row)
        def load_wo():
            for hf in range(2):
                P.add("pool", lambda e, hf=hf: e.dma_start(
                    out=WO[:, hf * 11:(hf + 1) * 11, :],
                    in_=w_out[layer, idx, hf * 1408:(hf + 1) * 1408, :].rearrange("(j p) n -> p j n", p=128)),
                    writes=[WOR[hf]], dma=True)
        wi_src = w_in[layer, idx]
        tpb = min(8, ntiles)
        nsbk = tpb // 4
        for blk in range(ntiles // tpb):
            for ti in range(tpb):
                t = blk * tpb + ti
                if src is not None:
                    P.add("sp", lambda e, t=t, ti=ti: e.dma_start(out=XT[ti][:], in_=src[t * 128:(t + 1) * 128, :]),
                          writes=[XTR[ti]], dma=True)
                else:
                    ldx(XD, XDR, t, XT[ti], XTR[ti])
                mod_transpose(XT[ti], XTR[ti], XMT[:, :, ti * 128:(ti + 1) * 128], XMTR[ti // 4])
            for jg in range(NJ // 2):
                wb, wbR = WI[jg % 2], WIR[jg % 2]
                P.add("pool", lambda e, wb=wb, jg=jg: e.dma_start(
                    out=wb[:].rearrange("p a b c -> p (a b c)"), in_=wi_src[jg]), writes=[wbR], dma=True)
                for jj in range(2):
                    j = jg * 2 + jj
                    for sbk in range(nsbk):
                        bg, bu = nb(), nb()
                        for gu, bk in ((0, bg), (1, bu)):
                            for kc in range(8):
                                P.add("pe", lambda e, wb=wb, gu=gu, kc=kc, bk=bk, sbk=sbk, jj=jj: e.matmul(
                                    PB[bk][:], lhsT=wb[:, kc, gu, jj * 128:(jj + 1) * 128],
                                    rhs=XMT[:, kc, sbk * 512:(sbk + 1) * 512],
                                    start=(kc == 0), stop=(kc == 7)), reads=[wbR, XMTR[sbk]], writes=[PBR[bk]])
                        sg, sgR = SG[sbk], SGR[sbk]
                        P.add("act", lambda e, sg=sg, bg=bg: e.activation(out=sg[:], in_=PB[bg][:], func=AF.Silu),
                              reads=[PBR[bg]], writes=[sgR])
                        P.add("dve", lambda e, sg=sg, bu=bu, j=j, sbk=sbk: e.tensor_tensor(
                            out=AT[:, j, sbk * 512:(sbk + 1) * 512], in0=sg[:], in1=PB[bu][:], op=ALU.mult),
                            reads=[sgR, PBR[bu]], writes=[ATR[j]])
                    if modn:
                        mod_chunks(modn)
                if blk == 0 and jg == 1:
                    load_wo()
            for ti in range(tpb):
                t = blk * tpb + ti
                for n in range(2):
                    bk = nb()
                    for j in range(NJ):
                        P.add("pe", lambda e, j=j, ti=ti, n=n, bk=bk: e.matmul(
                            PB[bk][:], lhsT=AT[:, j, ti * 128:(ti + 1) * 128], rhs=WO[:, j, n * 512:(n + 1) * 512],
                            start=(j == 0), stop=(j == NJ - 1)),
                            reads=[ATR[j], WOR[j // 11]], writes=[PBR[bk]])
                    residual_ln(XT[ti], XTR[ti], bk, n, 0.5)
                layer_norm(XT[ti], XTR[ti])
                if dst is not None:
                    P.add("act", lambda e, t=t, ti=ti: e.dma_start(out=dst[t * 128:(t + 1) * 128, :], in_=XT[ti][:]),
                          reads=[XTR[ti]], dma=True)
                else:
                    stx(XD, XDR, t, XT[ti], XTR[ti])
        endph(ph)

    def attn_head(qT_ap, kT_fn, nkt, nq, v_fn, bias_fn, ET, ETR, OT_ap, OTR, REC, RECR, rd, part="sp", p0=0):
        per = 512 // nq
        kt = 0
        while kt < nkt and "s" in part:
            g = min(per, nkt - kt)
            bk = nb()
            for i in range(g):
                b_ap = bias_fn(kt + i)
                P.add("pe", lambda e, i=i, kt=kt, bk=bk, b_ap=b_ap: e.matmul(
                    PB[bk][:, i * nq:(i + 1) * nq], lhsT=kT_fn(kt + i), rhs=qT_ap, start=True, stop=(b_ap is None)),
                    reads=rd[0], writes=[PBR[bk]])
                if b_ap is not None:
                    P.add("pe", lambda e, i=i, bk=bk, b_ap=b_ap: e.matmul(
                        PB[bk][:, i * nq:(i + 1) * nq], lhsT=ident_bf[:], rhs=b_ap, start=False, stop=True),
                        reads=rd[0] + [identR], writes=[PBR[bk]])
            P.add("act", lambda e, kt=kt, g=g, bk=bk: e.activation(
                out=ET[:, kt * nq:(kt + g) * nq], in_=PB[bk][:, 0:g * nq], func=AF.Exp), reads=[PBR[bk]], writes=[ETR])
            kt += g
        if "p" not in part:
            return
        bk = nb()
        for i in range(nkt):
            P.add("pe", lambda e, i=i, bk=bk: e.matmul(PB[bk][:, 0:nq], lhsT=v_fn(i), rhs=ET[:, i * nq:(i + 1) * nq],
                                                       start=(i == 0), stop=(i == nkt - 1)),
                  reads=rd[1] + [ETR], writes=[PBR[bk]])
        bs = nb()
        for i in range(nkt):
            P.add("pe", lambda e, i=i, bs=bs: e.matmul(PB[bs][:, 0:nq], lhsT=ones_bf[:, :],
                                                       rhs=ET[:, i * nq:(i + 1) * nq],
                                                       start=(i == 0), stop=(i == nkt - 1)),
                  reads=[onesR, ETR], writes=[PBR[bs]])
        P.add("dve", lambda e, bs=bs: e.reciprocal(out=REC[p0:p0 + 64, 0:nq], in_=PB[bs][p0:p0 + 64, 0:nq]),
              reads=[PBR[bs]], writes=[RECR])
        P.add("dve", lambda e, bk=bk: e.tensor_tensor(out=OT_ap, in0=PB[bk][p0:p0 + 64, 0:nq],
                                                      in1=REC[p0:p0 + 64, 0:nq], op=ALU.mult),
              reads=[PBR[bk], RECR], writes=[OTR])

    def out_proj_heads(XT, XTR, OT, OTR, WOH, WOHR, tcol):
        for n in range(2):
            bk = nb()
            for h in range(NH // 2):
                P.add("pe", lambda e, h=h, n=n, bk=bk: e.matmul(
                    PB[bk][:], lhsT=OT[:, h, tcol * 128:(tcol + 1) * 128], rhs=WOH[:, h, n * 512:(n + 1) * 512],
                    start=(h == 0), stop=(h == NH // 2 - 1)), reads=[OTR, WOHR], writes=[PBR[bk]])
            residual_ln(XT, XTR, bk, n, 1.0)
        layer_norm(XT, XTR)

    def attn_ctx():
        XD, XDR = XDp, XDpR
        ph = P.phase()
        QKT = P.psb(ph, "qkt", [128, 16, 1024], BF16); QKTR = Res()
        QZ = P.psb(ph, "qz", [128, NH, 1024], BF16)
        P.add("pool", lambda e: e.memset(QZ[:], 0.0), writes=[QKTR])
        VB = P.psb(ph, "vb", [128, 8, D], BF16); VBR = Res()
        pa = P.phase()
        hT = P.psb(pa, "hT", [128, 8, 1024], BF16); hTR = Res()
        XT = [P.psb(pa, "axt%d" % i, [128, D], F32) for i in range(2)]; XTR = [Res() for _ in range(2)]
        KV = [P.psb(pa, "kv%d" % i, [128, 512], F32) for i in range(2)]; KVR = [Res() for _ in range(2)]
        load_mod(0, 1, 0)
        for t in range(8):
            ldx(XD, XDR, t, XT[t % 2], XTR[t % 2])
            mod_transpose(XT[t % 2], XTR[t % 2], hT[:, :, t * 128:(t + 1) * 128], hTR)

        def ev_qk(mc, blk, n, bk):
            if mc < 8:
                for hp in range(2):
                    P.add("act", lambda e, hp=hp: e.activation(
                        out=QZ[hp * 64:(hp + 1) * 64, 2 * mc + hp, blk * 512:blk * 512 + n],
                        in_=PB[bk][hp * 64:(hp + 1) * 64, 0:n], func=AF.Identity, scale=0.125),
                        reads=[PBR[bk]], writes=[QKTR])
                return
            P.add("act", lambda e: e.activation(out=QKT[:, mc, blk * 512:blk * 512 + n], in_=PB[bk][:, 0:n],
                                                func=AF.Identity, scale=1.0), reads=[PBR[bk]], writes=[QKTR])
        proj_fm(pa, "qk", hT, hTR, 1024, w_qkv, 0, 16, ev_qk, wt=w_qkv_t)
        cnt = [0]

        def ev_kv(n, t, bk):
            k = cnt[0] % 2; cnt[0] += 1
            kv, kvR = KV[k], KVR[k]
            P.add("act", lambda e: e.activation(out=kv[:], in_=PB[bk][:], func=AF.Identity), reads=[PBR[bk]],
                  writes=[kvR])
            dst = nk if n < 2 else nv
            cc = (n % 2) * 512
            P.add("act", lambda e: e.dma_start(out=dst[t * 128:(t + 1) * 128, cc:cc + 512], in_=kv[:]), reads=[kvR],
                  dma=True)
            if n >= 2:
                P.add("dve", lambda e: e.tensor_copy(out=VB[:, t, cc:cc + 512], in_=kv[:]), reads=[kvR], writes=[VBR])
        proj_tm(pa, "kv", hT, hTR, 8, w_qkv, 1024, 4, ev_kv)
        mod_chunks(4)
        endph(pa)
        pb_ = P.phase()
        WOH = P.psb(pb_, "woh", [128, NH // 2, D], BF16); WOHR = Res()
        P.add("pool", lambda e: e.dma_start(out=WOH[:], in_=w_o.rearrange("(h p) n -> p h n", p=128)), writes=[WOHR],
              dma=True)
        ETs = [P.psb(pb_, "et%d" % i, [128, 512], BF16) for i in range(2)]; ETRs = [Res() for _ in range(2)]
        OT = P.psb(pb_, "ot", [128, NH // 2, 256], BF16); OTR = Res()
        RECs = [P.psb(pb_, "rec%d" % i, [128, 512], F32) for i in range(2)]; RECRs = [Res() for _ in range(2)]
        XT2 = [P.psb(pb_, "bxt%d" % i, [128, D], F32) for i in range(2)]; XT2R = [Res() for _ in range(2)]
        for s in range(4):
            def head_args(h, s=s):
                p0 = (h % 2) * 64
                qT_ap = QZ[:, h, s * 256:(s + 1) * 256]
                kT_fn = lambda kt, h=h, s=s: QKT[:, 8 + h // 2, s * 256 + kt * 128:s * 256 + (kt + 1) * 128]
                v_fn = lambda kt, h=h, s=s: VB[:, s * 2 + kt, (h // 2) * 128:(h // 2 + 1) * 128]
                return (qT_ap, kT_fn, 2, 256, v_fn, lambda kt: None, ETs[h % 2], ETRs[h % 2],
                        OT[p0:p0 + 64, h // 2, :], OTR, RECs[h % 2], RECRs[h % 2], ([QKTR], [VBR]), p0)
            def run_head(h, part):
                a = head_args(h)
                attn_head(*a[:-1], part=part, p0=a[-1])
            run_head(0, "s")
            for h in range(NH):
                if h + 1 < NH:
                    run_head(h + 1, "s")
                run_head(h, "p")
                if h % 2 == 1:
                    mod_chunks(1)
            for tt in range(2):
                t = s * 2 + tt
                ldx(XD, XDR, t, XT2[tt], XT2R[tt])
                out_proj_heads(XT2[tt], XT2R[tt], OT, OTR, WOH, WOHR, tt)
                stx(XD, XDR, t, XT2[tt], XT2R[tt])
        endph(pb_)
        endph(ph)

    def attn_lat():
        XD, XDR = XDs, XDsR
        pa = P.phase()
        hT = P.psb(pa, "hT", [128, 8, 2048], BF16); hTR = Res()
        XT = [P.psb(pa, "axt%d" % i, [128, D], F32) for i in range(2)]; XTR = [Res() for _ in range(2)]
        KS = [P.psb(pa, "ks%d" % i, [128, 512], BF16) for i in range(2)]; KSR = [Res() for _ in range(2)]
        load_mod(0, 1, 1)
        for t in range(16):
            ldx(XD, XDR, t, XT[t % 2], XTR[t % 2])
            mod_transpose(XT[t % 2], XTR[t % 2], hT[:, :, t * 128:(t + 1) * 128], hTR)
        cnt = [0]

        def ev_k(mc, blk, n, bk):
            k = cnt[0] % 2; cnt[0] += 1
            ks, ksR = KS[k], KSR[k]
            P.add("act", lambda e: e.activation(out=ks[:, 0:n], in_=PB[bk][:, 0:n], func=AF.Identity),
                  reads=[PBR[bk]], writes=[ksR])
            P.add("act", lambda e: e.dma_start(out=KTD[:, mc, blk * 512:blk * 512 + n], in_=ks[:, 0:n]), reads=[ksR],
                  writes=[KTDR], dma=True)
        proj_fm(pa, "k", hT, hTR, 2048, w_qkv, 1024, 8, ev_k, wt=w_qkv_t)

        def ev_v(n, t, bk):
            k = cnt[0] % 2; cnt[0] += 1
            ks, ksR = KS[k], KSR[k]
            P.add("act", lambda e: e.activation(out=ks[:], in_=PB[bk][:], func=AF.Identity), reads=[PBR[bk]],
                  writes=[ksR])
            P.add("act", lambda e: e.dma_start(out=VD[t * 128:(t + 1) * 128, n * 512:(n + 1) * 512], in_=ks[:]),
                  reads=[ksR], writes=[VDR], dma=True)
        proj_tm(pa, "v", hT, hTR, 16, w_qkv, 2048, 2, ev_v)
        for kt in range(2):
            xb, xbR = XB[kt], XBR[kt]
            P.add("pool", lambda e, kt=kt, xb=xb: e.dma_start(out=xb[:], in_=ck[kt * 128:(kt + 1) * 128, :]),
                  writes=[xbR], dma=True)
            kc_t = P.psb(pa, "kct%d" % kt, [128, 8, 128], BF16); kcR = Res()
            transpose_tile(xb, xbR, kc_t[:], kcR)
            P.add("act", lambda e, kt=kt, kc_t=kc_t: e.dma_start(out=KTD[:, :, 2048 + kt * 128:2048 + (kt + 1) * 128],
                                                                in_=kc_t[:]), reads=[kcR], writes=[KTDR], dma=True)
            vv = P.psb(pa, "cvt%d" % kt, [128, D], BF16); vvR = Res()
            P.add("pool", lambda e, kt=kt, vv=vv: e.dma_start(out=vv[:], in_=cv[kt * 128:(kt + 1) * 128, :]),
                  writes=[vvR], dma=True)
            P.add("sp", lambda e, kt=kt, vv=vv: e.dma_start(out=VD[2048 + kt * 128:2048 + (kt + 1) * 128, :], in_=vv[:]),
                  reads=[vvR], writes=[VDR], dma=True)
        endph(pa)
        pb_ = P.phase()
        WOH = P.psb(pb_, "woh", [128, NH // 2, D], BF16); WOHR = Res()
        P.add("pool", lambda e: e.dma_start(out=WOH[:], in_=w_o.rearrange("(h p) n -> p h n", p=128)), writes=[WOHR],
              dma=True)
        hTb = P.psb(pb_, "hTb", [128, 8, 512], BF16); hTbR = Res()
        QT = P.psb(pb_, "qt", [128, NH, 512], BF16); QTR = Res()
        P.add("pool", lambda e: e.memset(QT[:], 0.0), writes=[QTR])
        KTh = P.psb(pb_, "kth", [128, 8, 1280], BF16); KThR = Res()
        VBh = P.psb(pb_, "vbh", [128, 10, D], BF16); VBhR = Res()
        BT = [P.psb(pb_, "bt%d" % i, [128, 8, 512], BF16) for i in range(2)]; BTR = [Res() for _ in range(2)]
        ETs = [P.psb(pb_, "et%d" % i, [128, 10 * 512], BF16) for i in range(2)]; ETRs = [Res() for _ in range(2)]
        OT = P.psb(pb_, "ot", [128, NH // 2, 512], BF16); OTR = Res()
        RECs = [P.psb(pb_, "rec%d" % i, [128, 512], F32) for i in range(2)]; RECRs = [Res() for _ in range(2)]
        XT2 = [P.psb(pb_, "bxt%d" % i, [128, D], F32) for i in range(2)]; XT2R = [Res() for _ in range(2)]
        QW = ([P.psb(pb_, "qw%d" % i, [128, 8, 128], BF16) for i in range(3)], [Res() for _ in range(3)])

        def prep(j):
            hs = min(max(8 * j - 4, 0), 16)
            tk0 = hs * 64
            for ti in range(4):
                t = j * 4 + ti
                ldx(XD, XDR, t, XT2[ti % 2], XT2R[ti % 2])
                mod_transpose(XT2[ti % 2], XT2R[ti % 2], hTb[:, :, ti * 128:(ti + 1) * 128], hTbR)

            def ev_q(mc, blk, n, bk):
                for hp in range(2):
                    P.add("act", lambda e, hp=hp: e.activation(
                        out=QT[hp * 64:(hp + 1) * 64, 2 * mc + hp, :], in_=PB[bk][hp * 64:(hp + 1) * 64, :],
                        func=AF.Identity, scale=0.125), reads=[PBR[bk]], writes=[QTR])
            proj_fm(pb_, "q", hTb, hTbR, 512, w_qkv, 0, 8, ev_q, wt=w_qkv_t, wbuf=QW)
            P.add("sp", lambda e, tk0=tk0: e.dma_start(out=KTh[:, :, 0:1024], in_=KTD[:, :, tk0:tk0 + 1024]),
                  reads=[KTDR], writes=[KThR], dma=True)
            P.add("sp", lambda e, tk0=tk0: e.dma_start(
                out=VBh[:, 0:8, :], in_=VD[tk0:tk0 + 1024, :].rearrange("(t p) n -> p t n", p=128)),
                reads=[VDR], writes=[VBhR], dma=True)
            if j == 0:
                P.add("sp", lambda e: e.dma_start(out=KTh[:, :, 1024:1280], in_=KTD[:, :, 2048:2304]),
                      reads=[KTDR], writes=[KThR], dma=True)
                P.add("sp", lambda e: e.dma_start(
                    out=VBh[:, 8:10, :], in_=VD[2048:2304, :].rearrange("(t p) n -> p t n", p=128)),
                    reads=[VDR], writes=[VBhR], dma=True)

        prep(0)
        for j in range(4):
            hs = min(max(8 * j - 4, 0), 16)
            act_t = [t_ for t_ in range(8) if any(
                min(max(8 * j + qr - 4, 0), 24) <= hs + 2 * t_ + a_ < min(max(8 * j + qr - 4, 0), 24) + 8
                for qr in range(8) for a_ in range(2))] + [8, 9]

            def head_args(h, j=j, act_t=act_t):
                bt, btR = BT[h % 2], BTR[h % 2]
                p0 = (h % 2) * 64
                qT_ap = QT[:, h, :]
                kT_fn2 = lambda i, h=h: KTh[:, h // 2, act_t[i] * 128:(act_t[i] + 1) * 128]
                v_fn2 = lambda i, h=h: VBh[:, act_t[i], (h // 2) * 128:(h // 2 + 1) * 128]
                bias_fn2 = lambda i, bt=bt: (bt[:, act_t[i], :] if act_t[i] < 8 else None)
                return (qT_ap, kT_fn2, len(act_t), 512, v_fn2, bias_fn2, ETs[h % 2], ETRs[h % 2],
                        OT[p0:p0 + 64, h // 2, :], OTR, RECs[h % 2], RECRs[h % 2], ([QTR, KThR, btR], [VBhR]), p0)

            def load_bt(h, j=j):
                bt, btR = BT[h % 2], BTR[h % 2]
                P.add("pool", lambda e, bt=bt, h=h, j=j: e.dma_start(out=bt[:], in_=btab[h, j]), writes=[btR], dma=True)
            def run_head(h, part):
                a = head_args(h)
                attn_head(*a[:-1], part=part, p0=a[-1])
            load_bt(0)
            run_head(0, "s")
            for h in range(NH):
                if h + 1 < NH:
                    load_bt(h + 1)
                    run_head(h + 1, "s")
                run_head(h, "p")
            if j + 1 < 4:
                prep(j + 1)
            for ti in range(4):
                t = j * 4 + ti
                ldx(XD, XDR, t, XT2[ti % 2], XT2R[ti % 2])
                out_proj_heads(XT2[ti % 2], XT2R[ti % 2], OT, OTR, WOH, WOHR, ti)
                stx(XD, XDR, t, XT2[ti % 2], XT2R[ti % 2])
        endph(pb_)

    def fwd_dft(pc, L, rhs_fn, rdR_fn, consume):
        ntc = L // 128; nfc = FP[L] // 128
        FB = pc["FB"]; FBR = pc["FBR"]
        res_f = pc.get("res", False)
        for i in range(nfc):
            if res_f:
                fb_, fbR = FB[i], FBR[i]
            else:
                fb_, fbR = FB[i % 2], FBR[i % 2]
                P.add("sp", lambda e, fb_=fb_, i=i: e.dma_start(out=fb_[:].rearrange("p a b c -> p (a b c)"),
                                                                in_=fwd[L][i]), writes=[fbR], dma=True)
            br, bi = nb(), nb()
            for part, bk in ((0, br), (1, bi)):
                for tc in range(ntc):
                    P.add("pe", lambda e, fb_=fb_, part=part, tc=tc, bk=bk: e.matmul(
                        PB[bk][:], lhsT=fb_[:, tc, part, :], rhs=rhs_fn(part, tc),
                        start=(tc == 0), stop=(tc == ntc - 1)), reads=[fbR, rdR_fn(part)], writes=[PBR[bk]])
            consume(i, br, bi)

    def hyena(XD, XDR, L, nseq, row, own=False):
        ntc = L // 128
        ntok = nseq * L; ntiles = ntok // 128
        Fp = FP[L]; nfc = Fp // 128
        TB = TBL[L]; ntb = L // TB
        pf = P.phase()
        zt = P.psb(pf, "zt", [33, L], F32); ztR = Res()
        w1 = P.psb(pf, "w1", [33, 64], F32); w2 = P.psb(pf, "w2", [64, 64], F32); wR = Res()
        w3 = P.psb(pf, "w3", [65, 4096], F32)
        fb = P.psb(pf, "fb", [64, 6], F32); fbR = Res()
        h1 = P.psb(pf, "h1", [64, L], F32); h1R = Res()
        h2 = P.psb(pf, "h2", [65, L], F32); h2R = Res()
        sc1 = P.psb(pf, "sc1", [64, 512], F32); sc2 = P.psb(pf, "sc2", [64, 512], F32); scR = Res()
        HS = P.psb(pf, "hs", [128, ntc, 2, D], BF16); HSR = [Res(), Res()]
        NBF = 2 if L <= 256 else 1
        DEC = [P.psb(pf, "dec%d" % i, [128, 2, D], F32) for i in range(NBF)]; DECR = [Res() for _ in range(NBF)]
        HF = [P.psb(pf, "hf%d" % i, [128, 2, D], F32) for i in range(NBF)]; HFR = [Res() for _ in range(NBF)]
        KO = [P.psb(pf, "ko%d" % i, [128, 512], F32) for i in range(4)]; KOR = [Res() for _ in range(4)]
        pcf = {"FB": [P.psb(pf, "ffb%d" % i, [128, ntc, 2, 128], BF16) for i in range(2)], "FBR": [Res(), Res()]}
        P.add("sp", lambda e: e.dma_start(out=zt[:], in_=zT[L]), writes=[ztR], dma=True)
        P.add("sp", lambda e: e.dma_start(out=w1[:], in_=hy_w1), writes=[wR], dma=True)
        P.add("sp", lambda e: e.dma_start(out=w2[:], in_=hy_w2), writes=[wR], dma=True)
        P.add("sp", lambda e: e.dma_start(out=w3[:], in_=hy_w3a), writes=[wR], dma=True)
        P.add("sp", lambda e: e.dma_start(out=fb[:, 0:4], in_=hy_fb), writes=[fbR], dma=True)
        P.add("dve", lambda e: e.tensor_tensor(out=fb[:, 4:5], in0=fb[:, 0:1], in1=fb[:, 1:2], op=ALU.mult),
              reads=[fbR], writes=[fbR])
        P.add("dve", lambda e: e.tensor_tensor(out=fb[:, 5:6], in0=fb[:, 2:3], in1=fb[:, 3:4], op=ALU.mult),
              reads=[fbR], writes=[fbR])
        P.add("dve", lambda e: e.memset(h2[64:65, :], 1.0), writes=[h2R])
        TS = min(L, 512)
        for (wt, kdim, src, srcR, dstt, dstR, fcol, bcol) in (
                (w1, 33, zt, ztR, h1, h1R, 1, 4), (w2, 64, h1, h1R, h2, h2R, 3, 5)):
            for b in range(L // TS):
                bk = nb()
                P.add("pe", lambda e, wt=wt, kdim=kdim, src=src, b=b, bk=bk: e.matmul(
                    PB[bk][0:64, 0:TS], lhsT=wt[0:kdim, :], rhs=src[0:kdim, b * TS:(b + 1) * TS], start=True, stop=True),
                    reads=[wR, srcR], writes=[PBR[bk]])
                P.add("act", lambda e, bk=bk, fcol=fcol, bcol=bcol: e.activation(
                    out=sc1[:, 0:TS], in_=PB[bk][0:64, 0:TS], func=AF.Identity, scale=fb[:, fcol:fcol + 1],
                    bias=fb[:, bcol:bcol + 1]), reads=[PBR[bk], fbR], writes=[scR])
                for _ in range(2):
                    P.add("dve", lambda e: e.tensor_scalar(out=sc2[:, 0:TS], in0=sc1[:, 0:TS], scalar1=-PI,
                                                           scalar2=2 * PI, op0=ALU.is_lt, op1=ALU.mult),
                          reads=[scR], writes=[scR])
                    P.add("dve", lambda e: e.tensor_tensor(out=sc1[:, 0:TS], in0=sc1[:, 0:TS], in1=sc2[:, 0:TS],
                                                           op=ALU.add), reads=[scR], writes=[scR])
                    P.add("dve", lambda e: e.tensor_scalar(out=sc2[:, 0:TS], in0=sc1[:, 0:TS], scalar1=PI,
                                                           scalar2=-2 * PI, op0=ALU.is_gt, op1=ALU.mult),
                          reads=[scR], writes=[scR])
                    P.add("dve", lambda e: e.tensor_tensor(out=sc1[:, 0:TS], in0=sc1[:, 0:TS], in1=sc2[:, 0:TS],
                                                           op=ALU.add), reads=[scR], writes=[scR])
                P.add("act", lambda e, dstt=dstt, b=b: e.activation(out=dstt[0:64, b * TS:(b + 1) * TS], in_=sc1[:, 0:TS],
                                                                    func=AF.Sin), reads=[scR], writes=[dstR])
        kcnt = [0]
        for o in range(2):
            for tc in range(ntc):
                dec, decR, hf, hfR = DEC[tc % NBF], DECR[tc % NBF], HF[tc % NBF], HFR[tc % NBF]
                P.add("sp", lambda e, tc=tc, dec=dec: e.dma_start(out=dec[:], in_=decay[L][tc * 128:(tc + 1) * 128]),
                      writes=[decR], dma=True)
                for dr_ in range(2):
                    for dh in range(2):
                        c0 = o * 2048 + dr_ * 1024 + dh * 512
                        bk = nb()
                        P.add("pe", lambda e, tc=tc, c0=c0, bk=bk: e.matmul(
                            PB[bk][:], lhsT=h2[0:65, tc * 128:(tc + 1) * 128], rhs=w3[0:65, c0:c0 + 512],
                            start=True, stop=True), reads=[h2R, wR], writes=[PBR[bk]])
                        P.add("dve", lambda e, bk=bk, dr_=dr_, dh=dh, hf=hf, dec=dec: e.tensor_tensor(
                            out=hf[:, dr_, dh * 512:(dh + 1) * 512], in0=PB[bk][:], in1=dec[:, dr_, dh * 512:(dh + 1) * 512],
                            op=ALU.mult), reads=[PBR[bk], decR], writes=[hfR])
                P.add("pool", lambda e, tc=tc, hf=hf: e.tensor_tensor(out=HS[:, tc, 0, :], in0=hf[:, 0, :], in1=hf[:, 1, :],
                                                                     op=ALU.add), reads=[hfR], writes=[HSR[0]])
                P.add("pool", lambda e, tc=tc, hf=hf: e.tensor_tensor(out=HS[:, tc, 1, :], in0=hf[:, 0, :], in1=hf[:, 1, :],
                                                                     op=ALU.subtract), reads=[hfR], writes=[HSR[1]])
            for hh in range(2):
                def cons(i, br, bi, o=o, hh=hh):
                    for part, bk in ((0, br), (1, bi)):
                        k = kcnt[0] % 4; kcnt[0] += 1
                        ko, koR = KO[k], KOR[k]
                        if part == 0:
                            P.add("act", lambda e, ko=ko, bk=bk: e.activation(out=ko[:], in_=PB[bk][:], func=AF.Identity),
                                  reads=[PBR[bk]], writes=[koR])
                        else:
                            P.add("dve", lambda e, ko=ko, bk=bk: e.tensor_copy(out=ko[:], in_=PB[bk][:]),
                                  reads=[PBR[bk]], writes=[koR])
                        r0 = part * Fp + i * 128
                        P.add("act", lambda e, ko=ko, r0=r0: e.dma_start(
                            out=KD[L][o, r0:r0 + 128, hh * 512:(hh + 1) * 512], in_=ko[:]), reads=[koR], writes=[KDR],
                            dma=True)
                fwd_dft(pcf, L, lambda part, tc, hh=hh: HS[:, tc, part, hh * 512:(hh + 1) * 512],
                        lambda part: HSR[part], cons)
        endph(pf)
        pv = P.phase()
        VT = P.psb(pv, "vt", [128, ntiles, D], BF16)
        if own:
            ZFo = P.psb(pv, "zfo", [128, 8, 512], BF16); ZFoR = Res()
        VTR = {(s_, hh): Res() for s_ in range(nseq) for hh in range(2)}
        pi_ = P.phase()
        hT = P.psb(pi_, "hT", [128, 8, ntok], BF16); hTR = Res()
        XT = [P.psb(pi_, "hxt%d" % i, [128, D], F32) for i in range(2)]; XTR = [Res(), Res()]
        UP = [P.psb(pi_, "upad%d" % i, [128, nseq, L + 2], F32) for i in range(2)]; UPR = [Res(), Res()]
        UC = [P.psb(pi_, "uc%d" % i, [128, nseq, L], F32) for i in range(2)]; UCR = [Res(), Res()]
        UCb = [P.psb(pi_, "ucb%d" % i, [128, ntok], BF16) for i in range(2)]; UCbR = [Res(), Res()]
        CW = P.psb(pi_, "cw", [128, 24, 3], F32); CB = P.psb(pi_, "cb", [128, 24], F32); CWR = Res()
        P.add("sp", lambda e: e.dma_start(out=CW[:], in_=hy_cw), writes=[CWR], dma=True)
        P.add("sp", lambda e: e.dma_start(out=CB[:], in_=hy_cb), writes=[CWR], dma=True)
        for k in range(2):
            P.add("dve", lambda e, k=k: e.memset(UP[k][:, :, 0:1], 0.0), writes=[UPR[k]])
            P.add("dve", lambda e, k=k: e.memset(UP[k][:, :, L + 1:L + 2], 0.0), writes=[UPR[k]])
        load_mod(1, 1, row)
        for ti in range(ntiles):
            ldx(XD, XDR, ti, XT[ti % 2], XTR[ti % 2])
            mod_transpose(XT[ti % 2], XTR[ti % 2], hT[:, :, ti * 128:(ti + 1) * 128], hTR)
        nblk = ntok // 512

        def ev_u(mc, blk, n, bk):
            k = mc % 2
            up, upR, uc, ucR, ucb, ucbR = UP[k], UPR[k], UC[k], UCR[k], UCb[k], UCbR[k]
            if L >= 512:
                s_ = (blk * 512) // L; off = (blk * 512) % L
                P.add("act", lambda e: e.activation(out=up[:, s_, 1 + off:1 + off + 512], in_=PB[bk][:],
                                                    func=AF.Identity), reads=[PBR[bk]], writes=[upR])
            else:
                ns = 512 // L
                P.add("act", lambda e: e.activation(out=up[:, blk * ns:(blk + 1) * ns, 1:L + 1],
                                                    in_=PB[bk][:].rearrange("p (s t) -> p s t", s=ns),
                                                    func=AF.Identity), reads=[PBR[bk]], writes=[upR])
            if blk != nblk - 1:
                return
            P.add("dve", lambda e: e.tensor_scalar(out=uc[:], in0=up[:, :, 0:L], scalar1=CW[:, mc, 0:1],
                                                   scalar2=CB[:, mc:mc + 1], op0=ALU.mult, op1=ALU.add),
                  reads=[upR, CWR], writes=[ucR])
            P.add("dve", lambda e: e.scalar_tensor_tensor(out=uc[:], in0=up[:, :, 1:L + 1], scalar=CW[:, mc, 1:2],
                                                          in1=uc[:], op0=ALU.mult, op1=ALU.add),
                  reads=[upR, CWR, ucR], writes=[ucR])
            P.add("dve", lambda e: e.scalar_tensor_tensor(out=uc[:], in0=up[:, :, 2:L + 2], scalar=CW[:, mc, 2:3],
                                                          in1=uc[:], op0=ALU.mult, op1=ALU.add),
                  reads=[upR, CWR, ucR], writes=[ucR])
            P.add("act", lambda e: e.dma_start(out=UD[mc // 8, (mc % 8) * 128:(mc % 8 + 1) * 128, 0:ntok],
                                              in_=uc[:].rearrange("p s t -> p (s t)")),
                  reads=[ucR], writes=[UDR], dma=True)
            if own and mc >= 16:
                for tb_ in range(4):
                    P.add("act", lambda e, tb_=tb_: e.dma_start(
                        out=U2b[tb_ * D + (mc - 16) * 128:tb_ * D + (mc - 15) * 128, :],
                        in_=uc[:, 0, tb_ * 512:(tb_ + 1) * 512]), reads=[ucR], writes=[U2bR], dma=True)
            for fn_ in pend:
                fn_()
            del pend[:]
            if mc < 8:
                P.add("act", lambda e: e.activation(out=ucb[:], in_=uc[:].rearrange("p s t -> p (s t)"),
                                                    func=AF.Identity), reads=[ucR], writes=[ucbR])

                def tp(mc=mc, ucb=ucb, ucbR=ucbR):
                    for g in range(ntiles // 8):
                        seqs = sorted(set((g * 8 + q) // ntc for q in range(8)))
                        transpose_tile(ucb[:, g * 1024:(g + 1) * 1024], ucbR,
                                       VT[:, g * 8:(g + 1) * 8, mc * 128:(mc + 1) * 128],
                                       [VTR[(s_, mc // 4)] for s_ in seqs], nchunk=8)
                pend.append(tp)
        pend = []
        proj_fm(pi_, "hin", hT, hTR, ntok, hy_w_in, 0, 24, ev_u, wt=hy_w_in_t)
        for fn_ in pend:
            fn_()
        endph(pi_)
        pc_ = P.phase()
        nys = 2 if L <= 256 else 1
        YSs = [P.psb(pc_, "ys%d" % i, [128, 2 * nfc, 512], BF16) for i in range(nys)]
        YSRs = [Res() for _ in range(nys)]
        gcnt = [0]
        small = (L <= 256)
        nfb = nfc if small else 2
        pcd = {"FB": [P.psb(pc_, "dfb%d" % i, [128, ntc, 2, 128], BF16) for i in range(nfb)],
               "FBR": [Res() for _ in range(nfb)], "res": small}
        GB = [P.psb(pc_, "gb%d" % i, [128, nfc, TB], BF16) for i in range(2)]; GBR = [Res(), Res()]
        if small:
            for i in range(nfc):
                P.add("sp", lambda e, i=i: e.dma_start(out=pcd["FB"][i][:].rearrange("p a b c -> p (a b c)"),
                                                       in_=fwd[L][i]), writes=[pcd["FBR"][i]], dma=True)
            for gh in range(2):
                P.add("sp", lambda e, gh=gh: e.dma_start(
                    out=GB[gh][:].rearrange("p a b -> p (a b)"),
                    in_=inv[L][0, :, gh * nfc * TB:(gh + 1) * nfc * TB]), writes=[GBR[gh]], dma=True)
            KS = P.psb(pc_, "ksb", [128, 2, 2 * nfc, D], F32); KSR = Res()
            for o in range(2):
                P.add("sp", lambda e, o=o: e.dma_start(out=KS[:, o, :, :],
                                                       in_=KD[L][o].rearrange("(c p) n -> p c n", p=128)),
                      reads=[KDR], writes=[KSR], dma=True)
        KB = [P.psb(pc_, "kb%d" % i, [128, 2, 512], F32) for i in range(2)]; KBR = [Res(), Res()]
        T4 = [P.psb(pc_, "t4%d" % i, [128, 512], F32) for i in range(4)]; T4R = [Res(), Res()]
        EP = [P.psb(pc_, "ep%d" % i, [128, 2, TB], F32) for i in range(2)]; EPR = [Res(), Res()]
        ZT = [P.psb(pc_, "zt%d" % i, [128, TB], F32) for i in range(2)]; ZTR = [Res(), Res()]
        ZB = [P.psb(pc_, "zb%d" % i, [128, TB], BF16) for i in range(2)]; ZBR = [Res(), Res()]
        HB = P.psb(pc_, "hb", [128, 8, 2], F32); HBR = Res()
        P.add("sp", lambda e: e.dma_start(out=HB[:], in_=hy_biasT), writes=[HBR], dma=True)
        cnt = [0]
        if own:
            IDC = P.psb(pc_, "idc", [128, 8], I32); IDCR = Res()
            P.add("sp", lambda e: e.dma_start(out=IDC[:], in_=idx_ch_d), writes=[IDCR], dma=True)
        groups = [(o, s_, hh) for o in range(2) for s_ in range(nseq) for hh in range(2)]
        def group_body(gi, o, s_, hh):
            for _once in (0,):
                for _once2 in (0,):
                    YS, YSR = YSs[gi % nys], YSRs[gi % nys]

                    def consY(i, br, bi, o=o, hh=hh, YS=YS, YSR=YSR):
                        if small:
                            kre = KS[:, o, i, hh * 512:(hh + 1) * 512]
                            kim = KS[:, o, nfc + i, hh * 512:(hh + 1) * 512]
                            kbR = KSR
                        else:
                            kb, kbR = KB[i % 2], KBR[i % 2]
                            kre, kim = kb[:, 0, :], kb[:, 1, :]
                            for part in range(2):
                                r0 = part * Fp + i * 128
                                P.add("sp", lambda e, kb=kb, part=part, r0=r0: e.dma_start(
                                    out=kb[:, part, :], in_=KD[L][o, r0:r0 + 128, hh * 512:(hh + 1) * 512]),
                                    reads=[KDR], writes=[kbR], dma=True)
                        P.add("dve", lambda e: e.tensor_tensor(out=T4[0][:], in0=PB[br][:], in1=kre, op=ALU.mult),
                              reads=[PBR[br], kbR], writes=[T4R[0]])
                        P.add("dve", lambda e: e.tensor_tensor(out=T4[1][:], in0=PB[bi][:], in1=kim, op=ALU.mult),
                              reads=[PBR[bi], kbR], writes=[T4R[0]])
                        P.add("pool", lambda e: e.tensor_tensor(out=YS[:, i, :], in0=T4[0][:], in1=T4[1][:],
                                                                op=ALU.subtract), reads=[T4R[0]], writes=[YSR])
                        P.add("dve", lambda e: e.tensor_tensor(out=T4[2][:], in0=PB[br][:], in1=kim, op=ALU.mult),
                              reads=[PBR[br], kbR], writes=[T4R[1]])
                        P.add("dve", lambda e: e.tensor_tensor(out=T4[3][:], in0=PB[bi][:], in1=kre, op=ALU.mult),
                              reads=[PBR[bi], kbR], writes=[T4R[1]])
                        P.add("pool", lambda e: e.tensor_tensor(out=YS[:, nfc + i, :], in0=T4[2][:], in1=T4[3][:],
                                                                op=ALU.add), reads=[T4R[1]], writes=[YSR])
                    fwd_dft(pcd, L, lambda part, tc, s_=s_, hh=hh: VT[:, s_ * ntc + tc, hh * 512:(hh + 1) * 512],
                            lambda part, s_=s_, hh=hh: VTR[(s_, hh)], consY)
                    yield
                    own2 = own and o == 1
                    for tb in range(1 if own2 else ntb):
                        for gh in range(2):
                            if small or (own2 and hh == 1):
                                break
                            gsrc = inv_own if own2 else inv[L][tb]
                            P.add("sp", lambda e, gsrc=gsrc, gh=gh: e.dma_start(
                                out=GB[gh][:].rearrange("p a b -> p (a b)"),
                                in_=gsrc[:, gh * nfc * TB:(gh + 1) * nfc * TB]), writes=[GBR[gh]], dma=True)
                        ibk = [nb() for _ in range(4)]
                        for gh in range(2):
                            for cc in range(4):
                                for f_ in range(nfc):
                                    fc = gh * nfc + f_
                                    P.add("pe", lambda e, fc=fc, f_=f_, gh=gh, cc=cc, bk=ibk[cc], YS=YS: e.matmul(
                                        PB[bk][:, 0:TB], lhsT=YS[:, fc, cc * 128:(cc + 1) * 128], rhs=GB[gh][:, f_, :],
                                        start=(fc == 0), stop=(fc == 2 * nfc - 1)), reads=[YSR, GBR[gh]],
                                        writes=[PBR[ibk[cc]]])
                        for cc in range(4):
                            c = hh * 4 + cc
                            bk = ibk[cc]
                            k = cnt[0] % 2; cnt[0] += 1
                            ep, epR, ztt, zttR, zb, zbR = EP[k], EPR[k], ZT[k], ZTR[k], ZB[k], ZBR[k]
                            vsrc = UD[0] if o == 0 else ZD
                            vsrcR = UDR if o == 0 else ZDR
                            col = s_ * L + tb * TB
                            if own2:
                                P.add("pool", lambda e, ep=ep, c=c: e.indirect_dma_start(
                                    out=ep[:, 0, :], out_offset=None, in_=ZDb[:, :],
                                    in_offset=bass.IndirectOffsetOnAxis(ap=IDC[:, c:c + 1], axis=0)),
                                    reads=[ZDbR, IDCR], writes=[epR], dma=True)
                                P.add("pool", lambda e, ep=ep, c=c: e.indirect_dma_start(
                                    out=ep[:, 1, :], out_offset=None, in_=U2b[:, :],
                                    in_offset=bass.IndirectOffsetOnAxis(ap=IDC[:, c:c + 1], axis=0)),
                                    reads=[U2bR, IDCR], writes=[epR], dma=True)
                            else:
                                P.add("sp", lambda e, ep=ep, vsrc=vsrc, c=c, col=col: e.dma_start(
                                    out=ep[:, 0, :], in_=vsrc[c * 128:(c + 1) * 128, col:col + TB]),
                                    reads=[vsrcR], writes=[epR], dma=True)
                                P.add("sp", lambda e, ep=ep, o=o, c=c, col=col: e.dma_start(
                                    out=ep[:, 1, :], in_=UD[1 + o, c * 128:(c + 1) * 128, col:col + TB]),
                                    reads=[UDR], writes=[epR], dma=True)
                            P.add("dve", lambda e, ep=ep, ztt=ztt, c=c, o=o, bk=bk: e.scalar_tensor_tensor(
                                out=ztt[:], in0=ep[:, 0, :], scalar=HB[:, c, o:o + 1], in1=PB[bk][:, 0:TB],
                                op0=ALU.mult, op1=ALU.add), reads=[epR, HBR, PBR[bk]], writes=[zttR])
                            P.add("pool", lambda e, ep=ep, ztt=ztt: e.tensor_tensor(out=ztt[:], in0=ztt[:], in1=ep[:, 1, :],
                                                                                   op=ALU.mult),
                                  reads=[epR, zttR], writes=[zttR])
                            P.add("act", lambda e, ztt=ztt, zb=zb: e.activation(out=zb[:], in_=ztt[:], func=AF.Identity),
                                  reads=[zttR], writes=[zbR])
                            if o == 0:
                                if own:
                                    P.add("pool", lambda e, ztt=ztt, c=c, tb=tb: e.dma_start(
                                        out=ZDb[tb * D + c * 128:tb * D + (c + 1) * 128, :], in_=ztt[:]),
                                        reads=[zttR], writes=[ZDbR], dma=True)
                                else:
                                    P.add("pool", lambda e, ztt=ztt, c=c, col=col: e.dma_start(
                                        out=ZD[c * 128:(c + 1) * 128, col:col + TB], in_=ztt[:]),
                                        reads=[zttR], writes=[ZDR], dma=True)
                                nch = TB // 128
                                transpose_tile(zb, zbR,
                                               VT[:, s_ * ntc + tb * nch:s_ * ntc + (tb + 1) * nch, c * 128:(c + 1) * 128],
                                               VTR[(s_, hh)], nchunk=nch)
                            elif own:
                                P.add("dve", lambda e, zb=zb, c=c: e.tensor_copy(out=ZFo[:, c, :], in_=zb[:]),
                                      reads=[zbR], writes=[ZFoR])
                            else:
                                P.add("act", lambda e, zb=zb, c=c, col=col: e.dma_start(
                                    out=ZFD[c * 128:(c + 1) * 128, col:col + TB], in_=zb[:]),
                                    reads=[zbR], writes=[ZFDR], dma=True)
        gens = [group_body(gi, *g_) for gi, g_ in enumerate(groups)]
        if small:
            next(gens[0])
            for gi in range(len(gens)):
                if gi + 1 < len(gens):
                    next(gens[gi + 1])
                for _ in gens[gi]:
                    pass
        else:
            for g_ in gens:
                for _ in g_:
                    pass
        endph(pc_)
        if own:
            po = P.phase()
            WOU = P.psb(po, "wou", [128, 8, D], BF16); WOUR = Res()
            XT3 = [P.psb(po, "oxt%d" % i, [128, D], F32) for i in range(2)]; XT3R = [Res(), Res()]
            IDT = P.psb(po, "idt", [128, 4], I32); IDTR = Res()
            P.add("sp", lambda e: e.dma_start(out=IDT[:], in_=idx_tok_d), writes=[IDTR], dma=True)
            P.add("pool", lambda e: e.dma_start(out=WOU[:], in_=hy_w_out.rearrange("(c p) n -> p c n", p=128)),
                  writes=[WOUR], dma=True)
            for ti in range(4):
                xt, xtR = XT3[ti % 2], XT3R[ti % 2]
                P.add("pool", lambda e, xt=xt, ti=ti: e.indirect_dma_start(
                    out=xt[:, :], out_offset=None, in_=XD[:, :],
                    in_offset=bass.IndirectOffsetOnAxis(ap=IDT[:, ti:ti + 1], axis=0)),
                    reads=list(XDR) + [IDTR], writes=[xtR], dma=True)
                for n in range(2):
                    bk = nb()
                    for c in range(8):
                        P.add("pe", lambda e, c=c, ti=ti, n=n, bk=bk: e.matmul(
                            PB[bk][:], lhsT=ZFo[:, c, ti * 128:(ti + 1) * 128], rhs=WOU[:, c, n * 512:(n + 1) * 512],
                            start=(c == 0), stop=(c == 7)), reads=[ZFoR, WOUR], writes=[PBR[bk]])
                    residual_ln(xt, xtR, bk, n, 1.0)
                layer_norm(xt, xtR)
                stx(XDo, XDoR, ti, xt, xtR)
            endph(po)
            endph(pv)
            return
        endph(pv)
        po = P.phase()
        ZF = P.psb(po, "zf", [128, 8, ntok], BF16); ZFR = Res()
        WOU = P.psb(po, "wou", [128, 8, D], BF16); WOUR = Res()
        XT3 = [P.psb(po, "oxt%d" % i, [128, D], F32) for i in range(2)]; XT3R = [Res(), Res()]
        P.add("sp", lambda e: e.dma_start(out=ZF[:], in_=ZFD[:, 0:ntok].rearrange("(c p) t -> p c t", p=128)),
              reads=[ZFDR], writes=[ZFR], dma=True)
        P.add("pool", lambda e: e.dma_start(out=WOU[:], in_=hy_w_out.rearrange("(c p) n -> p c n", p=128)),
              writes=[WOUR], dma=True)
        for ti in range(ntiles):
            xt, xtR = XT3[ti % 2], XT3R[ti % 2]
            ldx(XD, XDR, ti, xt, xtR)
            for n in range(2):
                bk = nb()
                for c in range(8):
                    P.add("pe", lambda e, c=c, ti=ti, n=n, bk=bk: e.matmul(
                        PB[bk][:], lhsT=ZF[:, c, ti * 128:(ti + 1) * 128], rhs=WOU[:, c, n * 512:(n + 1) * 512],
                        start=(c == 0), stop=(c == 7)), reads=[ZFR, WOUR], writes=[PBR[bk]])
                residual_ln(xt, xtR, bk, n, 1.0)
            layer_norm(xt, xtR)
            stx(XD, XDR, ti, xt, xtR)
        endph(po)

    def copy_in(src, XD, XDR, ntiles):
        for t in range(ntiles):
            P.add("sp", lambda e, t=t: e.dma_start(out=XD[t * 128:(t + 1) * 128, :], in_=src[t * 128:(t + 1) * 128, :]),
                  writes=[XDR[t]], dma=True)

    mod_chunks(12)

    def out_copy(XD, XDR, dst, ntiles):
        for t in range(ntiles):
            P.add("sp", lambda e, t=t: e.dma_start(out=dst[t * 128:(t + 1) * 128, :], in_=XD[t * 128:(t + 1) * 128, :]),
                  reads=[XDR[t]], dma=True)

    do_p = stage in (2, 4, 99)
    do_s = stage in (3, 5, 99)
    if do_p:
        ffn(XDp, XDpR, 8, 0, 0, 0, 0, modn=1, src=xp)
        mod_upto(24)
        attn_ctx()
        if stage != 2:
            mod_upto(48)
            ffn(XDp, XDpR, 8, 0, 1, 2, 0, modn=1)
            mod_upto(60)
            ffn(XDp, XDpR, 8, 1, 0, 0, 0, modn=1)
            mod_upto(60)
            hyena(XDp, XDpR, 256, 4, 0)
            mod_upto(72)
            ffn(XDp, XDpR, 8, 1, 1, 2, 0, dst=yp)
        else:
            out_copy(XDp, XDpR, yp, 8)
    if do_s:
        mod_upto(72)
        ffn(XDs, XDsR, 16, 0, 0, 0, 1, src=xs)
        attn_lat()
        if stage != 3:
            ffn(XDs, XDsR, 16, 0, 1, 2, 1)
            ffn(XDs, XDsR, 16, 1, 0, 0, 1)
            hyena(XDs, XDsR, 2048, 1, 1, own=True)
            ffn(XDo, XDoR, 4, 1, 1, 2, 1, dst=yso)
        else:
            out_copy(XDs, XDsR, ys, 16)
    P.emit()
    return nc

def _bf(a):
    return np.asarray(a, dtype=np.float32).astype(ml_dtypes.bfloat16)


def prep_inputs(inp):
    f = lambda k: np.ascontiguousarray(np.asarray(inp[k], dtype=np.float32))
    shared = {
        "mod_w": np.ascontiguousarray(f("mod_w").reshape(2, 8, 128, 36, 256).transpose(0, 3, 2, 1, 4)
                                      ).reshape(2, 36, 128, 8 * 256),
        "mod_b": f("mod_b"), "ln_g": f("ln_g"), "ln_b": f("ln_b"),
        "ffn_w_in": np.ascontiguousarray(f("ffn_w_in").reshape(2, 2, 8, 128, 2, 11, 256).transpose(0, 1, 5, 3, 2, 4, 6)
                                         ).reshape(2, 2, 11, 128, 8 * 2 * 256),
        "ffn_w_out": f("ffn_w_out"),
        "attn_w_qkv_t": np.ascontiguousarray(f("attn_w_qkv")[0].reshape(8, 128, 24, 128).transpose(2, 1, 0, 3)
                                             ).reshape(24, 128, 8 * 128),
        "hy_w_in_t": np.ascontiguousarray(f("hy_w_in")[0].reshape(8, 128, 24, 128).transpose(2, 1, 0, 3)
                                          ).reshape(24, 128, 8 * 128),
        "ident": _bf(np.eye(128)),
        "attn_w_qkv": f("attn_w_qkv")[0], "attn_w_o": f("attn_w_o")[0],
        "btab": make_btab(inp["attn_rpb"][0]),
    }
    shared.update(hyena_consts(inp))
    xpa = f("x_prompt"); xsa = f("x_sample"); cka = f("cache_k"); cva = f("cache_v")
    ca = f("c"); cctx = f("c_ctx")
    maps = []
    for c in range(8):
        b = c // 4
        cond = np.stack([cctx, ca[b]], 0)
        condT = np.ascontiguousarray(cond.reshape(2, 8, 128).transpose(2, 0, 1))
        m = dict(shared)
        m.update({
            "xp": np.ascontiguousarray(xpa[4 * c:4 * c + 4].reshape(1024, 1024)),
            "xs": np.ascontiguousarray(xsa[b]),
            "ck": np.ascontiguousarray(cka[b, 0].reshape(256, 1024)),
            "cv": np.ascontiguousarray(cva[b, 0].reshape(256, 1024)),
            "condT": condT,
            "idx_tok": np.ascontiguousarray(((c % 4) * 512 + np.arange(4)[None, :] * 128
                                             + np.arange(128)[:, None]).astype(np.int32)),
            "idx_ch": np.ascontiguousarray(((c % 4) * 1024 + np.arange(8)[None, :] * 128
                                            + np.arange(128)[:, None]).astype(np.int32)),
            "inv_own": np.ascontiguousarray(shared["inv2048"][c % 4]),
        })
        maps.append(m)
    return maps


def make_btab(rpb):
    rpb = np.asarray(rpb, dtype=np.float32)
    out = np.empty((16, 4, 2, 64, 8, 8, 64), np.float32)
    a = np.arange(2)[:, None, None]; t = np.arange(8)[None, :, None]; qr = np.arange(8)[None, None, :]
    kcol = np.arange(64)[:, None]; qcol = np.arange(64)[None, :]
    cstart = np.clip(qcol - 8, 0, 48)
    vcol = (kcol >= cstart) & (kcol < cstart + 16)
    dc = np.clip(kcol - qcol + 15, 0, 30)
    for j in range(4):
        hs = min(max(8 * j - 4, 0), 16)
        kr = hs + 2 * t + a
        r = 8 * j + qr
        rs = np.clip(r - 4, 0, 24)
        vrow = (kr >= rs) & (kr < rs + 8)
        dr = np.clip(kr - r + 7, 0, 14)
        val = rpb[:, dr[:, None, :, :, None], dc[None, :, None, None, :]]
        ok = vrow[:, None, :, :, None] & vcol[None, :, None, None, :]
        out[:, j] = np.where(ok[None], val, np.float32(-30000.0))
    return np.ascontiguousarray(out.reshape(16, 4, 128, 8, 512))


def hyena_consts(inp):
    f = lambda k: np.asarray(inp[k], dtype=np.float32)[0]
    cw = f("hy_conv_w"); cb = f("hy_conv_b"); hb = f("hy_bias")
    out = {
        "hy_w_in": np.ascontiguousarray(f("hy_w_in")), "hy_w_out": np.ascontiguousarray(f("hy_w_out")),
        "hy_cw": np.ascontiguousarray(cw.T.reshape(24, 128, 3).transpose(1, 0, 2)),
        "hy_cb": np.ascontiguousarray(cb.reshape(24, 128).T),
        "hy_biasT": np.ascontiguousarray(hb.reshape(2, 8, 128).transpose(2, 1, 0)),
        "hy_w1": np.ascontiguousarray(f("hy_f_w1")), "hy_w2": np.ascontiguousarray(f("hy_f_w2")),
        "hy_w3a": np.ascontiguousarray(np.concatenate([f("hy_f_w3"), f("hy_f_b3")[None, :]], 0)),
        "hy_fb": np.ascontiguousarray(np.stack([f("hy_f_b1"), f("hy_f_freq1"), f("hy_f_b2"), f("hy_f_freq2")], 1)),
    }
    deltas = np.abs(np.linspace(np.log(1e-2) / 1.5, np.log(1e-2) / 0.3, 1024, dtype=np.float32)).astype(np.float64)
    for L in (256, 2048):
        pos = np.arange(L, dtype=np.float64)[:, None]
        t = pos / max(L - 1, 1)
        bands = np.linspace(1e-4, 15, 16, dtype=np.float32).astype(np.float64)
        ang = 2.0 * np.pi * pos / L * bands
        z = np.concatenate([t, np.cos(ang), -np.sin(ang)], -1)
        out["zT%d" % L] = np.ascontiguousarray(z.T.astype(np.float32))
        dec = np.exp(-t * deltas[None, :])
        dec2 = np.stack([dec, dec], 1)
        dec2[0, 1, :] = 0.0
        out["decay%d" % L] = np.ascontiguousarray(dec2.astype(np.float32))
        Fp = 384 if L == 256 else 2176
        N = 2 * L
        fidx = np.arange(Fp, dtype=np.float64)[None, :]
        valid = (fidx <= L)
        th = 2.0 * np.pi * pos * fidx / N
        fw = np.concatenate([np.cos(th) * valid, -np.sin(th) * valid], 1)
        ntc = L // 128; nfc = Fp // 128
        fwt = fw.reshape(ntc, 128, 2, nfc, 128).transpose(3, 1, 0, 2, 4)
        out["fwd%d" % L] = _bf(np.ascontiguousarray(fwt).reshape(nfc, 128, ntc * 2 * 128))
        wf = np.where((fidx == 0) | (fidx == L), 1.0, 2.0) * valid / N
        iv = np.concatenate([(np.cos(th) * wf).T, (-np.sin(th) * wf).T], 0)
        TB = 256 if L == 256 else 512
        ivt = iv.reshape(2 * nfc, 128, L // TB, TB).transpose(2, 1, 0, 3)
        out["inv%d" % L] = _bf(np.ascontiguousarray(ivt).reshape(L // TB, 128, 2 * nfc * TB))
    return out


def kernel(**inputs):
    maps = prep_inputs(inputs)
    nc = build(99)
    res = run_bass_kernel_spmd(nc, maps, core_ids=list(range(8)))
    r = res.results
    yp = np.concatenate([np.asarray(r[c]["yp"], np.float32).reshape(4, 256, 1024) for c in range(8)], 0)
    ys = np.stack([np.concatenate([np.asarray(r[4 * b + q]["yso"], np.float32) for q in range(4)], 0)
                   for b in range(2)], 0)
    nk = np.concatenate([np.asarray(r[c]["nk"], np.float32).reshape(4, 1, 256, 16, 64) for c in range(8)], 0)
    nv = np.concatenate([np.asarray(r[c]["nv"], np.float32).reshape(4, 1, 256, 16, 64) for c in range(8)], 0)
    return (yp, ys, nk, nv)
```

```python
import numpy as np
import concourse.bass as bass
import concourse.mybir as mybir
from contextlib import ExitStack

F32 = mybir.dt.float32
BF16 = mybir.dt.bfloat16
I32 = mybir.dt.int32
ALU = mybir.AluOpType
AF = mybir.ActivationFunctionType
AX = mybir.AxisListType

COMPUTE = ("pe", "act", "dve", "pool")
NDMA_SLOTS = 12


class Res:
    __slots__ = ("name", "last_w", "readers")

    def __init__(self, name=""):
        self.name = name
        self.last_w = None
        self.readers = []


class Op:
    __slots__ = ("eng", "idx", "fn", "waits", "needs_inc", "is_dma", "slot", "slot_val", "clock", "cnt")

    def __init__(self, eng, idx, fn, is_dma):
        self.eng = eng
        self.idx = idx
        self.fn = fn
        self.waits = []
        self.needs_inc = False
        self.is_dma = is_dma
        self.slot = None
        self.slot_val = 0
        self.clock = None
        self.cnt = 0


class Prog:
    def __init__(self, nc):
        self.nc = nc
        self.engs = {"pe": nc.tensor, "act": nc.scalar, "dve": nc.vector, "pool": nc.gpsimd, "sp": nc.sync}
        self.ops = {e: [] for e in self.engs}
        self.clock = {e: {f: -1 for f in COMPUTE} for e in self.engs}
        self.known_dma = {e: set() for e in self.engs}
        self.slot_last = {e: [None] * NDMA_SLOTS for e in self.engs}
        self.slot_cnt = {e: [0] * NDMA_SLOTS for e in self.engs}
        self.slot_rr = {e: 0 for e in self.engs}
        self.stack = ExitStack()

    def sb(self, name, shape, dt):
        return self.stack.enter_context(self.nc.sbuf_tensor(name, list(shape), dt))

    def ps(self, name, shape, dt=F32):
        return self.stack.enter_context(self.nc.psum_tensor(name, list(shape), dt))

    def add(self, eng, fn, reads=(), writes=(), dma=False):
        op = Op(eng, len(self.ops[eng]), fn, dma)
        deps = []
        for r in reads:
            if r.last_w is not None:
                deps.append(r.last_w)
        for w in writes:
            if w.last_w is not None:
                deps.append(w.last_w)
            deps.extend(w.readers)
        ck = self.clock[eng]
        kd = self.known_dma[eng]
        if dma:
            s = self.slot_rr[eng]
            self.slot_rr[eng] = (s + 1) % NDMA_SLOTS
            prev = self.slot_last[eng][s]
            if prev is not None:
                deps.append(prev)
            self.slot_cnt[eng][s] += 1
            op.slot = s
            op.slot_val = 16 * self.slot_cnt[eng][s]
            self.slot_last[eng][s] = op
        seen = set()
        for d in deps:
            if id(d) in seen:
                continue
            seen.add(id(d))
            if not d.is_dma and d.eng not in COMPUTE:
                continue
            if d.is_dma:
                if id(d) in kd:
                    continue
                kd.add(id(d))
                op.waits.append(d)
                for f in COMPUTE:
                    if d.clock[f] > ck[f]:
                        ck[f] = d.clock[f]
            else:
                if d.eng == eng and eng == "pe":
                    continue
                if ck[d.eng] >= d.idx:
                    continue
                d.needs_inc = True
                op.waits.append(d)
                for f in COMPUTE:
                    if d.clock[f] > ck[f]:
                        ck[f] = d.clock[f]
                if d.idx > ck[d.eng]:
                    ck[d.eng] = d.idx
        op.clock = dict(ck)
        if eng == "pe":
            pass
        self.ops[eng].append(op)
        for r in reads:
            r.readers.append(op)
        for w in writes:
            w.last_w = op
            w.readers = []
        return op


    def phase(self):
        ph = ExitStack()
        return ph

    def psb(self, ph, name, shape, dt):
        self._uid = getattr(self, "_uid", 0) + 1
        return ph.enter_context(self.nc.sbuf_tensor("%s_%d" % (name, self._uid), list(shape), dt))

    def fence(self, tiny):
        if not hasattr(self, "_fres"):
            self._fres = {e: Res("fence_" + e) for e in COMPUTE}
        outstanding = []
        for e in self.engs:
            for s in range(NDMA_SLOTS):
                if self.slot_last[e][s] is not None:
                    outstanding.append(self.slot_last[e][s])
        dres = Res("fence_dma")
        for e in COMPUTE:
            op = self.add(e, tiny[e], writes=[self._fres[e]])
            for d in outstanding:
                if id(d) not in self.known_dma[e]:
                    self.known_dma[e].add(id(d))
                    op.waits.append(d)
        for e in self.engs:
            if e in COMPUTE:
                self.add(e, tiny[e], reads=[self._fres[x] for x in COMPUTE if x != e], writes=[self._fres[e]])
            else:
                op = self.add(e, lambda eng: None, reads=[self._fres[x] for x in COMPUTE])
                for d in outstanding:
                    if id(d) not in self.known_dma[e]:
                        self.known_dma[e].add(id(d))
                        op.waits.append(d)

    def emit(self, final_waits=()):
        nc = self.nc
        st = self.stack
        sems = {e: st.enter_context(nc.semaphore("s_" + e)) for e in COMPUTE}
        dsems = {e: [st.enter_context(nc.semaphore("d_%s%d" % (e, i))) for i in range(NDMA_SLOTS)]
                 for e in self.engs}
        for e in COMPUTE:
            c = 0
            for op in self.ops[e]:
                if op.needs_inc:
                    c += 1
                op.cnt = c
        block = st.enter_context(nc.Block())
        self.n_waits = 0

        def run(ename, eng):
            for op in self.ops[ename]:
                for d in op.waits:
                    if d.is_dma:
                        eng.wait_ge(dsems[d.eng][d.slot], d.slot_val)
                    else:
                        eng.wait_ge(sems[d.eng], d.cnt)
                    self.n_waits += 1
                ins = op.fn(eng)
                if ins is None:
                    continue
                if op.is_dma:
                    ins.then_inc(dsems[ename][op.slot], 16)
                elif op.needs_inc:
                    ins.then_inc(sems[ename], 1)
            for s in range(NDMA_SLOTS):
                last = self.slot_last[ename][s]
                if last is not None:
                    eng.wait_ge(dsems[ename][s], last.slot_val)

        block.tensor(lambda eng: run("pe", eng))
        block.scalar(lambda eng: run("act", eng))
        block.vector(lambda eng: run("dve", eng))
        block.gpsimd(lambda eng: run("pool", eng))
        block.sync(lambda eng: run("sp", eng))
        st.close()

import ml_dtypes
from concourse.bass_utils import run_bass_kernel_spmd

D = 1024
DFF = 2816
NJ = 22
NH = 16
DH = 64
ALPHA_C = float(4.0 ** 0.25)
LN_EPS = 1e-5
NEG = -30000.0
PI = float(np.pi)


def build(stage=99):
    nc = bass.Bass("TRN2", target_bir_lowering=False)
    P = Prog(nc)

    def din(name, shape, dt=F32):
        return nc.dram_tensor(name, list(shape), dt, kind="ExternalInput").ap()

    def dout(name, shape):
        return nc.dram_tensor(name, list(shape), F32, kind="ExternalOutput").ap()

    def dscr(name, shape, dt=F32):
        return nc.dram_tensor(name, list(shape), dt).ap()

    xp = din("xp", [1024, D]); xs = din("xs", [2048, D])
    ck = din("ck", [256, D]); cv = din("cv", [256, D])
    condT = din("condT", [128, 2, 8])
    mod_w = din("mod_w", [2, 36, 128, 8 * 256]); mod_b = din("mod_b", [2, 9 * D])
    ln_g = din("ln_g", [2, 3, D]); ln_b = din("ln_b", [2, 3, D])
    w_in = din("ffn_w_in", [2, 2, NJ // 2, 128, 8 * 2 * 256]); w_out = din("ffn_w_out", [2, 2, DFF, D])
    w_qkv = din("attn_w_qkv", [D, 3 * D]); w_o = din("attn_w_o", [D, D])
    w_qkv_t = din("attn_w_qkv_t", [24, 128, 8 * 128]); hy_w_in_t = din("hy_w_in_t", [24, 128, 8 * 128])
    btab = din("btab", [NH, 4, 128, 8, 512])
    hy_w_in = din("hy_w_in", [D, 3 * D]); hy_w_out = din("hy_w_out", [D, D])
    hy_cw = din("hy_cw", [128, 24, 3]); hy_cb = din("hy_cb", [128, 24])
    hy_biasT = din("hy_biasT", [128, 8, 2])
    hy_w1 = din("hy_w1", [33, 64]); hy_w2 = din("hy_w2", [64, 64]); hy_w3a = din("hy_w3a", [65, 4096])
    hy_fb = din("hy_fb", [64, 4])
    zT = {256: din("zT256", [33, 256]), 2048: din("zT2048", [33, 2048])}
    decay = {256: din("decay256", [256, 2, D]), 2048: din("decay2048", [2048, 2, D])}
    FP = {256: 384, 2048: 2176}
    TBL = {256: 256, 2048: 512}
    fwd = {L: din("fwd%d" % L, [FP[L] // 128, 128, (L // 128) * 2 * 128], BF16) for L in (256, 2048)}
    inv = {L: din("inv%d" % L, [L // TBL[L], 128, 2 * (FP[L] // 128) * TBL[L]], BF16) for L in (256, 2048)}
    ident_d = din("ident", [128, 128], BF16)
    idx_tok_d = din("idx_tok", [128, 4], I32)
    idx_ch_d = din("idx_ch", [128, 8], I32)
    inv_own = din("inv_own", [128, 2 * (FP[2048] // 128) * 512], BF16)
    yso = dout("yso", [512, D])
    XDo = dscr("xdo", [512, D]); XDoR = [Res() for _ in range(4)]
    ZDb = dscr("zdb", [4 * D, 512]); ZDbR = Res()
    U2b = dscr("u2b", [4 * D, 512]); U2bR = Res()
    yp = dout("yp", [1024, D])
    ys = dout("ys", [2048, D]) if stage != 99 else None
    nk = dout("nk", [1024, D]); nv = dout("nv", [1024, D])
    modv = dscr("modv", [2, 3, 2, 3, D]); modvR = Res("modv")
    XDp = dscr("xdp", [1024, D]); XDs = dscr("xds", [2048, D])
    XDpR = [Res() for _ in range(8)]; XDsR = [Res() for _ in range(16)]
    KTD = dscr("ktd", [128, 8, 2304], BF16); KTDR = Res()
    VD = dscr("vd", [2304, D], BF16); VDR = Res()
    UD = dscr("ud", [3, D, 2048]); UDR = Res()
    ZD = dscr("zd", [D, 2048]); ZDR = Res()
    ZFD = dscr("zfd", [D, 2048], BF16); ZFDR = Res()
    KD = {L: dscr("kd%d" % L, [2, 2 * FP[L], D]) for L in (256, 2048)}; KDR = Res()

    MOD = P.sb("mod", [128, 3, D], F32); MODR = Res("mod")
    LNGB = P.sb("lngb", [128, 2, D], F32); LNGBR = Res("lngb")
    ident_bf = P.sb("ident_bf", [128, 128], BF16); identR = Res("ident")
    ones_bf = P.sb("ones_bf", [128, 128], BF16); onesR = Res("ones")
    EPS = P.sb("eps", [128, 1], F32); epsR = Res("eps")
    XB = [P.sb("xb%d" % i, [128, D], BF16) for i in range(2)]; XBR = [Res() for _ in range(2)]
    TMPF = [P.sb("tmpf%d" % i, [128, D], F32) for i in range(2)]; TMPFR = [Res() for _ in range(2)]
    ST = [P.sb("st%d" % i, [128, 2, 6], F32) for i in range(2)]
    MV = [P.sb("mv%d" % i, [128, 4], F32) for i in range(2)]
    STR = [Res() for _ in range(2)]
    SCR = {e: P.sb("scr_" + e, [128, 2], F32) for e in ("act", "dve", "pool")}
    PB = [P.ps("pb%d" % i, [128, 512], F32) for i in range(8)]
    PBR = [Res("pb%d" % i) for i in range(8)]
    rot = {"xb": 0, "st": 0, "pt": 0, "nb": 0}

    def nb():
        rot["nb"] = (rot["nb"] + 1) % 6
        return 2 + rot["nb"]

    P.add("sp", lambda e: e.dma_start(out=ident_bf[:], in_=ident_d), writes=[identR], dma=True)
    P.add("dve", lambda e: e.memset(EPS[:], LN_EPS), writes=[epsR])
    P.add("dve", lambda e: e.memset(ones_bf[:], 1.0), writes=[onesR])
    P.add("act", lambda e: e.activation(out=SCR["act"][:, 0:2], in_=EPS[:, 0:1].to_broadcast([128, 2]),
                                        func=AF.Identity), reads=[epsR])
    tiny = {
        "pe": lambda e: e.matmul(PB[7][0:1, 0:1], lhsT=ident_bf[:, 0:1], rhs=ident_bf[:, 0:1], start=True, stop=True),
        "act": lambda e: e.activation(out=SCR["act"][:, 0:1], in_=SCR["act"][:, 1:2], func=AF.Identity),
        "dve": lambda e: e.memset(SCR["dve"][:, 0:1], 0.0),
        "pool": lambda e: e.memset(SCR["pool"][:, 0:1], 0.0),
    }

    def fence():
        P.add("pe", tiny["pe"], reads=[identR], writes=[PBR[7]])
        P.fence(tiny)

    def endph(ph):
        fence(); ph.close()

    def ldx(XD, XDR, t, buf, bufR):
        P.add("sp", lambda e: e.dma_start(out=buf[:], in_=XD[t * 128:(t + 1) * 128, :]), reads=[XDR[t]],
              writes=[bufR], dma=True)

    def stx(XD, XDR, t, buf, bufR):
        P.add("act", lambda e: e.dma_start(out=XD[t * 128:(t + 1) * 128, :], in_=buf[:]), reads=[bufR],
              writes=[XDR[t]], dma=True)

    cT = P.sb("cT", [128, 2, 8], F32); cTR = Res()
    sT = P.sb("sT", [128, 2, 8], F32)
    s2 = P.sb("s2", [128, 8, 2], BF16); s2R = Res()
    WM = [P.sb("wm%d" % i, [128, 8, 256], BF16) for i in range(2)]; WMR = [Res() for _ in range(2)]
    MRS = [P.sb("mrs%d" % i, [2, 2, 256], F32) for i in range(2)]; MRSR = [Res() for _ in range(2)]
    P.add("sp", lambda e: e.dma_start(out=cT[:], in_=condT), writes=[cTR], dma=True)
    P.add("act", lambda e: e.activation(out=sT[:], in_=cT[:], func=AF.Silu), reads=[cTR], writes=[cTR])
    for r in range(2):
        P.add("dve", lambda e, r=r: e.tensor_copy(out=s2[:, :, r], in_=sT[:, r, :]), reads=[cTR], writes=[s2R])
    mod_tasks = [(l, sb_, ci) for l in range(2) for sb_ in range(3) for ci in range(12)]
    mod_pos = [0]

    def mod_upto(n):
        mod_chunks(max(0, n - mod_pos[0]))

    def mod_chunks(n):
        for _ in range(n):
            if mod_pos[0] >= len(mod_tasks):
                return
            layer, sub, ci = mod_tasks[mod_pos[0]]
            k = mod_pos[0] % 2
            mod_pos[0] += 1
            v, q = ci // 4, ci % 4
            wb, wbR, mr, mrR = WM[k], WMR[k], MRS[k], MRSR[k]
            c0 = (sub * 3 + v) * D + q * 256
            P.add("pool", lambda e, wb=wb, c0=c0, layer=layer: e.dma_start(
                out=wb[:].rearrange("p a b -> p (a b)"), in_=mod_w[layer, c0 // 256]), writes=[wbR], dma=True)
            P.add("sp", lambda e, mr=mr, c0=c0, layer=layer: e.dma_start(
                out=mr[:, 1, :], in_=mod_b[layer, c0:c0 + 256].partition_broadcast(2)), writes=[mrR], dma=True)
            bk = nb()
            for kc in range(8):
                P.add("pe", lambda e, wb=wb, kc=kc, bk=bk: e.matmul(
                    PB[bk][0:2, 0:256], lhsT=s2[:, kc, :], rhs=wb[:, kc, :], start=(kc == 0), stop=(kc == 7)),
                    reads=[s2R, wbR], writes=[PBR[bk]])
            addc = 1.0 if v == 1 else 0.0
            P.add("dve", lambda e, bk=bk, mr=mr, addc=addc: e.scalar_tensor_tensor(
                out=mr[:, 0, :], in0=PB[bk][0:2, 0:256], scalar=addc, in1=mr[:, 1, :], op0=ALU.add, op1=ALU.add),
                reads=[PBR[bk], mrR], writes=[mrR])
            P.add("act", lambda e, mr=mr, layer=layer, sub=sub, v=v, q=q: e.dma_start(
                out=modv[layer, sub, :, v, q * 256:(q + 1) * 256], in_=mr[:, 0, :]),
                reads=[mrR], writes=[modvR], dma=True)

    def load_mod(layer, sub, row):
        P.add("sp", lambda e: e.dma_start(out=MOD[:], in_=modv[layer, sub, row].partition_broadcast(128)),
              reads=[modvR], writes=[MODR], dma=True)
        P.add("sp", lambda e: e.dma_start(out=LNGB[:, 0, :], in_=ln_g[layer, sub].partition_broadcast(128)),
              writes=[LNGBR], dma=True)
        P.add("sp", lambda e: e.dma_start(out=LNGB[:, 1, :], in_=ln_b[layer, sub].partition_broadcast(128)),
              writes=[LNGBR], dma=True)

    def layer_norm(Xt, XtR):
        k = rot["st"]; rot["st"] ^= 1
        st, mv, sR = ST[k], MV[k], STR[k]
        for c in range(2):
            P.add("dve", lambda e, c=c: e.bn_stats(out=st[:, c, :], in_=Xt[:, c * 512:(c + 1) * 512]),
                  reads=[XtR], writes=[sR])
        P.add("dve", lambda e: e.bn_aggr(out=mv[:, 0:2], in_=st[:]), reads=[sR], writes=[sR])
        P.add("act", lambda e: e.activation(out=mv[:, 2:3], in_=mv[:, 1:2], func=AF.Sqrt, bias=EPS[:, 0:1],
                                            scale=1.0), reads=[sR, epsR], writes=[sR])
        P.add("dve", lambda e: e.reciprocal(out=mv[:, 3:4], in_=mv[:, 2:3]), reads=[sR], writes=[sR])
        P.add("dve", lambda e: e.scalar_tensor_tensor(out=mv[:, 2:3], in0=mv[:, 0:1], scalar=-1.0, in1=mv[:, 3:4],
                                                      op0=ALU.mult, op1=ALU.mult), reads=[sR], writes=[sR])
        P.add("act", lambda e: e.activation(out=Xt[:], in_=Xt[:], func=AF.Identity, scale=mv[:, 3:4],
                                            bias=mv[:, 2:3]), reads=[XtR, sR], writes=[XtR])
        P.add("dve", lambda e: e.tensor_tensor(out=Xt[:], in0=Xt[:], in1=LNGB[:, 0, :], op=ALU.mult),
              reads=[XtR, LNGBR], writes=[XtR])
        P.add("dve", lambda e: e.tensor_tensor(out=Xt[:], in0=Xt[:], in1=LNGB[:, 1, :], op=ALU.add),
              reads=[XtR, LNGBR], writes=[XtR])

    def residual_ln(Xt, XtR, bk, n, gscale):
        k = rot["xb"]; rot["xb"] ^= 1
        tf, tfR = TMPF[k], TMPFR[k]
        P.add("dve", lambda e: e.scalar_tensor_tensor(
            out=tf[:, 0:512], in0=PB[bk][:], scalar=gscale, in1=MOD[:, 2, n * 512:(n + 1) * 512],
            op0=ALU.mult, op1=ALU.mult), reads=[PBR[bk], MODR], writes=[tfR])
        P.add("dve", lambda e: e.scalar_tensor_tensor(
            out=Xt[:, n * 512:(n + 1) * 512], in0=Xt[:, n * 512:(n + 1) * 512], scalar=ALPHA_C,
            in1=tf[:, 0:512], op0=ALU.mult, op1=ALU.add), reads=[XtR, tfR], writes=[XtR])

    def transpose_tile(xb, xbR, dst3, dstR, nchunk=8):
        b = rot["pt"]; rot["pt"] ^= 1
        pt = PB[b][:].bitcast(BF16)
        for c in range(nchunk):
            P.add("pe", lambda e, c=c: e.transpose(pt[:, c * 128:(c + 1) * 128], xb[:, c * 128:(c + 1) * 128],
                                                   ident_bf[:]), reads=[xbR, identR], writes=[PBR[b]])
        P.add("act", lambda e: e.activation(out=dst3, in_=pt[:, 0:nchunk * 128].rearrange("p (c t) -> p c t", c=nchunk),
                                            func=AF.Identity), reads=[PBR[b]],
              writes=(dstR if isinstance(dstR, list) else [dstR]))

    def mod_transpose(Xt, XtR, dst3, dstR):
        k = rot["xb"]; rot["xb"] ^= 1
        tf, tfR, xb, xbR = TMPF[k], TMPFR[k], XB[k], XBR[k]
        P.add("dve", lambda e: e.tensor_tensor(out=tf[:], in0=Xt[:], in1=MOD[:, 1, :], op=ALU.mult),
              reads=[XtR, MODR], writes=[tfR])
        P.add("dve", lambda e: e.tensor_tensor(out=xb[:], in0=tf[:], in1=MOD[:, 0, :], op=ALU.add),
              reads=[tfR, MODR], writes=[xbR])
        transpose_tile(xb, xbR, dst3, dstR)

    def proj_fm(ph, tag, hT, hTR, ntok, wsrc, col0, nchunks, evac, wt=None, wbuf=None):
        if wbuf is not None:
            WB, WBR = wbuf
        else:
            WB = [P.psb(ph, "%s_w%d" % (tag, i), [128, 8, 128], BF16) for i in range(3)]
            WBR = [Res() for _ in range(3)]
        nblk = (ntok + 511) // 512
        for mc in range(nchunks):
            wb, wbR = WB[mc % 3], WBR[mc % 3]
            c0 = col0 + mc * 128
            if wt is not None:
                P.add("pool", lambda e, wb=wb, c0=c0: e.dma_start(
                    out=wb[:].rearrange("p a b -> p (a b)"), in_=wt[c0 // 128]), writes=[wbR], dma=True)
            else:
                P.add("pool", lambda e, wb=wb, c0=c0: e.dma_start(
                    out=wb[:], in_=wsrc[:, c0:c0 + 128].rearrange("(kc p) n -> p kc n", p=128)), writes=[wbR], dma=True)
            for blk in range(nblk):
                n = min(512, ntok - blk * 512)
                bk = nb()
                for kc in range(8):
                    P.add("pe", lambda e, wb=wb, kc=kc, bk=bk, blk=blk, n=n: e.matmul(
                        PB[bk][:, 0:n], lhsT=wb[:, kc, :], rhs=hT[:, kc, blk * 512:blk * 512 + n],
                        start=(kc == 0), stop=(kc == 7)), reads=[wbR, hTR], writes=[PBR[bk]])
                evac(mc, blk, n, bk)

    def proj_tm(ph, tag, hT, hTR, ntiles, wsrc, col0, nn, evac):
        WB = [P.psb(ph, "%s_w%d" % (tag, i), [128, 8, 512], BF16) for i in range(2)]; WBR = [Res() for _ in range(2)]
        for n in range(nn):
            wb, wbR = WB[n % 2], WBR[n % 2]
            c0 = col0 + n * 512
            P.add("pool", lambda e, wb=wb, c0=c0: e.dma_start(
                out=wb[:], in_=wsrc[:, c0:c0 + 512].rearrange("(kc p) n -> p kc n", p=128)), writes=[wbR], dma=True)
            for t in range(ntiles):
                bk = nb()
                for kc in range(8):
                    P.add("pe", lambda e, wb=wb, kc=kc, bk=bk, t=t: e.matmul(
                        PB[bk][:], lhsT=hT[:, kc, t * 128:(t + 1) * 128], rhs=wb[:, kc, :],
                        start=(kc == 0), stop=(kc == 7)), reads=[wbR, hTR], writes=[PBR[bk]])
                evac(n, t, bk)

    def ffn(XD, XDR, ntiles, layer, idx, sub, row, dst=None, modn=0, src=None):
        ph = P.phase()
        XMT = P.psb(ph, "xmt", [128, 8, 1024], BF16); XMTR = [Res(), Res()]
        AT = P.psb(ph, "at", [128, NJ, 1024], BF16); ATR = [Res() for _ in range(NJ)]
        WO = P.psb(ph, "wo", [128, NJ, D], BF16); WOR = [Res(), Res()]
        WI = [P.psb(ph, "wi%d" % i, [128, 8, 2, 256], BF16) for i in range(2)]; WIR = [Res() for _ in range(2)]
        SG = [P.psb(ph, "sg%d" % i, [128, 512], F32) for i in range(2)]; SGR = [Res() for _ in range(2)]
        XT = [P.psb(ph, "xt%d" % i, [128, D], F32) for i in range(8)]; XTR = [Res() for _ in range(8)]
        load_mod(layer, sub, row)
        def load_wo():
            for hf in range(2):
                P.add("pool", lambda e, hf=hf: e.dma_start(
                    out=WO[:, hf * 11:(hf + 1) * 11, :],
                    in_=w_out[layer, idx, hf * 1408:(hf + 1) * 1408, :].rearrange("(j p) n -> p j n", p=128)),
                    writes=[WOR[hf]], dma=True)
        wi_src = w_in[layer, idx]
        tpb = min(8, ntiles)
        nsbk = tpb // 4
        for blk in range(ntiles // tpb):
            for ti in range(tpb):
                t = blk * tpb + ti
                if src is not None:
                    P.add("sp", lambda e, t=t, ti=ti: e.dma_start(out=XT[ti][:], in_=src[t * 128:(t + 1) * 128, :]),
                          writes=[XTR[ti]], dma=True)
                else:
                    ldx(XD, XDR, t, XT[ti], XTR[ti])
                mod_transpose(XT[ti], XTR[ti], XMT[:, :, ti * 128:(ti + 1) * 128], XMTR[ti // 4])
            for jg in range(NJ // 2):
                wb, wbR = WI[jg % 2], WIR[jg % 2]
                P.add("pool", lambda e, wb=wb, jg=jg: e.dma_start(
                    out=wb[:].rearrange("p a b c -> p (a b c)"), in_=wi_src[jg]), writes=[wbR], dma=True)
                for jj in range(2):
                    j = jg * 2 + jj
                    for sbk in range(nsbk):
                        bg, bu = nb(), nb()
                        for gu, bk in ((0, bg), (1, bu)):
                            for kc in range(8):
                                P.add("pe", lambda e, wb=wb, gu=gu, kc=kc, bk=bk, sbk=sbk, jj=jj: e.matmul(
                                    PB[bk][:], lhsT=wb[:, kc, gu, jj * 128:(jj + 1) * 128],
                                    rhs=XMT[:, kc, sbk * 512:(sbk + 1) * 512],
                                    start=(kc == 0), stop=(kc == 7)), reads=[wbR, XMTR[sbk]], writes=[PBR[bk]])
                        sg, sgR = SG[sbk], SGR[sbk]
                        P.add("act", lambda e, sg=sg, bg=bg: e.activation(out=sg[:], in_=PB[bg][:], func=AF.Silu),
                              reads=[PBR[bg]], writes=[sgR])
                        P.add("dve", lambda e, sg=sg, bu=bu, j=j, sbk=sbk: e.tensor_tensor(
                            out=AT[:, j, sbk * 512:(sbk + 1) * 512], in0=sg[:], in1=PB[bu][:], op=ALU.mult),
                            reads=[sgR, PBR[bu]], writes=[ATR[j]])
                    if modn:
                        mod_chunks(modn)
                if blk == 0 and jg == 1:
                    load_wo()
            for ti in range(tpb):
                t = blk * tpb + ti
                for n in range(2):
                    bk = nb()
                    for j in range(NJ):
                        P.add("pe", lambda e, j=j, ti=ti, n=n, bk=bk: e.matmul(
                            PB[bk][:], lhsT=AT[:, j, ti * 128:(ti + 1) * 128], rhs=WO[:, j, n * 512:(n + 1) * 512],
                            start=(j == 0), stop=(j == NJ - 1)),
                            reads=[ATR[j], WOR[j // 11]], writes=[PBR[bk]])
                    residual_ln(XT[ti], XTR[ti], bk, n, 0.5)
                layer_norm(XT[ti], XTR[ti])
                if dst is not None:
                    P.add("act", lambda e, t=t, ti=ti: e.dma_start(out=dst[t * 128:(t + 1) * 128, :], in_=XT[ti][:]),
                          reads=[XTR[ti]], dma=True)
                else:
                    stx(XD, XDR, t, XT[ti], XTR[ti])
        endph(ph)

    def attn_head(qT_ap, kT_fn, nkt, nq, v_fn, bias_fn, ET, ETR, OT_ap, OTR, REC, RECR, rd, part="sp", p0=0):
        per = 512 // nq
        kt = 0
        while kt < nkt and "s" in part:
            g = min(per, nkt - kt)
            bk = nb()
            for i in range(g):
                b_ap = bias_fn(kt + i)
                P.add("pe", lambda e, i=i, kt=kt, bk=bk, b_ap=b_ap: e.matmul(
                    PB[bk][:, i * nq:(i + 1) * nq], lhsT=kT_fn(kt + i), rhs=qT_ap, start=True, stop=(b_ap is None)),
                    reads=rd[0], writes=[PBR[bk]])
                if b_ap is not None:
                    P.add("pe", lambda e, i=i, bk=bk, b_ap=b_ap: e.matmul(
                        PB[bk][:, i * nq:(i + 1) * nq], lhsT=ident_bf[:], rhs=b_ap, start=False, stop=True),
                        reads=rd[0] + [identR], writes=[PBR[bk]])
            P.add("act", lambda e, kt=kt, g=g, bk=bk: e.activation(
                out=ET[:, kt * nq:(kt + g) * nq], in_=PB[bk][:, 0:g * nq], func=AF.Exp), reads=[PBR[bk]], writes=[ETR])
            kt += g
        if "p" not in part:
            return
        bk = nb()
        for i in range(nkt):
            P.add("pe", lambda e, i=i, bk=bk: e.matmul(PB[bk][:, 0:nq], lhsT=v_fn(i), rhs=ET[:, i * nq:(i + 1) * nq],
                                                       start=(i == 0), stop=(i == nkt - 1)),
                  reads=rd[1] + [ETR], writes=[PBR[bk]])
        bs = nb()
        for i in range(nkt):
            P.add("pe", lambda e, i=i, bs=bs: e.matmul(PB[bs][:, 0:nq], lhsT=ones_bf[:, :],
                                                       rhs=ET[:, i * nq:(i + 1) * nq],
                                                       start=(i == 0), stop=(i == nkt - 1)),
                  reads=[onesR, ETR], writes=[PBR[bs]])
        P.add("dve", lambda e, bs=bs: e.reciprocal(out=REC[p0:p0 + 64, 0:nq], in_=PB[bs][p0:p0 + 64, 0:nq]),
              reads=[PBR[bs]], writes=[RECR])
        P.add("dve", lambda e, bk=bk: e.tensor_tensor(out=OT_ap, in0=PB[bk][p0:p0 + 64, 0:nq],
                                                      in1=REC[p0:p0 + 64, 0:nq], op=ALU.mult),
              reads=[PBR[bk], RECR], writes=[OTR])

    def out_proj_heads(XT, XTR, OT, OTR, WOH, WOHR, tcol):
        for n in range(2):
            bk = nb()
            for h in range(NH // 2):
                P.add("pe", lambda e, h=h, n=n, bk=bk: e.matmul(
                    PB[bk][:], lhsT=OT[:, h, tcol * 128:(tcol + 1) * 128], rhs=WOH[:, h, n * 512:(n + 1) * 512],
                    start=(h == 0), stop=(h == NH // 2 - 1)), reads=[OTR, WOHR], writes=[PBR[bk]])
            residual_ln(XT, XTR, bk, n, 1.0)
        layer_norm(XT, XTR)

    def attn_ctx():
        XD, XDR = XDp, XDpR
        ph = P.phase()
        QKT = P.psb(ph, "qkt", [128, 16, 1024], BF16); QKTR = Res()
        QZ = P.psb(ph, "qz", [128, NH, 1024], BF16)
        P.add("pool", lambda e: e.memset(QZ[:], 0.0), writes=[QKTR])
        VB = P.psb(ph, "vb", [128, 8, D], BF16); VBR = Res()
        pa = P.phase()
        hT = P.psb(pa, "hT", [128, 8, 1024], BF16); hTR = Res()
        XT = [P.psb(pa, "axt%d" % i, [128, D], F32) for i in range(2)]; XTR = [Res() for _ in range(2)]
        KV = [P.psb(pa, "kv%d" % i, [128, 512], F32) for i in range(2)]; KVR = [Res() for _ in range(2)]
        load_mod(0, 1, 0)
        for t in range(8):
            ldx(XD, XDR, t, XT[t % 2], XTR[t % 2])
            mod_transpose(XT[t % 2], XTR[t % 2], hT[:, :, t * 128:(t + 1) * 128], hTR)

        def ev_qk(mc, blk, n, bk):
            if mc < 8:
                for hp in range(2):
                    P.add("act", lambda e, hp=hp: e.activation(
                        out=QZ[hp * 64:(hp + 1) * 64, 2 * mc + hp, blk * 512:blk * 512 + n],
                        in_=PB[bk][hp * 64:(hp + 1) * 64, 0:n], func=AF.Identity, scale=0.125),
                        reads=[PBR[bk]], writes=[QKTR])
                return
            P.add("act", lambda e: e.activation(out=QKT[:, mc, blk * 512:blk * 512 + n], in_=PB[bk][:, 0:n],
                                                func=AF.Identity, scale=1.0), reads=[PBR[bk]], writes=[QKTR])
        proj_fm(pa, "qk", hT, hTR, 1024, w_qkv, 0, 16, ev_qk, wt=w_qkv_t)
        cnt = [0]

        def ev_kv(n, t, bk):
            k = cnt[0] % 2; cnt[0] += 1
            kv, kvR = KV[k], KVR[k]
            P.add("act", lambda e: e.activation(out=kv[:], in_=PB[bk][:], func=AF.Identity), reads=[PBR[bk]],
                  writes=[kvR])
            dst = nk if n < 2 else nv
            cc = (n % 2) * 512
            P.add("act", lambda e: e.dma_start(out=dst[t * 128:(t + 1) * 128, cc:cc + 512], in_=kv[:]), reads=[kvR],
                  dma=True)
            if n >= 2:
                P.add("dve", lambda e: e.tensor_copy(out=VB[:, t, cc:cc + 512], in_=kv[:]), reads=[kvR], writes=[VBR])
        proj_tm(pa, "kv", hT, hTR, 8, w_qkv, 1024, 4, ev_kv)
        mod_chunks(4)
        endph(pa)
        pb_ = P.phase()
        WOH = P.psb(pb_, "woh", [128, NH // 2, D], BF16); WOHR = Res()
        P.add("pool", lambda e: e.dma_start(out=WOH[:], in_=w_o.rearrange("(h p) n -> p h n", p=128)), writes=[WOHR],
              dma=True)
        ETs = [P.psb(pb_, "et%d" % i, [128, 512], BF16) for i in range(2)]; ETRs = [Res() for _ in range(2)]
        OT = P.psb(pb_, "ot", [128, NH // 2, 256], BF16); OTR = Res()
        RECs = [P.psb(pb_, "rec%d" % i, [128, 512], F32) for i in range(2)]; RECRs = [Res() for _ in range(2)]
        XT2 = [P.psb(pb_, "bxt%d" % i, [128, D], F32) for i in range(2)]; XT2R = [Res() for _ in range(2)]
        for s in range(4):
            def head_args(h, s=s):
                p0 = (h % 2) * 64
                qT_ap = QZ[:, h, s * 256:(s + 1) * 256]
                kT_fn = lambda kt, h=h, s=s: QKT[:, 8 + h // 2, s * 256 + kt * 128:s * 256 + (kt + 1) * 128]
                v_fn = lambda kt, h=h, s=s: VB[:, s * 2 + kt, (h // 2) * 128:(h // 2 + 1) * 128]
                return (qT_ap, kT_fn, 2, 256, v_fn, lambda kt: None, ETs[h % 2], ETRs[h % 2],
                        OT[p0:p0 + 64, h // 2, :], OTR, RECs[h % 2], RECRs[h % 2], ([QKTR], [VBR]), p0)
            def run_head(h, part):
                a = head_args(h)
                attn_head(*a[:-1], part=part, p0=a[-1])
            run_head(0, "s")
            for h in range(NH):
                if h + 1 < NH:
                    run_head(h + 1, "s")
                run_head(h, "p")
                if h % 2 == 1:
                    mod_chunks(1)
            for tt in range(2):
                t = s * 2 + tt
                ldx(XD, XDR, t, XT2[tt], XT2R[tt])
                out_proj_heads(XT2[tt], XT2R[tt], OT, OTR, WOH, WOHR, tt)
                stx(XD, XDR, t, XT2[tt], XT2R[tt])
        endph(pb_)
        ph.close()

    def attn_lat():
        XD, XDR = XDs, XDsR
        pa = P.phase()
        hT = P.psb(pa, "hT", [128, 8, 2048], BF16); hTR = Res()
        XT = [P.psb(pa, "axt%d" % i, [128, D], F32) for i in range(2)]; XTR = [Res() for _ in range(2)]
        KS = [P.psb(pa, "ks%d" % i, [128, 512], BF16) for i in range(2)]; KSR = [Res() for _ in range(2)]
        load_mod(0, 1, 1)
        for t in range(16):
            ldx(XD, XDR, t, XT[t % 2], XTR[t % 2])
            mod_transpose(XT[t % 2], XTR[t % 2], hT[:, :, t * 128:(t + 1) * 128], hTR)
        cnt = [0]

        def ev_k(mc, blk, n, bk):
            k = cnt[0] % 2; cnt[0] += 1
            ks, ksR = KS[k], KSR[k]
            P.add("act", lambda e: e.activation(out=ks[:, 0:n], in_=PB[bk][:, 0:n], func=AF.Identity),
                  reads=[PBR[bk]], writes=[ksR])
            P.add("act", lambda e: e.dma_start(out=KTD[:, mc, blk * 512:blk * 512 + n], in_=ks[:, 0:n]), reads=[ksR],
                  writes=[KTDR], dma=True)
        proj_fm(pa, "k", hT, hTR, 2048, w_qkv, 1024, 8, ev_k, wt=w_qkv_t)

        def ev_v(n, t, bk):
            k = cnt[0] % 2; cnt[0] += 1
            ks, ksR = KS[k], KSR[k]
            P.add("act", lambda e: e.activation(out=ks[:], in_=PB[bk][:], func=AF.Identity), reads=[PBR[bk]],
                  writes=[ksR])
            P.add("act", lambda e: e.dma_start(out=VD[t * 128:(t + 1) * 128, n * 512:(n + 1) * 512], in_=ks[:]),
                  reads=[ksR], writes=[VDR], dma=True)
        proj_tm(pa, "v", hT, hTR, 16, w_qkv, 2048, 2, ev_v)
        for kt in range(2):
            xb, xbR = XB[kt], XBR[kt]
            P.add("pool", lambda e, kt=kt, xb=xb: e.dma_start(out=xb[:], in_=ck[kt * 128:(kt + 1) * 128, :]),
                  writes=[xbR], dma=True)
            kc_t = P.psb(pa, "kct%d" % kt, [128, 8, 128], BF16); kcR = Res()
            transpose_tile(xb, xbR, kc_t[:], kcR)
            P.add("act", lambda e, kt=kt, kc_t=kc_t: e.dma_start(out=KTD[:, :, 2048 + kt * 128:2048 + (kt + 1) * 128],
                                                                in_=kc_t[:]), reads=[kcR], writes=[KTDR], dma=True)
            vv = P.psb(pa, "cvt%d" % kt, [128, D], BF16); vvR = Res()
            P.add("pool", lambda e, kt=kt, vv=vv: e.dma_start(out=vv[:], in_=cv[kt * 128:(kt + 1) * 128, :]),
                  writes=[vvR], dma=True)
            P.add("sp", lambda e, kt=kt, vv=vv: e.dma_start(out=VD[2048 + kt * 128:2048 + (kt + 1) * 128, :], in_=vv[:]),
                  reads=[vvR], writes=[VDR], dma=True)
        endph(pa)
        pb_ = P.phase()
        WOH = P.psb(pb_, "woh", [128, NH // 2, D], BF16); WOHR = Res()
        P.add("pool", lambda e: e.dma_start(out=WOH[:], in_=w_o.rearrange("(h p) n -> p h n", p=128)), writes=[WOHR],
              dma=True)
        hTb = P.psb(pb_, "hTb", [128, 8, 512], BF16); hTbR = Res()
        QT = P.psb(pb_, "qt", [128, NH, 512], BF16); QTR = Res()
        P.add("pool", lambda e: e.memset(QT[:], 0.0), writes=[QTR])
        KTh = P.psb(pb_, "kth", [128, 8, 1280], BF16); KThR = Res()
        VBh = P.psb(pb_, "vbh", [128, 10, D], BF16); VBhR = Res()
        BT = [P.psb(pb_, "bt%d" % i, [128, 8, 512], BF16) for i in range(2)]; BTR = [Res() for _ in range(2)]
        ETs = [P.psb(pb_, "et%d" % i, [128, 10 * 512], BF16) for i in range(2)]; ETRs = [Res() for _ in range(2)]
        OT = P.psb(pb_, "ot", [128, NH // 2, 512], BF16); OTR = Res()
        RECs = [P.psb(pb_, "rec%d" % i, [128, 512], F32) for i in range(2)]; RECRs = [Res() for _ in range(2)]
        XT2 = [P.psb(pb_, "bxt%d" % i, [128, D], F32) for i in range(2)]; XT2R = [Res() for _ in range(2)]
        QW = ([P.psb(pb_, "qw%d" % i, [128, 8, 128], BF16) for i in range(3)], [Res() for _ in range(3)])

        def prep(j):
            hs = min(max(8 * j - 4, 0), 16)
            tk0 = hs * 64
            for ti in range(4):
                t = j * 4 + ti
                ldx(XD, XDR, t, XT2[ti % 2], XT2R[ti % 2])
                mod_transpose(XT2[ti % 2], XT2R[ti % 2], hTb[:, :, ti * 128:(ti + 1) * 128], hTbR)

            def ev_q(mc, blk, n, bk):
                for hp in range(2):
                    P.add("act", lambda e, hp=hp: e.activation(
                        out=QT[hp * 64:(hp + 1) * 64, 2 * mc + hp, :], in_=PB[bk][hp * 64:(hp + 1) * 64, :],
                        func=AF.Identity, scale=0.125), reads=[PBR[bk]], writes=[QTR])
            proj_fm(pb_, "q", hTb, hTbR, 512, w_qkv, 0, 8, ev_q, wt=w_qkv_t, wbuf=QW)
            P.add("sp", lambda e, tk0=tk0: e.dma_start(out=KTh[:, :, 0:1024], in_=KTD[:, :, tk0:tk0 + 1024]),
                  reads=[KTDR], writes=[KThR], dma=True)
            P.add("sp", lambda e, tk0=tk0: e.dma_start(
                out=VBh[:, 0:8, :], in_=VD[tk0:tk0 + 1024, :].rearrange("(t p) n -> p t n", p=128)),
                reads=[VDR], writes=[VBhR], dma=True)
            if j == 0:
                P.add("sp", lambda e: e.dma_start(out=KTh[:, :, 1024:1280], in_=KTD[:, :, 2048:2304]),
                      reads=[KTDR], writes=[KThR], dma=True)
                P.add("sp", lambda e: e.dma_start(
                    out=VBh[:, 8:10, :], in_=VD[2048:2304, :].rearrange("(t p) n -> p t n", p=128)),
                    reads=[VDR], writes=[VBhR], dma=True)

        prep(0)
        for j in range(4):
            hs = min(max(8 * j - 4, 0), 16)
            act_t = [t_ for t_ in range(8) if any(
                min(max(8 * j + qr - 4, 0), 24) <= hs + 2 * t_ + a_ < min(max(8 * j + qr - 4, 0), 24) + 8
                for qr in range(8) for a_ in range(2))] + [8, 9]

            def head_args(h, j=j, act_t=act_t):
                bt, btR = BT[h % 2], BTR[h % 2]
                p0 = (h % 2) * 64
                qT_ap = QT[:, h, :]
                kT_fn2 = lambda i, h=h: KTh[:, h // 2, act_t[i] * 128:(act_t[i] + 1) * 128]
                v_fn2 = lambda i, h=h: VBh[:, act_t[i], (h // 2) * 128:(h // 2 + 1) * 128]
                bias_fn2 = lambda i, bt=bt: (bt[:, act_t[i], :] if act_t[i] < 8 else None)
                return (qT_ap, kT_fn2, len(act_t), 512, v_fn2, bias_fn2, ETs[h % 2], ETRs[h % 2],
                        OT[p0:p0 + 64, h // 2, :], OTR, RECs[h % 2], RECRs[h % 2], ([QTR, KThR, btR], [VBhR]), p0)

            def load_bt(h, j=j):
                bt, btR = BT[h % 2], BTR[h % 2]
                P.add("pool", lambda e, bt=bt, h=h, j=j: e.dma_start(out=bt[:], in_=btab[h, j]), writes=[btR], dma=True)
            def run_head(h, part):
                a = head_args(h)
                attn_head(*a[:-1], part=part, p0=a[-1])
            load_bt(0)
            run_head(0, "s")
            for h in range(NH):
                if h + 1 < NH:
                    load_bt(h + 1)
                    run_head(h + 1, "s")
                run_head(h, "p")
            if j + 1 < 4:
                prep(j + 1)
            for ti in range(4):
                t = j * 4 + ti
                ldx(XD, XDR, t, XT2[ti % 2], XT2R[ti % 2])
                out_proj_heads(XT2[ti % 2], XT2R[ti % 2], OT, OTR, WOH, WOHR, ti)
                stx(XD, XDR, t, XT2[ti % 2], XT2R[ti % 2])
        endph(pb_)

    def fwd_dft(pc, L, rhs_fn, rdR_fn, consume):
        ntc = L // 128; nfc = FP[L] // 128
        FB = pc["FB"]; FBR = pc["FBR"]
        res_f = pc.get("res", False)
        for i in range(nfc):
            if res_f:
                fb_, fbR = FB[i], FBR[i]
            else:
                fb_, fbR = FB[i % 2], FBR[i % 2]
                P.add("sp", lambda e, fb_=fb_, i=i: e.dma_start(out=fb_[:].rearrange("p a b c -> p (a b c)"),
                                                                in_=fwd[L][i]), writes=[fbR], dma=True)
            br, bi = nb(), nb()
            for part, bk in ((0, br), (1, bi)):
                for tc in range(ntc):
                    P.add("pe", lambda e, fb_=fb_, part=part, tc=tc, bk=bk: e.matmul(
                        PB[bk][:], lhsT=fb_[:, tc, part, :], rhs=rhs_fn(part, tc),
                        start=(tc == 0), stop=(tc == ntc - 1)), reads=[fbR, rdR_fn(part)], writes=[PBR[bk]])
            consume(i, br, bi)

    def hyena(XD, XDR, L, nseq, row, own=False):
        ntc = L // 128
        ntok = nseq * L; ntiles = ntok // 128
        Fp = FP[L]; nfc = Fp // 128
        TB = TBL[L]; ntb = L // TB
        pf = P.phase()
        zt = P.psb(pf, "zt", [33, L], F32); ztR = Res()
        w1 = P.psb(pf, "w1", [33, 64], F32); w2 = P.psb(pf, "w2", [64, 64], F32); wR = Res()
        w3 = P.psb(pf, "w3", [65, 4096], F32)
        fb = P.psb(pf, "fb", [64, 6], F32); fbR = Res()
        h1 = P.psb(pf, "h1", [64, L], F32); h1R = Res()
        h2 = P.psb(pf, "h2", [65, L], F32); h2R = Res()
        sc1 = P.psb(pf, "sc1", [64, 512], F32); sc2 = P.psb(pf, "sc2", [64, 512], F32); scR = Res()
        HS = P.psb(pf, "hs", [128, ntc, 2, D], BF16); HSR = [Res(), Res()]
        NBF = 2 if L <= 256 else 1
        DEC = [P.psb(pf, "dec%d" % i, [128, 2, D], F32) for i in range(NBF)]; DECR = [Res() for _ in range(NBF)]
        HF = [P.psb(pf, "hf%d" % i, [128, 2, D], F32) for i in range(NBF)]; HFR = [Res() for _ in range(NBF)]
        KO = [P.psb(pf, "ko%d" % i, [128, 512], F32) for i in range(4)]; KOR = [Res() for _ in range(4)]
        pcf = {"FB": [P.psb(pf, "ffb%d" % i, [128, ntc, 2, 128], BF16) for i in range(2)], "FBR": [Res(), Res()]}
        P.add("sp", lambda e: e.dma_start(out=zt[:], in_=zT[L]), writes=[ztR], dma=True)
        P.add("sp", lambda e: e.dma_start(out=w1[:], in_=hy_w1), writes=[wR], dma=True)
        P.add("sp", lambda e: e.dma_start(out=w2[:], in_=hy_w2), writes=[wR], dma=True)
        P.add("sp", lambda e: e.dma_start(out=w3[:], in_=hy_w3a), writes=[wR], dma=True)
        P.add("sp", lambda e: e.dma_start(out=fb[:, 0:4], in_=hy_fb), writes=[fbR], dma=True)
        P.add("dve", lambda e: e.tensor_tensor(out=fb[:, 4:5], in0=fb[:, 0:1], in1=fb[:, 1:2], op=ALU.mult),
              reads=[fbR], writes=[fbR])
        P.add("dve", lambda e: e.tensor_tensor(out=fb[:, 5:6], in0=fb[:, 2:3], in1=fb[:, 3:4], op=ALU.mult),
              reads=[fbR], writes=[fbR])
        P.add("dve", lambda e: e.memset(h2[64:65, :], 1.0), writes=[h2R])
        TS = min(L, 512)
        for (wt, kdim, src, srcR, dstt, dstR, fcol, bcol) in (
                (w1, 33, zt, ztR, h1, h1R, 1, 4), (w2, 64, h1, h1R, h2, h2R, 3, 5)):
            for b in range(L // TS):
                bk = nb()
                P.add("pe", lambda e, wt=wt, kdim=kdim, src=src, b=b, bk=bk: e.matmul(
                    PB[bk][0:64, 0:TS], lhsT=wt[0:kdim, :], rhs=src[0:kdim, b * TS:(b + 1) * TS], start=True, stop=True),
                    reads=[wR, srcR], writes=[PBR[bk]])
                P.add("act", lambda e, bk=bk, fcol=fcol, bcol=bcol: e.activation(
                    out=sc1[:, 0:TS], in_=PB[bk][0:64, 0:TS], func=AF.Identity, scale=fb[:, fcol:fcol + 1],
                    bias=fb[:, bcol:bcol + 1]), reads=[PBR[bk], fbR], writes=[scR])
                for _ in range(2):
                    P.add("dve", lambda e: e.tensor_scalar(out=sc2[:, 0:TS], in0=sc1[:, 0:TS], scalar1=-PI,
                                                           scalar2=2 * PI, op0=ALU.is_lt, op1=ALU.mult),
                          reads=[scR], writes=[scR])
                    P.add("dve", lambda e: e.tensor_tensor(out=sc1[:, 0:TS], in0=sc1[:, 0:TS], in1=sc2[:, 0:TS],
                                                           op=ALU.add), reads=[scR], writes=[scR])
                    P.add("dve", lambda e: e.tensor_scalar(out=sc2[:, 0:TS], in0=sc1[:, 0:TS], scalar1=PI,
                                                           scalar2=-2 * PI, op0=ALU.is_gt, op1=ALU.mult),
                          reads=[scR], writes=[scR])
                    P.add("dve", lambda e: e.tensor_tensor(out=sc1[:, 0:TS], in0=sc1[:, 0:TS], in1=sc2[:, 0:TS],
                                                           op=ALU.add), reads=[scR], writes=[scR])
                P.add("act", lambda e, dstt=dstt, b=b: e.activation(out=dstt[0:64, b * TS:(b + 1) * TS], in_=sc1[:, 0:TS],
                                                                    func=AF.Sin), reads=[scR], writes=[dstR])
        kcnt = [0]
        for o in range(2):
            for tc in range(ntc):
                dec, decR, hf, hfR = DEC[tc % NBF], DECR[tc % NBF], HF[tc % NBF], HFR[tc % NBF]
                P.add("sp", lambda e, tc=tc, dec=dec: e.dma_start(out=dec[:], in_=decay[L][tc * 128:(tc + 1) * 128]),
                      writes=[decR], dma=True)
                for dr_ in range(2):
                    for dh in range(2):
                        c0 = o * 2048 + dr_ * 1024 + dh * 512
                        bk = nb()
                        P.add("pe", lambda e, tc=tc, c0=c0, bk=bk: e.matmul(
                            PB[bk][:], lhsT=h2[0:65, tc * 128:(tc + 1) * 128], rhs=w3[0:65, c0:c0 + 512],
                            start=True, stop=True), reads=[h2R, wR], writes=[PBR[bk]])
                        P.add("dve", lambda e, bk=bk, dr_=dr_, dh=dh, hf=hf, dec=dec: e.tensor_tensor(
                            out=hf[:, dr_, dh * 512:(dh + 1) * 512], in0=PB[bk][:], in1=dec[:, dr_, dh * 512:(dh + 1) * 512],
                            op=ALU.mult), reads=[PBR[bk], decR], writes=[hfR])
                P.add("pool", lambda e, tc=tc, hf=hf: e.tensor_tensor(out=HS[:, tc, 0, :], in0=hf[:, 0, :], in1=hf[:, 1, :],
                                                                     op=ALU.add), reads=[hfR], writes=[HSR[0]])
                P.add("pool", lambda e, tc=tc, hf=hf: e.tensor_tensor(out=HS[:, tc, 1, :], in0=hf[:, 0, :], in1=hf[:, 1, :],
                                                                     op=ALU.subtract), reads=[hfR], writes=[HSR[1]])
            for hh in range(2):
                def cons(i, br, bi, o=o, hh=hh):
                    for part, bk in ((0, br), (1, bi)):
                        k = kcnt[0] % 4; kcnt[0] += 1
                        ko, koR = KO[k], KOR[k]
                        if part == 0:
                            P.add("act", lambda e, ko=ko, bk=bk: e.activation(out=ko[:], in_=PB[bk][:], func=AF.Identity),
                                  reads=[PBR[bk]], writes=[koR])
                        else:
                            P.add("dve", lambda e, ko=ko, bk=bk: e.tensor_copy(out=ko[:], in_=PB[bk][:]),
                                  reads=[PBR[bk]], writes=[koR])
                        r0 = part * Fp + i * 128
                        P.add("act", lambda e, ko=ko, r0=r0: e.dma_start(
                            out=KD[L][o, r0:r0 + 128, hh * 512:(hh + 1) * 512], in_=ko[:]), reads=[koR], writes=[KDR],
                            dma=True)
                fwd_dft(pcf, L, lambda part, tc, hh=hh: HS[:, tc, part, hh * 512:(hh + 1) * 512],
                        lambda part: HSR[part], cons)
        endph(pf)
        pv = P.phase()
        VT = P.psb(pv, "vt", [128, ntiles, D], BF16)
        if own:
            ZFo = P.psb(pv, "zfo", [128, 8, 512], BF16); ZFoR = Res()
        VTR = {(s_, hh): Res() for s_ in range(nseq) for hh in range(2)}
        pi_ = P.phase()
        hT = P.psb(pi_, "hT", [128, 8, ntok], BF16); hTR = Res()
        XT = [P.psb(pi_, "hxt%d" % i, [128, D], F32) for i in range(2)]; XTR = [Res(), Res()]
        UP = [P.psb(pi_, "upad%d" % i, [128, nseq, L + 2], F32) for i in range(2)]; UPR = [Res(), Res()]
        UC = [P.psb(pi_, "uc%d" % i, [128, nseq, L], F32) for i in range(2)]; UCR = [Res(), Res()]
        UCb = [P.psb(pi_, "ucb%d" % i, [128, ntok], BF16) for i in range(2)]; UCbR = [Res(), Res()]
        CW = P.psb(pi_, "cw", [128, 24, 3], F32); CB = P.psb(pi_, "cb", [128, 24], F32); CWR = Res()
        P.add("sp", lambda e: e.dma_start(out=CW[:], in_=hy_cw), writes=[CWR], dma=True)
        P.add("sp", lambda e: e.dma_start(out=CB[:], in_=hy_cb), writes=[CWR], dma=True)
        for k in range(2):
            P.add("dve", lambda e, k=k: e.memset(UP[k][:, :, 0:1], 0.0), writes=[UPR[k]])
            P.add("dve", lambda e, k=k: e.memset(UP[k][:, :, L + 1:L + 2], 0.0), writes=[UPR[k]])
        load_mod(1, 1, row)
        for ti in range(ntiles):
            ldx(XD, XDR, ti, XT[ti % 2], XTR[ti % 2])
            mod_transpose(XT[ti % 2], XTR[ti % 2], hT[:, :, ti * 128:(ti + 1) * 128], hTR)
        nblk = ntok // 512

        def ev_u(mc, blk, n, bk):
            k = mc % 2
            up, upR, uc, ucR, ucb, ucbR = UP[k], UPR[k], UC[k], UCR[k], UCb[k], UCbR[k]
            if L >= 512:
                s_ = (blk * 512) // L; off = (blk * 512) % L
                P.add("act", lambda e: e.activation(out=up[:, s_, 1 + off:1 + off + 512], in_=PB[bk][:],
                                                    func=AF.Identity), reads=[PBR[bk]], writes=[upR])
            else:
                ns = 512 // L
                P.add("act", lambda e: e.activation(out=up[:, blk * ns:(blk + 1) * ns, 1:L + 1],
                                                    in_=PB[bk][:].rearrange("p (s t) -> p s t", s=ns),
                                                    func=AF.Identity), reads=[PBR[bk]], writes=[upR])
            if blk != nblk - 1:
                return
            P.add("dve", lambda e: e.tensor_scalar(out=uc[:], in0=up[:, :, 0:L], scalar1=CW[:, mc, 0:1],
                                                   scalar2=CB[:, mc:mc + 1], op0=ALU.mult, op1=ALU.add),
                  reads=[upR, CWR], writes=[ucR])
            P.add("dve", lambda e: e.scalar_tensor_tensor(out=uc[:], in0=up[:, :, 1:L + 1], scalar=CW[:, mc, 1:2],
                                                          in1=uc[:], op0=ALU.mult, op1=ALU.add),
                  reads=[upR, CWR, ucR], writes=[ucR])
            P.add("dve", lambda e: e.scalar_tensor_tensor(out=uc[:], in0=up[:, :, 2:L + 2], scalar=CW[:, mc, 2:3],
                                                          in1=uc[:], op0=ALU.mult, op1=ALU.add),
                  reads=[upR, CWR, ucR], writes=[ucR])
            P.add("act", lambda e: e.dma_start(out=UD[mc // 8, (mc % 8) * 128:(mc % 8 + 1) * 128, 0:ntok],
                                              in_=uc[:].rearrange("p s t -> p (s t)")),
                  reads=[ucR], writes=[UDR], dma=True)
            if own and mc >= 16:
                for tb_ in range(4):
                    P.add("act", lambda e, tb_=tb_: e.dma_start(
                        out=U2b[tb_ * D + (mc - 16) * 128:tb_ * D + (mc - 15) * 128, :],
                        in_=uc[:, 0, tb_ * 512:(tb_ + 1) * 512]), reads=[ucR], writes=[U2bR], dma=True)
            for fn_ in pend:
                fn_()
            del pend[:]
            if mc < 8:
                P.add("act", lambda e: e.activation(out=ucb[:], in_=uc[:].rearrange("p s t -> p (s t)"),
                                                    func=AF.Identity), reads=[ucR], writes=[ucbR])

                def tp(mc=mc, ucb=ucb, ucbR=ucbR):
                    for g in range(ntiles // 8):
                        seqs = sorted(set((g * 8 + q) // ntc for q in range(8)))
                        transpose_tile(ucb[:, g * 1024:(g + 1) * 1024], ucbR,
                                       VT[:, g * 8:(g + 1) * 8, mc * 128:(mc + 1) * 128],
                                       [VTR[(s_, mc // 4)] for s_ in seqs], nchunk=8)
                pend.append(tp)
        pend = []
        proj_fm(pi_, "hin", hT, hTR, ntok, hy_w_in, 0, 24, ev_u, wt=hy_w_in_t)
        for fn_ in pend:
            fn_()
        endph(pi_)
        pc_ = P.phase()
        nys = 2 if L <= 256 else 1
        YSs = [P.psb(pc_, "ys%d" % i, [128, 2 * nfc, 512], BF16) for i in range(nys)]
        YSRs = [Res() for _ in range(nys)]
        gcnt = [0]
        small = (L <= 256)
        nfb = nfc if small else 2
        pcd = {"FB": [P.psb(pc_, "dfb%d" % i, [128, ntc, 2, 128], BF16) for i in range(nfb)],
               "FBR": [Res() for _ in range(nfb)], "res": small}
        GB = [P.psb(pc_, "gb%d" % i, [128, nfc, TB], BF16) for i in range(2)]; GBR = [Res(), Res()]
        if small:
            for i in range(nfc):
                P.add("sp", lambda e, i=i: e.dma_start(out=pcd["FB"][i][:].rearrange("p a b c -> p (a b c)"),
                                                       in_=fwd[L][i]), writes=[pcd["FBR"][i]], dma=True)
            for gh in range(2):
                P.add("sp", lambda e, gh=gh: e.dma_start(
                    out=GB[gh][:].rearrange("p a b -> p (a b)"),
                    in_=inv[L][0, :, gh * nfc * TB:(gh + 1) * nfc * TB]), writes=[GBR[gh]], dma=True)
            KS = P.psb(pc_, "ksb", [128, 2, 2 * nfc, D], F32); KSR = Res()
            for o in range(2):
                P.add("sp", lambda e, o=o: e.dma_start(out=KS[:, o, :, :],
                                                       in_=KD[L][o].rearrange("(c p) n -> p c n", p=128)),
                      reads=[KDR], writes=[KSR], dma=True)
        KB = [P.psb(pc_, "kb%d" % i, [128, 2, 512], F32) for i in range(2)]; KBR = [Res(), Res()]
        T4 = [P.psb(pc_, "t4%d" % i, [128, 512], F32) for i in range(4)]; T4R = [Res(), Res()]
        EP = [P.psb(pc_, "ep%d" % i, [128, 2, TB], F32) for i in range(2)]; EPR = [Res(), Res()]
        ZT = [P.psb(pc_, "zt%d" % i, [128, TB], F32) for i in range(2)]; ZTR = [Res(), Res()]
        ZB = [P.psb(pc_, "zb%d" % i, [128, TB], BF16) for i in range(2)]; ZBR = [Res(), Res()]
        HB = P.psb(pc_, "hb", [128, 8, 2], F32); HBR = Res()
        P.add("sp", lambda e: e.dma_start(out=HB[:], in_=hy_biasT), writes=[HBR], dma=True)
        cnt = [0]
        if own:
            IDC = P.psb(pc_, "idc", [128, 8], I32); IDCR = Res()
            P.add("sp", lambda e: e.dma_start(out=IDC[:], in_=idx_ch_d), writes=[IDCR], dma=True)
        groups = [(o, s_, hh) for o in range(2) for s_ in range(nseq) for hh in range(2)]
        def group_body(gi, o, s_, hh):
            for _once in (0,):
                for _once2 in (0,):
                    YS, YSR = YSs[gi % nys], YSRs[gi % nys]

                    def consY(i, br, bi, o=o, hh=hh, YS=YS, YSR=YSR):
                        if small:
                            kre = KS[:, o, i, hh * 512:(hh + 1) * 512]
                            kim = KS[:, o, nfc + i, hh * 512:(hh + 1) * 512]
                            kbR = KSR
                        else:
                            kb, kbR = KB[i % 2], KBR[i % 2]
                            kre, kim = kb[:, 0, :], kb[:, 1, :]
                            for part in range(2):
                                r0 = part * Fp + i * 128
                                P.add("sp", lambda e, kb=kb, part=part, r0=r0: e.dma_start(
                                    out=kb[:, part, :], in_=KD[L][o, r0:r0 + 128, hh * 512:(hh + 1) * 512]),
                                    reads=[KDR], writes=[kbR], dma=True)
                        P.add("dve", lambda e: e.tensor_tensor(out=T4[0][:], in0=PB[br][:], in1=kre, op=ALU.mult),
                              reads=[PBR[br], kbR], writes=[T4R[0]])
                        P.add("dve", lambda e: e.tensor_tensor(out=T4[1][:], in0=PB[bi][:], in1=kim, op=ALU.mult),
                              reads=[PBR[bi], kbR], writes=[T4R[0]])
                        P.add("pool", lambda e: e.tensor_tensor(out=YS[:, i, :], in0=T4[0][:], in1=T4[1][:],
                                                                op=ALU.subtract), reads=[T4R[0]], writes=[YSR])
                        P.add("dve", lambda e: e.tensor_tensor(out=T4[2][:], in0=PB[br][:], in1=kim, op=ALU.mult),
                              reads=[PBR[br], kbR], writes=[T4R[1]])
                        P.add("dve", lambda e: e.tensor_tensor(out=T4[3][:], in0=PB[bi][:], in1=kre, op=ALU.mult),
                              reads=[PBR[bi], kbR], writes=[T4R[1]])
                        P.add("pool", lambda e: e.tensor_tensor(out=YS[:, nfc + i, :], in0=T4[2][:], in1=T4[3][:],
                                                                op=ALU.add), reads=[T4R[1]], writes=[YSR])
                    fwd_dft(pcd, L, lambda part, tc, s_=s_, hh=hh: VT[:, s_ * ntc + tc, hh * 512:(hh + 1) * 512],
                            lambda part, s_=s_, hh=hh: VTR[(s_, hh)], consY)
                    yield
                    own2 = own and o == 1
                    for tb in range(1 if own2 else ntb):
                        for gh in range(2):
                            if small or (own2 and hh == 1):
                                break
                            gsrc = inv_own if own2 else inv[L][tb]
                            P.add("sp", lambda e, gsrc=gsrc, gh=gh: e.dma_start(
                                out=GB[gh][:].rearrange("p a b -> p (a b)"),
                                in_=gsrc[:, gh * nfc * TB:(gh + 1) * nfc * TB]), writes=[GBR[gh]], dma=True)
                        ibk = [nb() for _ in range(4)]
                        for gh in range(2):
                            for cc in range(4):
                                for f_ in range(nfc):
                                    fc = gh * nfc + f_
                                    P.add("pe", lambda e, fc=fc, f_=f_, gh=gh, cc=cc, bk=ibk[cc], YS=YS: e.matmul(
                                        PB[bk][:, 0:TB], lhsT=YS[:, fc, cc * 128:(cc + 1) * 128], rhs=GB[gh][:, f_, :],
                                        start=(fc == 0), stop=(fc == 2 * nfc - 1)), reads=[YSR, GBR[gh]],
                                        writes=[PBR[ibk[cc]]])
                        for cc in range(4):
                            c = hh * 4 + cc
                            bk = ibk[cc]
                            k = cnt[0] % 2; cnt[0] += 1
                            ep, epR, ztt, zttR, zb, zbR = EP[k], EPR[k], ZT[k], ZTR[k], ZB[k], ZBR[k]
                            vsrc = UD[0] if o == 0 else ZD
                            vsrcR = UDR if o == 0 else ZDR
                            col = s_ * L + tb * TB
                            if own2:
                                P.add("pool", lambda e, ep=ep, c=c: e.indirect_dma_start(
                                    out=ep[:, 0, :], out_offset=None, in_=ZDb[:, :],
                                    in_offset=bass.IndirectOffsetOnAxis(ap=IDC[:, c:c + 1], axis=0)),
                                    reads=[ZDbR, IDCR], writes=[epR], dma=True)
                                P.add("pool", lambda e, ep=ep, c=c: e.indirect_dma_start(
                                    out=ep[:, 1, :], out_offset=None, in_=U2b[:, :],
                                    in_offset=bass.IndirectOffsetOnAxis(ap=IDC[:, c:c + 1], axis=0)),
                                    reads=[U2bR, IDCR], writes=[epR], dma=True)
                            else:
                                P.add("sp", lambda e, ep=ep, vsrc=vsrc, c=c, col=col: e.dma_start(
                                    out=ep[:, 0, :], in_=vsrc[c * 128:(c + 1) * 128, col:col + TB]),
                                    reads=[vsrcR], writes=[epR], dma=True)
                                P.add("sp", lambda e, ep=ep, o=o, c=c, col=col: e.dma_start(
                                    out=ep[:, 1, :], in_=UD[1 + o, c * 128:(c + 1) * 128, col:col + TB]),
                                    reads=[UDR], writes=[epR], dma=True)
                            P.add("dve", lambda e, ep=ep, ztt=ztt, c=c, o=o, bk=bk: e.scalar_tensor_tensor(
                                out=ztt[:], in0=ep[:, 0, :], scalar=HB[:, c, o:o + 1], in1=PB[bk][:, 0:TB],
                                op0=ALU.mult, op1=ALU.add), reads=[epR, HBR, PBR[bk]], writes=[zttR])
                            P.add("pool", lambda e, ep=ep, ztt=ztt: e.tensor_tensor(out=ztt[:], in0=ztt[:], in1=ep[:, 1, :],
                                                                                   op=ALU.mult),
                                  reads=[epR, zttR], writes=[zttR])
                            P.add("act", lambda e, ztt=ztt, zb=zb: e.activation(out=zb[:], in_=ztt[:], func=AF.Identity),
                                  reads=[zttR], writes=[zbR])
                            if o == 0:
                                if own:
                                    P.add("pool", lambda e, ztt=ztt, c=c, tb=tb: e.dma_start(
                                        out=ZDb[tb * D + c * 128:tb * D + (c + 1) * 128, :], in_=ztt[:]),
                                        reads=[zttR], writes=[ZDbR], dma=True)
                                else:
                                    P.add("pool", lambda e, ztt=ztt, c=c, col=col: e.dma_start(
                                        out=ZD[c * 128:(c + 1) * 128, col:col + TB], in_=ztt[:]),
                                        reads=[zttR], writes=[ZDR], dma=True)
                                nch = TB // 128
                                transpose_tile(zb, zbR,
                                               VT[:, s_ * ntc + tb * nch:s_ * ntc + (tb + 1) * nch, c * 128:(c + 1) * 128],
                                               VTR[(s_, hh)], nchunk=nch)
                            elif own:
                                P.add("dve", lambda e, zb=zb, c=c: e.tensor_copy(out=ZFo[:, c, :], in_=zb[:]),
                                      reads=[zbR], writes=[ZFoR])
                            else:
                                P.add("act", lambda e, zb=zb, c=c, col=col: e.dma_start(
                                    out=ZFD[c * 128:(c + 1) * 128, col:col + TB], in_=zb[:]),
                                    reads=[zbR], writes=[ZFDR], dma=True)
        gens = [group_body(gi, *g_) for gi, g_ in enumerate(groups)]
        if small:
            next(gens[0])
            for gi in range(len(gens)):
                if gi + 1 < len(gens):
                    next(gens[gi + 1])
                for _ in gens[gi]:
                    pass
        else:
            for g_ in gens:
                for _ in g_:
                    pass
        endph(pc_)
        if own:
            po = P.phase()
            WOU = P.psb(po, "wou", [128, 8, D], BF16); WOUR = Res()
            XT3 = [P.psb(po, "oxt%d" % i, [128, D], F32) for i in range(2)]; XT3R = [Res(), Res()]
            IDT = P.psb(po, "idt", [128, 4], I32); IDTR = Res()
            P.add("sp", lambda e: e.dma_start(out=IDT[:], in_=idx_tok_d), writes=[IDTR], dma=True)
            P.add("pool", lambda e: e.dma_start(out=WOU[:], in_=hy_w_out.rearrange("(c p) n -> p c n", p=128)),
                  writes=[WOUR], dma=True)
            for ti in range(4):
                xt, xtR = XT3[ti % 2], XT3R[ti % 2]
                P.add("pool", lambda e, xt=xt, ti=ti: e.indirect_dma_start(
                    out=xt[:, :], out_offset=None, in_=XD[:, :],
                    in_offset=bass.IndirectOffsetOnAxis(ap=IDT[:, ti:ti + 1], axis=0)),
                    reads=list(XDR) + [IDTR], writes=[xtR], dma=True)
                for n in range(2):
                    bk = nb()
                    for c in range(8):
                        P.add("pe", lambda e, c=c, ti=ti, n=n, bk=bk: e.matmul(
                            PB[bk][:], lhsT=ZFo[:, c, ti * 128:(ti + 1) * 128], rhs=WOU[:, c, n * 512:(n + 1) * 512],
                            start=(c == 0), stop=(c == 7)), reads=[ZFoR, WOUR], writes=[PBR[bk]])
                    residual_ln(xt, xtR, bk, n, 1.0)
                layer_norm(xt, xtR)
                stx(XDo, XDoR, ti, xt, xtR)
            endph(po)
            pv.close()
            return
        endph(pv)
        po = P.phase()
        ZF = P.psb(po, "zf", [128, 8, ntok], BF16); ZFR = Res()
        WOU = P.psb(po, "wou", [128, 8, D], BF16); WOUR = Res()
        XT3 = [P.psb(po, "oxt%d" % i, [128, D], F32) for i in range(2)]; XT3R = [Res(), Res()]
        P.add("sp", lambda e: e.dma_start(out=ZF[:], in_=ZFD[:, 0:ntok].rearrange("(c p) t -> p c t", p=128)),
              reads=[ZFDR], writes=[ZFR], dma=True)
        P.add("pool", lambda e: e.dma_start(out=WOU[:], in_=hy_w_out.rearrange("(c p) n -> p c n", p=128)),
              writes=[WOUR], dma=True)
        for ti in range(ntiles):
            xt, xtR = XT3[ti % 2], XT3R[ti % 2]
            ldx(XD, XDR, ti, xt, xtR)
            for n in range(2):
                bk = nb()
                for c in range(8):
                    P.add("pe", lambda e, c=c, ti=ti, n=n, bk=bk: e.matmul(
                        PB[bk][:], lhsT=ZF[:, c, ti * 128:(ti + 1) * 128], rhs=WOU[:, c, n * 512:(n + 1) * 512],
                        start=(c == 0), stop=(c == 7)), reads=[ZFR, WOUR], writes=[PBR[bk]])
                residual_ln(xt, xtR, bk, n, 1.0)
            layer_norm(xt, xtR)
            stx(XD, XDR, ti, xt, xtR)
        endph(po)

    def copy_in(src, XD, XDR, ntiles):
        for t in range(ntiles):
            P.add("sp", lambda e, t=t: e.dma_start(out=XD[t * 128:(t + 1) * 128, :], in_=src[t * 128:(t + 1) * 128, :]),
                  writes=[XDR[t]], dma=True)

    mod_chunks(12)

    def out_copy(XD, XDR, dst, ntiles):
        for t in range(ntiles):
            P.add("sp", lambda e, t=t: e.dma_start(out=dst[t * 128:(t + 1) * 128, :], in_=XD[t * 128:(t + 1) * 128, :]),
                  reads=[XDR[t]], dma=True)

    do_p = stage in (2, 4, 99)
    do_s = stage in (3, 5, 99)
    if do_p:
        ffn(XDp, XDpR, 8, 0, 0, 0, 0, modn=1, src=xp)
        mod_upto(24)
        attn_ctx()
        if stage != 2:
            mod_upto(48)
            ffn(XDp, XDpR, 8, 0, 1, 2, 0, modn=1)
            mod_upto(60)
            ffn(XDp, XDpR, 8, 1, 0, 0, 0, modn=1)
            mod_upto(60)
            hyena(XDp, XDpR, 256, 4, 0)
            mod_upto(72)
            ffn(XDp, XDpR, 8, 1, 1, 2, 0, dst=yp)
        else:
            out_copy(XDp, XDpR, yp, 8)
    if do_s:
        mod_upto(72)
        ffn(XDs, XDsR, 16, 0, 0, 0, 1, src=xs)
        attn_lat()
        if stage != 3:
            ffn(XDs, XDsR, 16, 0, 1, 2, 1)
            ffn(XDs, XDsR, 16, 1, 0, 0, 1)
            hyena(XDs, XDsR, 2048, 1, 1, own=True)
            ffn(XDo, XDoR, 4, 1, 1, 2, 1, dst=yso)
        else:
            out_copy(XDs, XDsR, ys, 16)
    P.emit()
    return nc

def _bf(a):
    return np.asarray(a, dtype=np.float32).astype(ml_dtypes.bfloat16)


def prep_inputs(inp):
    f = lambda k: np.ascontiguousarray(np.asarray(inp[k], dtype=np.float32))
    shared = {
        "mod_w": np.ascontiguousarray(f("mod_w").reshape(2, 8, 128, 36, 256).transpose(0, 3, 2, 1, 4)
                                      ).reshape(2, 36, 128, 8 * 256),
        "mod_b": f("mod_b"), "ln_g": f("ln_g"), "ln_b": f("ln_b"),
        "ffn_w_in": np.ascontiguousarray(f("ffn_w_in").reshape(2, 2, 8, 128, 2, 11, 256).transpose(0, 1, 5, 3, 2, 4, 6)
                                         ).reshape(2, 2, 11, 128, 8 * 2 * 256),
        "ffn_w_out": f("ffn_w_out"),
        "attn_w_qkv_t": np.ascontiguousarray(f("attn_w_qkv")[0].reshape(8, 128, 24, 128).transpose(2, 1, 0, 3)
                                             ).reshape(24, 128, 8 * 128),
        "hy_w_in_t": np.ascontiguousarray(f("hy_w_in")[0].reshape(8, 128, 24, 128).transpose(2, 1, 0, 3)
                                          ).reshape(24, 128, 8 * 128),
        "ident": _bf(np.eye(128)),
        "attn_w_qkv": f("attn_w_qkv")[0], "attn_w_o": f("attn_w_o")[0],
        "btab": make_btab(inp["attn_rpb"][0]),
    }
    shared.update(hyena_consts(inp))
    xpa = f("x_prompt"); xsa = f("x_sample"); cka = f("cache_k"); cva = f("cache_v")
    ca = f("c"); cctx = f("c_ctx")
    maps = []
    for c in range(8):
        b = c // 4
        cond = np.stack([cctx, ca[b]], 0)
        condT = np.ascontiguousarray(cond.reshape(2, 8, 128).transpose(2, 0, 1))
        m = dict(shared)
        m.update({
            "xp": np.ascontiguousarray(xpa[4 * c:4 * c + 4].reshape(1024, 1024)),
            "xs": np.ascontiguousarray(xsa[b]),
            "ck": np.ascontiguousarray(cka[b, 0].reshape(256, 1024)),
            "cv": np.ascontiguousarray(cva[b, 0].reshape(256, 1024)),
            "condT": condT,
            "idx_tok": np.ascontiguousarray(((c % 4) * 512 + np.arange(4)[None, :] * 128
                                             + np.arange(128)[:, None]).astype(np.int32)),
            "idx_ch": np.ascontiguousarray(((c % 4) * 1024 + np.arange(8)[None, :] * 128
                                            + np.arange(128)[:, None]).astype(np.int32)),
            "inv_own": np.ascontiguousarray(shared["inv2048"][c % 4]),
        })
        maps.append(m)
    return maps


def make_btab(rpb):
    rpb = np.asarray(rpb, dtype=np.float32)
    out = np.empty((16, 4, 2, 64, 8, 8, 64), np.float32)
    a = np.arange(2)[:, None, None]; t = np.arange(8)[None, :, None]; qr = np.arange(8)[None, None, :]
    kcol = np.arange(64)[:, None]; qcol = np.arange(64)[None, :]
    cstart = np.clip(qcol - 8, 0, 48)
    vcol = (kcol >= cstart) & (kcol < cstart + 16)
    dc = np.clip(kcol - qcol + 15, 0, 30)
    for j in range(4):
        hs = min(max(8 * j - 4, 0), 16)
        kr = hs + 2 * t + a
        r = 8 * j + qr
        rs = np.clip(r - 4, 0, 24)
        vrow = (kr >= rs) & (kr < rs + 8)
        dr = np.clip(kr - r + 7, 0, 14)
        val = rpb[:, dr[:, None, :, :, None], dc[None, :, None, None, :]]
        ok = vrow[:, None, :, :, None] & vcol[None, :, None, None, :]
        out[:, j] = np.where(ok[None], val, np.float32(-30000.0))
    return np.ascontiguousarray(out.reshape(16, 4, 128, 8, 512))


def hyena_consts(inp):
    f = lambda k: np.asarray(inp[k], dtype=np.float32)[0]
    cw = f("hy_conv_w"); cb = f("hy_conv_b"); hb = f("hy_bias")
    out = {
        "hy_w_in": np.ascontiguousarray(f("hy_w_in")), "hy_w_out": np.ascontiguousarray(f("hy_w_out")),
        "hy_cw": np.ascontiguousarray(cw.T.reshape(24, 128, 3).transpose(1, 0, 2)),
        "hy_cb": np.ascontiguousarray(cb.reshape(24, 128).T),
        "hy_biasT": np.ascontiguousarray(hb.reshape(2, 8, 128).transpose(2, 1, 0)),
        "hy_w1": np.ascontiguousarray(f("hy_f_w1")), "hy_w2": np.ascontiguousarray(f("hy_f_w2")),
        "hy_w3a": np.ascontiguousarray(np.concatenate([f("hy_f_w3"), f("hy_f_b3")[None, :]], 0)),
        "hy_fb": np.ascontiguousarray(np.stack([f("hy_f_b1"), f("hy_f_freq1"), f("hy_f_b2"), f("hy_f_freq2")], 1)),
    }
    deltas = np.abs(np.linspace(np.log(1e-2) / 1.5, np.log(1e-2) / 0.3, 1024, dtype=np.float32)).astype(np.float64)
    for L in (256, 2048):
        pos = np.arange(L, dtype=np.float64)[:, None]
        t = pos / max(L - 1, 1)
        bands = np.linspace(1e-4, 15, 16, dtype=np.float32).astype(np.float64)
        ang = 2.0 * np.pi * pos / L * bands
        z = np.concatenate([t, np.cos(ang), -np.sin(ang)], -1)
        out["zT%d" % L] = np.ascontiguousarray(z.T.astype(np.float32))
        dec = np.exp(-t * deltas[None, :])
        dec2 = np.stack([dec, dec], 1)
        dec2[0, 1, :] = 0.0
        out["decay%d" % L] = np.ascontiguousarray(dec2.astype(np.float32))
        Fp = 384 if L == 256 else 2176
        N = 2 * L
        fidx = np.arange(Fp, dtype=np.float64)[None, :]
        valid = (fidx <= L)
        th = 2.0 * np.pi * pos * fidx / N
        fw = np.concatenate([np.cos(th) * valid, -np.sin(th) * valid], 1)
        ntc = L // 128; nfc = Fp // 128
        fwt = fw.reshape(ntc, 128, 2, nfc, 128).transpose(3, 1, 0, 2, 4)
        out["fwd%d" % L] = _bf(np.ascontiguousarray(fwt).reshape(nfc, 128, ntc * 2 * 128))
        wf = np.where((fidx == 0) | (fidx == L), 1.0, 2.0) * valid / N
        iv = np.concatenate([(np.cos(th) * wf).T, (-np.sin(th) * wf).T], 0)
        TB = 256 if L == 256 else 512
        ivt = iv.reshape(2 * nfc, 128, L // TB, TB).transpose(2, 1, 0, 3)
        out["inv%d" % L] = _bf(np.ascontiguousarray(ivt).reshape(L // TB, 128, 2 * nfc * TB))
    return out


def kernel(**inputs):
    maps = prep_inputs(inputs)
    nc = build(99)
    res = run_bass_kernel_spmd(nc, maps, core_ids=list(range(8)))
    r = res.results
    yp = np.concatenate([np.asarray(r[c]["yp"], np.float32).reshape(4, 256, 1024) for c in range(8)], 0)
    ys = np.stack([np.concatenate([np.asarray(r[4 * b + q]["yso"], np.float32) for q in range(4)], 0)
                   for b in range(2)], 0)
    nk = np.concatenate([np.asarray(r[c]["nk"], np.float32).reshape(4, 1, 256, 16, 64) for c in range(8)], 0)
    nv = np.concatenate([np.asarray(r[c]["nv"], np.float32).reshape(4, 1, 256, 16, 64) for c in range(8)], 0)
    return (yp, ys, nk, nv)
```

```python
import numpy as np
import concourse.bass as bass
import concourse.mybir as mybir
from contextlib import ExitStack

F32 = mybir.dt.float32
BF16 = mybir.dt.bfloat16
I32 = mybir.dt.int32
ALU = mybir.AluOpType
AF = mybir.ActivationFunctionType
AX = mybir.AxisListType

COMPUTE = ("pe", "act", "dve", "pool")
NDMA_SLOTS = 12


class Res:
    __slots__ = ("name", "last_w", "readers")

    def __init__(self, name=""):
        self.name = name
        self.last_w = None
        self.readers = []


class Op:
    __slots__ = ("eng", "idx", "fn", "waits", "needs_inc", "is_dma", "slot", "slot_val", "clock", "cnt")

    def __init__(self, eng, idx, fn, is_dma):
        self.eng = eng
        self.idx = idx
        self.fn = fn
        self.waits = []
        self.needs_inc = False
        self.is_dma = is_dma
        self.slot = None
        self.slot_val = 0
        self.clock = None
        self.cnt = 0


class Prog:
    def __init__(self, nc):
        self.nc = nc
        self.engs = {"pe": nc.tensor, "act": nc.scalar, "dve": nc.vector, "pool": nc.gpsimd, "sp": nc.sync}
        self.ops = {e: [] for e in self.engs}
        self.clock = {e: {f: -1 for f in COMPUTE} for e in self.engs}
        self.known_dma = {e: set() for e in self.engs}
        self.slot_last = {e: [None] * NDMA_SLOTS for e in self.engs}
        self.slot_cnt = {e: [0] * NDMA_SLOTS for e in self.engs}
        self.slot_rr = {e: 0 for e in self.engs}
        self.stack = ExitStack()

    def sb(self, name, shape, dt):
        return self.stack.enter_context(self.nc.sbuf_tensor(name, list(shape), dt))

    def ps(self, name, shape, dt=F32):
        return self.stack.enter_context(self.nc.psum_tensor(name, list(shape), dt))

    def add(self, eng, fn, reads=(), writes=(), dma=False):
        op = Op(eng, len(self.ops[eng]), fn, dma)
        deps = []
        for r in reads:
            if r.last_w is not None:
                deps.append(r.last_w)
        for w in writes:
            if w.last_w is not None:
                deps.append(w.last_w)
            deps.extend(w.readers)
        ck = self.clock[eng]
        kd = self.known_dma[eng]
        if dma:
            s = self.slot_rr[eng]
            self.slot_rr[eng] = (s + 1) % NDMA_SLOTS
            prev = self.slot_last[eng][s]
            if prev is not None:
                deps.append(prev)
            self.slot_cnt[eng][s] += 1
            op.slot = s
            op.slot_val = 16 * self.slot_cnt[eng][s]
            self.slot_last[eng][s] = op
        seen = set()
        for d in deps:
            if id(d) in seen:
                continue
            seen.add(id(d))
            if not d.is_dma and d.eng not in COMPUTE:
                continue
            if d.is_dma:
                if id(d) in kd:
                    continue
                kd.add(id(d))
                op.waits.append(d)
                for f in COMPUTE:
                    if d.clock[f] > ck[f]:
                        ck[f] = d.clock[f]
            else:
                if d.eng == eng and eng == "pe":
                    continue
                if ck[d.eng] >= d.idx:
                    continue
                d.needs_inc = True
                op.waits.append(d)
                for f in COMPUTE:
                    if d.clock[f] > ck[f]:
                        ck[f] = d.clock[f]
                if d.idx > ck[d.eng]:
                    ck[d.eng] = d.idx
        op.clock = dict(ck)
        if eng == "pe":
            pass
        self.ops[eng].append(op)
        for r in reads:
            r.readers.append(op)
        for w in writes:
            w.last_w = op
            w.readers = []
        return op


    def phase(self):
        ph = ExitStack()
        return ph

    def psb(self, ph, name, shape, dt):
        self._uid = getattr(self, "_uid", 0) + 1
        return ph.enter_context(self.nc.sbuf_tensor("%s_%d" % (name, self._uid), list(shape), dt))

    def fence(self, tiny):
        if not hasattr(self, "_fres"):
            self._fres = {e: Res("fence_" + e) for e in COMPUTE}
        outstanding = []
        for e in self.engs:
            for s in range(NDMA_SLOTS):
                if self.slot_last[e][s] is not None:
                    outstanding.append(self.slot_last[e][s])
        dres = Res("fence_dma")
        for e in COMPUTE:
            op = self.add(e, tiny[e], writes=[self._fres[e]])
            for d in outstanding:
                if id(d) not in self.known_dma[e]:
                    self.known_dma[e].add(id(d))
                    op.waits.append(d)
        for e in self.engs:
            if e in COMPUTE:
                self.add(e, tiny[e], reads=[self._fres[x] for x in COMPUTE if x != e], writes=[self._fres[e]])
            else:
                op = self.add(e, lambda eng: None, reads=[self._fres[x] for x in COMPUTE])
                for d in outstanding:
                    if id(d) not in self.known_dma[e]:
                        self.known_dma[e].add(id(d))
                        op.waits.append(d)

    def emit(self, final_waits=()):
        nc = self.nc
        st = self.stack
        sems = {e: st.enter_context(nc.semaphore("s_" + e)) for e in COMPUTE}
        dsems = {e: [st.enter_context(nc.semaphore("d_%s%d" % (e, i))) for i in range(NDMA_SLOTS)]
                 for e in self.engs}
        for e in COMPUTE:
            c = 0
            for op in self.ops[e]:
                if op.needs_inc:
                    c += 1
                op.cnt = c
        block = st.enter_context(nc.Block())
        self.n_waits = 0

        def run(ename, eng):
            for op in self.ops[ename]:
                for d in op.waits:
                    if d.is_dma:
                        eng.wait_ge(dsems[d.eng][d.slot], d.slot_val)
                    else:
                        eng.wait_ge(sems[d.eng], d.cnt)
                    self.n_waits += 1
                ins = op.fn(eng)
                if ins is None:
                    continue
                if op.is_dma:
                    ins.then_inc(dsems[ename][op.slot], 16)
                elif op.needs_inc:
                    ins.then_inc(sems[ename], 1)
            for s in range(NDMA_SLOTS):
                last = self.slot_last[ename][s]
                if last is not None:
                    eng.wait_ge(dsems[ename][s], last.slot_val)

        block.tensor(lambda eng: run("pe", eng))
        block.scalar(lambda eng: run("act", eng))
        block.vector(lambda eng: run("dve", eng))
        block.gpsimd(lambda eng: run("pool", eng))
        block.sync(lambda eng: run("sp", eng))
        st.close()

import ml_dtypes
from concourse.bass_utils import run_bass_kernel_spmd

D = 1024
DFF = 2816
NJ = 22
NH = 16
DH = 64
ALPHA_C = float(4.0 ** 0.25)
LN_EPS = 1e-5
NEG = -30000.0
PI = float(np.pi)


def build(stage=99):
    nc = bass.Bass("TRN2", target_bir_lowering=False)
    P = Prog(nc)

    def din(name, shape, dt=F32):
        return nc.dram_tensor(name, list(shape), dt, kind="ExternalInput").ap()

    def dout(name, shape):
        return nc.dram_tensor(name, list(shape), F32, kind="ExternalOutput").ap()

    def dscr(name, shape, dt=F32):
        return nc.dram_tensor(name, list(shape), dt).ap()

    xp = din("xp", [1024, D]); xs = din("xs", [2048, D])
    ck = din("ck", [256, D]); cv = din("cv", [256, D])
    condT = din("condT", [128, 2, 8])
    mod_w = din("mod_w", [2, 36, 128, 8 * 256]); mod_b = din("mod_b", [2, 9 * D])
    ln_g = din("ln_g", [2, 3, D]); ln_b = din("ln_b", [2, 3, D])
    w_in = din("ffn_w_in", [2, 2, NJ // 2, 128, 8 * 2 * 256]); w_out = din("ffn_w_out", [2, 2, DFF, D])
    w_qkv = din("attn_w_qkv", [D, 3 * D]); w_o = din("attn_w_o", [D, D])
    w_qkv_t = din("attn_w_qkv_t", [24, 128, 8 * 128]); hy_w_in_t = din("hy_w_in_t", [24, 128, 8 * 128])
    btab = din("btab", [NH, 4, 128, 8, 512])
    hy_w_in = din("hy_w_in", [D, 3 * D]); hy_w_out = din("hy_w_out", [D, D])
    hy_cw = din("hy_cw", [128, 24, 3]); hy_cb = din("hy_cb", [128, 24])
    hy_biasT = din("hy_biasT", [128, 8, 2])
    hy_w1 = din("hy_w1", [33, 64]); hy_w2 = din("hy_w2", [64, 64]); hy_w3a = din("hy_w3a", [65, 4096])
    hy_fb = din("hy_fb", [64, 4])
    zT = {256: din("zT256", [33, 256]), 2048: din("zT2048", [33, 2048])}
    decay = {256: din("decay256", [256, 2, D]), 2048: din("decay2048", [2048, 2, D])}
    FP = {256: 384, 2048: 2176}
    TBL = {256: 256, 2048: 512}
    fwd = {L: din("fwd%d" % L, [FP[L] // 128, 128, (L // 128) * 2 * 128], BF16) for L in (256, 2048)}
    inv = {L: din("inv%d" % L, [L // TBL[L], 128, 2 * (FP[L] // 128) * TBL[L]], BF16) for L in (256, 2048)}
    ident_d = din("ident", [128, 128], BF16)
    idx_tok_d = din("idx_tok", [128, 4], I32)
    idx_ch_d = din("idx_ch", [128, 8], I32)
    inv_own = din("inv_own", [128, 2 * (FP[2048] // 128) * 512], BF16)
    yso = dout("yso", [512, D])
    XDo = dscr("xdo", [512, D]); XDoR = [Res() for _ in range(4)]
    ZDb = dscr("zdb", [4 * D, 512]); ZDbR = Res()
    U2b = dscr("u2b", [4 * D, 512]); U2bR = Res()
    yp = dout("yp", [1024, D])
    ys = dout("ys", [2048, D]) if stage != 99 else None
    nk = dout("nk", [1024, D]); nv = dout("nv", [1024, D])
    modv = dscr("modv", [2, 3, 2, 3, D]); modvR = Res("modv")
    XDp = dscr("xdp", [1024, D]); XDs = dscr("xds", [2048, D])
    XDpR = [Res() for _ in range(8)]; XDsR = [Res() for _ in range(16)]
    KTD = dscr("ktd", [128, 8, 2304], BF16); KTDR = Res()
    VD = dscr("vd", [2304, D], BF16); VDR = Res()
    UD = dscr("ud", [3, D, 2048]); UDR = Res()
    ZD = dscr("zd", [D, 2048]); ZDR = Res()
    ZFD = dscr("zfd", [D, 2048], BF16); ZFDR = Res()
    KD = {L: dscr("kd%d" % L, [2, 2 * FP[L], D]) for L in (256, 2048)}; KDR = Res()

    MOD = P.sb("mod", [128, 3, D], F32); MODR = Res("mod")
    LNGB = P.sb("lngb", [128, 2, D], F32); LNGBR = Res("lngb")
    ident_bf = P.sb("ident_bf", [128, 128], BF16); identR = Res("ident")
    ones_bf = P.sb("ones_bf", [128, 128], BF16); onesR = Res("ones")
    EPS = P.sb("eps", [128, 1], F32); epsR = Res("eps")
    XB = [P.sb("xb%d" % i, [128, D], BF16) for i in range(2)]; XBR = [Res() for _ in range(2)]
    TMPF = [P.sb("tmpf%d" % i, [128, D], F32) for i in range(2)]; TMPFR = [Res() for _ in range(2)]
    ST = [P.sb("st%d" % i, [128, 2, 6], F32) for i in range(2)]
    MV = [P.sb("mv%d" % i, [128, 4], F32) for i in range(2)]
    STR = [Res() for _ in range(2)]
    SCR = {e: P.sb("scr_" + e, [128, 2], F32) for e in ("act", "dve", "pool")}
    PB = [P.ps("pb%d" % i, [128, 512], F32) for i in range(8)]
    PBR = [Res("pb%d" % i) for i in range(8)]
    rot = {"xb": 0, "st": 0, "pt": 0, "nb": 0}

    def nb():
        rot["nb"] = (rot["nb"] + 1) % 6
        return 2 + rot["nb"]

    P.add("sp", lambda e: e.dma_start(out=ident_bf[:], in_=ident_d), writes=[identR], dma=True)
    P.add("dve", lambda e: e.memset(EPS[:], LN_EPS), writes=[epsR])
    P.add("dve", lambda e: e.memset(ones_bf[:], 1.0), writes=[onesR])
    P.add("act", lambda e: e.activation(out=SCR["act"][:, 0:2], in_=EPS[:, 0:1].to_broadcast([128, 2]),
                                        func=AF.Identity), reads=[epsR])
    tiny = {
        "pe": lambda e: e.matmul(PB[7][0:1, 0:1], lhsT=ident_bf[:, 0:1], rhs=ident_bf[:, 0:1], start=True, stop=True),
        "act": lambda e: e.activation(out=SCR["act"][:, 0:1], in_=SCR["act"][:, 1:2], func=AF.Identity),
        "dve": lambda e: e.memset(SCR["dve"][:, 0:1], 0.0),
        "pool": lambda e: e.memset(SCR["pool"][:, 0:1], 0.0),
    }

    def fence():
        P.add("pe", tiny["pe"], reads=[identR], writes=[PBR[7]])
        P.fence(tiny)

    def endph(ph):
        fence(); ph.close()

    def ldx(XD, XDR, t, buf, bufR):
        P.add("sp", lambda e: e.dma_start(out=buf[:], in_=XD[t * 128:(t + 1) * 128, :]), reads=[XDR[t]],
              writes=[bufR], dma=True)

    def stx(XD, XDR, t, buf, bufR):
        P.add("act", lambda e: e.dma_start(out=XD[t * 128:(t + 1) * 128, :], in_=buf[:]), reads=[bufR],
              writes=[XDR[t]], dma=True)

    cT = P.sb("cT", [128, 2, 8], F32); cTR = Res()
    sT = P.sb("sT", [128, 2, 8], F32)
    s2 = P.sb("s2", [128, 8, 2], BF16); s2R = Res()
    WM = [P.sb("wm%d" % i, [128, 8, 256], BF16) for i in range(2)]; WMR = [Res() for _ in range(2)]
    MRS = [P.sb("mrs%d" % i, [2, 2, 256], F32) for i in range(2)]; MRSR = [Res() for _ in range(2)]
    P.add("sp", lambda e: e.dma_start(out=cT[:], in_=condT), writes=[cTR], dma=True)
    P.add("act", lambda e: e.activation(out=sT[:], in_=cT[:], func=AF.Silu), reads=[cTR], writes=[cTR])
    for r in range(2):
        P.add("dve", lambda e, r=r: e.tensor_copy(out=s2[:, :, r], in_=sT[:, r, :]), reads=[cTR], writes=[s2R])
    mod_tasks = [(l, sb_, ci) for l in range(2) for sb_ in range(3) for ci in range(12)]
    mod_pos = [0]

    def mod_upto(n):
        mod_chunks(max(0, n - mod_pos[0]))

    def mod_chunks(n):
        for _ in range(n):
            if mod_pos[0] >= len(mod_tasks):
                return
            layer, sub, ci = mod_tasks[mod_pos[0]]
            k = mod_pos[0] % 2
            mod_pos[0] += 1
            v, q = ci // 4, ci % 4
            wb, wbR, mr, mrR = WM[k], WMR[k], MRS[k], MRSR[k]
            c0 = (sub * 3 + v) * D + q * 256
            P.add("pool", lambda e, wb=wb, c0=c0, layer=layer: e.dma_start(
                out=wb[:].rearrange("p a b -> p (a b)"), in_=mod_w[layer, c0 // 256]), writes=[wbR], dma=True)
            P.add("sp", lambda e, mr=mr, c0=c0, layer=layer: e.dma_start(
                out=mr[:, 1, :], in_=mod_b[layer, c0:c0 + 256].partition_broadcast(2)), writes=[mrR], dma=True)
            bk = nb()
            for kc in range(8):
                P.add("pe", lambda e, wb=wb, kc=kc, bk=bk: e.matmul(
                    PB[bk][0:2, 0:256], lhsT=s2[:, kc, :], rhs=wb[:, kc, :], start=(kc == 0), stop=(kc == 7)),
                    reads=[s2R, wbR], writes=[PBR[bk]])
            addc = 1.0 if v == 1 else 0.0
            P.add("dve", lambda e, bk=bk, mr=mr, addc=addc: e.scalar_tensor_tensor(
                out=mr[:, 0, :], in0=PB[bk][0:2, 0:256], scalar=addc, in1=mr[:, 1, :], op0=ALU.add, op1=ALU.add),
                reads=[PBR[bk], mrR], writes=[mrR])
            P.add("act", lambda e, mr=mr, layer=layer, sub=sub, v=v, q=q: e.dma_start(
                out=modv[layer, sub, :, v, q * 256:(q + 1) * 256], in_=mr[:, 0, :]),
                reads=[mrR], writes=[modvR], dma=True)

    def load_mod(layer, sub, row):
        P.add("sp", lambda e: e.dma_start(out=MOD[:], in_=modv[layer, sub, row].partition_broadcast(128)),
              reads=[modvR], writes=[MODR], dma=True)
        P.add("act", lambda e: e.dma_start(out=LNGB[:, 0, :], in_=ln_g[layer, sub].partition_broadcast(128)),
              writes=[LNGBR], dma=True)
        P.add("act", lambda e: e.dma_start(out=LNGB[:, 1, :], in_=ln_b[layer, sub].partition_broadcast(128)),
              writes=[LNGBR], dma=True)

    def layer_norm(Xt, XtR):
        k = rot["st"]; rot["st"] ^= 1
        st, mv, sR = ST[k], MV[k], STR[k]
        for c in range(2):
            P.add("dve", lambda e, c=c: e.bn_stats(out=st[:, c, :], in_=Xt[:, c * 512:(c + 1) * 512]),
                  reads=[XtR], writes=[sR])
        P.add("dve", lambda e: e.bn_aggr(out=mv[:, 0:2], in_=st[:]), reads=[sR], writes=[sR])
        P.add("act", lambda e: e.activation(out=mv[:, 2:3], in_=mv[:, 1:2], func=AF.Sqrt, bias=EPS[:, 0:1],
                                            scale=1.0), reads=[sR, epsR], writes=[sR])
        P.add("dve", lambda e: e.reciprocal(out=mv[:, 3:4], in_=mv[:, 2:3]), reads=[sR], writes=[sR])
        P.add("dve", lambda e: e.scalar_tensor_tensor(out=mv[:, 2:3], in0=mv[:, 0:1], scalar=-1.0, in1=mv[:, 3:4],
                                                      op0=ALU.mult, op1=ALU.mult), reads=[sR], writes=[sR])
        P.add("act", lambda e: e.activation(out=Xt[:], in_=Xt[:], func=AF.Identity, scale=mv[:, 3:4],
                                            bias=mv[:, 2:3]), reads=[XtR, sR], writes=[XtR])
        P.add("dve", lambda e: e.tensor_tensor(out=Xt[:], in0=Xt[:], in1=LNGB[:, 0, :], op=ALU.mult),
              reads=[XtR, LNGBR], writes=[XtR])
        P.add("dve", lambda e: e.tensor_tensor(out=Xt[:], in0=Xt[:], in1=LNGB[:, 1, :], op=ALU.add),
              reads=[XtR, LNGBR], writes=[XtR])

    def residual_ln(Xt, XtR, bk, n, gscale):
        k = rot["xb"]; rot["xb"] ^= 1
        tf, tfR = TMPF[k], TMPFR[k]
        P.add("dve", lambda e: e.scalar_tensor_tensor(
            out=tf[:, 0:512], in0=PB[bk][:], scalar=gscale, in1=MOD[:, 2, n * 512:(n + 1) * 512],
            op0=ALU.mult, op1=ALU.mult), reads=[PBR[bk], MODR], writes=[tfR])
        P.add("dve", lambda e: e.scalar_tensor_tensor(
            out=Xt[:, n * 512:(n + 1) * 512], in0=Xt[:, n * 512:(n + 1) * 512], scalar=ALPHA_C,
            in1=tf[:, 0:512], op0=ALU.mult, op1=ALU.add), reads=[XtR, tfR], writes=[XtR])

    def transpose_tile(xb, xbR, dst3, dstR, nchunk=8):
        b = rot["pt"]; rot["pt"] ^= 1
        pt = PB[b][:].bitcast(BF16)
        for c in range(nchunk):
            P.add("pe", lambda e, c=c: e.transpose(pt[:, c * 128:(c + 1) * 128], xb[:, c * 128:(c + 1) * 128],
                                                   ident_bf[:]), reads=[xbR, identR], writes=[PBR[b]])
        P.add("act", lambda e: e.activation(out=dst3, in_=pt[:, 0:nchunk * 128].rearrange("p (c t) -> p c t", c=nchunk),
                                            func=AF.Identity), reads=[PBR[b]],
              writes=(dstR if isinstance(dstR, list) else [dstR]))

    def mod_transpose(Xt, XtR, dst3, dstR):
        k = rot["xb"]; rot["xb"] ^= 1
        tf, tfR, xb, xbR = TMPF[k], TMPFR[k], XB[k], XBR[k]
        P.add("dve", lambda e: e.tensor_tensor(out=tf[:], in0=Xt[:], in1=MOD[:, 1, :], op=ALU.mult),
              reads=[XtR, MODR], writes=[tfR])
        P.add("dve", lambda e: e.tensor_tensor(out=xb[:], in0=tf[:], in1=MOD[:, 0, :], op=ALU.add),
              reads=[tfR, MODR], writes=[xbR])
        transpose_tile(xb, xbR, dst3, dstR)

    def proj_fm(ph, tag, hT, hTR, ntok, wsrc, col0, nchunks, evac, wt=None, wbuf=None):
        if wbuf is not None:
            WB, WBR = wbuf
        else:
            WB = [P.psb(ph, "%s_w%d" % (tag, i), [128, 8, 128], BF16) for i in range(3)]
            WBR = [Res() for _ in range(3)]
        nblk = (ntok + 511) // 512
        for mc in range(nchunks):
            wb, wbR = WB[mc % 3], WBR[mc % 3]
            c0 = col0 + mc * 128
            if wt is not None:
                P.add("pool", lambda e, wb=wb, c0=c0: e.dma_start(
                    out=wb[:].rearrange("p a b -> p (a b)"), in_=wt[c0 // 128]), writes=[wbR], dma=True)
            else:
                P.add("pool", lambda e, wb=wb, c0=c0: e.dma_start(
                    out=wb[:], in_=wsrc[:, c0:c0 + 128].rearrange("(kc p) n -> p kc n", p=128)), writes=[wbR], dma=True)
            for blk in range(nblk):
                n = min(512, ntok - blk * 512)
                bk = nb()
                for kc in range(8):
                    P.add("pe", lambda e, wb=wb, kc=kc, bk=bk, blk=blk, n=n: e.matmul(
                        PB[bk][:, 0:n], lhsT=wb[:, kc, :], rhs=hT[:, kc, blk * 512:blk * 512 + n],
                        start=(kc == 0), stop=(kc == 7)), reads=[wbR, hTR], writes=[PBR[bk]])
                evac(mc, blk, n, bk)

    def proj_tm(ph, tag, hT, hTR, ntiles, wsrc, col0, nn, evac):
        WB = [P.psb(ph, "%s_w%d" % (tag, i), [128, 8, 512], BF16) for i in range(2)]; WBR = [Res() for _ in range(2)]
        for n in range(nn):
            wb, wbR = WB[n % 2], WBR[n % 2]
            c0 = col0 + n * 512
            P.add("pool", lambda e, wb=wb, c0=c0: e.dma_start(
                out=wb[:], in_=wsrc[:, c0:c0 + 512].rearrange("(kc p) n -> p kc n", p=128)), writes=[wbR], dma=True)
            for t in range(ntiles):
                bk = nb()
                for kc in range(8):
                    P.add("pe", lambda e, wb=wb, kc=kc, bk=bk, t=t: e.matmul(
                        PB[bk][:], lhsT=hT[:, kc, t * 128:(t + 1) * 128], rhs=wb[:, kc, :],
                        start=(kc == 0), stop=(kc == 7)), reads=[wbR, hTR], writes=[PBR[bk]])
                evac(n, t, bk)

    def ffn(XD, XDR, ntiles, layer, idx, sub, row, dst=None, modn=0, src=None):
        ph = P.phase()
        XMT = P.psb(ph, "xmt", [128, 8, 1024], BF16); XMTR = [Res(), Res()]
        AT = P.psb(ph, "at", [128, NJ, 1024], BF16); ATR = [Res() for _ in range(NJ)]
        WO = P.psb(ph, "wo", [128, NJ, D], BF16); WOR = [Res(), Res()]
        WI = [P.psb(ph, "wi%d" % i, [128, 8, 2, 256], BF16) for i in range(2)]; WIR = [Res() for _ in range(2)]
        SG = [P.psb(ph, "sg%d" % i, [128, 512], F32) for i in range(2)]; SGR = [Res() for _ in range(2)]
        XT = [P.psb(ph, "xt%d" % i, [128, D], F32) for i in range(8)]; XTR = [Res() for _ in range(8)]
        load_mod(layer, sub, row)
        def load_wo():
            for hf in range(2):
                P.add("pool", lambda e, hf=hf: e.dma_start(
                    out=WO[:, hf * 11:(hf + 1) * 11, :],
                    in_=w_out[layer, idx, hf * 1408:(hf + 1) * 1408, :].rearrange("(j p) n -> p j n", p=128)),
                    writes=[WOR[hf]], dma=True)
        wi_src = w_in[layer, idx]
        tpb = min(8, ntiles)
        nsbk = tpb // 4
        for blk in range(ntiles // tpb):
            for ti in range(tpb):
                t = blk * tpb + ti
                if src is not None:
                    P.add("sp", lambda e, t=t, ti=ti: e.dma_start(out=XT[ti][:], in_=src[t * 128:(t + 1) * 128, :]),
                          writes=[XTR[ti]], dma=True)
                else:
                    ldx(XD, XDR, t, XT[ti], XTR[ti])
                mod_transpose(XT[ti], XTR[ti], XMT[:, :, ti * 128:(ti + 1) * 128], XMTR[ti // 4])
            for jg in range(NJ // 2):
                wb, wbR = WI[jg % 2], WIR[jg % 2]
                P.add("pool", lambda e, wb=wb, jg=jg: e.dma_start(
                    out=wb[:].rearrange("p a b c -> p (a b c)"), in_=wi_src[jg]), writes=[wbR], dma=True)
                for jj in range(2):
                    j = jg * 2 + jj
                    for sbk in range(nsbk):
                        bg, bu = nb(), nb()
                        for gu, bk in ((0, bg), (1, bu)):
                            for kc in range(8):
                                P.add("pe", lambda e, wb=wb, gu=gu, kc=kc, bk=bk, sbk=sbk, jj=jj: e.matmul(
                                    PB[bk][:], lhsT=wb[:, kc, gu, jj * 128:(jj + 1) * 128],
                                    rhs=XMT[:, kc, sbk * 512:(sbk + 1) * 512],
                                    start=(kc == 0), stop=(kc == 7)), reads=[wbR, XMTR[sbk]], writes=[PBR[bk]])
                        sg, sgR = SG[sbk], SGR[sbk]
                        P.add("act", lambda e, sg=sg, bg=bg: e.activation(out=sg[:], in_=PB[bg][:], func=AF.Silu),
                              reads=[PBR[bg]], writes=[sgR])
                        P.add("dve", lambda e, sg=sg, bu=bu, j=j, sbk=sbk: e.tensor_tensor(
                            out=AT[:, j, sbk * 512:(sbk + 1) * 512], in0=sg[:], in1=PB[bu][:], op=ALU.mult),
                            reads=[sgR, PBR[bu]], writes=[ATR[j]])
                    if modn:
                        mod_chunks(modn)
                if blk == 0 and jg == 1:
                    load_wo()
            for ti in range(tpb):
                t = blk * tpb + ti
                for n in range(2):
                    bk = nb()
                    for j in range(NJ):
                        P.add("pe", lambda e, j=j, ti=ti, n=n, bk=bk: e.matmul(
                            PB[bk][:], lhsT=AT[:, j, ti * 128:(ti + 1) * 128], rhs=WO[:, j, n * 512:(n + 1) * 512],
                            start=(j == 0), stop=(j == NJ - 1)),
                            reads=[ATR[j], WOR[j // 11]], writes=[PBR[bk]])
                    residual_ln(XT[ti], XTR[ti], bk, n, 0.5)
                layer_norm(XT[ti], XTR[ti])
                if dst is not None:
                    P.add("act", lambda e, t=t, ti=ti: e.dma_start(out=dst[t * 128:(t + 1) * 128, :], in_=XT[ti][:]),
                          reads=[XTR[ti]], dma=True)
                else:
                    stx(XD, XDR, t, XT[ti], XTR[ti])
        endph(ph)

    def attn_head(qT_ap, kT_fn, nkt, nq, v_fn, bias_fn, ET, ETR, OT_ap, OTR, REC, RECR, rd, part="sp", p0=0):
        per = 512 // nq
        kt = 0
        while kt < nkt and "s" in part:
            g = min(per, nkt - kt)
            bk = nb()
            for i in range(g):
                b_ap = bias_fn(kt + i)
                P.add("pe", lambda e, i=i, kt=kt, bk=bk, b_ap=b_ap: e.matmul(
                    PB[bk][:, i * nq:(i + 1) * nq], lhsT=kT_fn(kt + i), rhs=qT_ap, start=True, stop=(b_ap is None)),
                    reads=rd[0], writes=[PBR[bk]])
                if b_ap is not None:
                    P.add("pe", lambda e, i=i, bk=bk, b_ap=b_ap: e.matmul(
                        PB[bk][:, i * nq:(i + 1) * nq], lhsT=ident_bf[:], rhs=b_ap, start=False, stop=True),
                        reads=rd[0] + [identR], writes=[PBR[bk]])
            P.add("act", lambda e, kt=kt, g=g, bk=bk: e.activation(
                out=ET[:, kt * nq:(kt + g) * nq], in_=PB[bk][:, 0:g * nq], func=AF.Exp), reads=[PBR[bk]], writes=[ETR])
            kt += g
        if "p" not in part:
            return
        bk = nb()
        for i in range(nkt):
            P.add("pe", lambda e, i=i, bk=bk: e.matmul(PB[bk][:, 0:nq], lhsT=v_fn(i), rhs=ET[:, i * nq:(i + 1) * nq],
                                                       start=(i == 0), stop=(i == nkt - 1)),
                  reads=rd[1] + [ETR], writes=[PBR[bk]])
        bs = nb()
        for i in range(nkt):
            P.add("pe", lambda e, i=i, bs=bs: e.matmul(PB[bs][:, 0:nq], lhsT=ones_bf[:, :],
                                                       rhs=ET[:, i * nq:(i + 1) * nq],
                                                       start=(i == 0), stop=(i == nkt - 1)),
                  reads=[onesR, ETR], writes=[PBR[bs]])
        P.add("dve", lambda e, bs=bs: e.reciprocal(out=REC[p0:p0 + 64, 0:nq], in_=PB[bs][p0:p0 + 64, 0:nq]),
              reads=[PBR[bs]], writes=[RECR])
        P.add("dve", lambda e, bk=bk: e.tensor_tensor(out=OT_ap, in0=PB[bk][p0:p0 + 64, 0:nq],
                                                      in1=REC[p0:p0 + 64, 0:nq], op=ALU.mult),
              reads=[PBR[bk], RECR], writes=[OTR])

    def out_proj_heads(XT, XTR, OT, OTR, WOH, WOHR, tcol):
        for n in range(2):
            bk = nb()
            for h in range(NH // 2):
                P.add("pe", lambda e, h=h, n=n, bk=bk: e.matmul(
                    PB[bk][:], lhsT=OT[:, h, tcol * 128:(tcol + 1) * 128], rhs=WOH[:, h, n * 512:(n + 1) * 512],
                    start=(h == 0), stop=(h == NH // 2 - 1)), reads=[OTR, WOHR], writes=[PBR[bk]])
            residual_ln(XT, XTR, bk, n, 1.0)
        layer_norm(XT, XTR)

    def attn_ctx():
        XD, XDR = XDp, XDpR
        ph = P.phase()
        QKT = P.psb(ph, "qkt", [128, 16, 1024], BF16); QKTR = Res()
        QZ = P.psb(ph, "qz", [128, NH, 1024], BF16)
        P.add("pool", lambda e: e.memset(QZ[:], 0.0), writes=[QKTR])
        VB = P.psb(ph, "vb", [128, 8, D], BF16); VBR = Res()
        pa = P.phase()
        hT = P.psb(pa, "hT", [128, 8, 1024], BF16); hTR = Res()
        XT = [P.psb(pa, "axt%d" % i, [128, D], F32) for i in range(2)]; XTR = [Res() for _ in range(2)]
        KV = [P.psb(pa, "kv%d" % i, [128, 512], F32) for i in range(2)]; KVR = [Res() for _ in range(2)]
        load_mod(0, 1, 0)
        for t in range(8):
            ldx(XD, XDR, t, XT[t % 2], XTR[t % 2])
            mod_transpose(XT[t % 2], XTR[t % 2], hT[:, :, t * 128:(t + 1) * 128], hTR)

        def ev_qk(mc, blk, n, bk):
            if mc < 8:
                for hp in range(2):
                    P.add("act", lambda e, hp=hp: e.activation(
                        out=QZ[hp * 64:(hp + 1) * 64, 2 * mc + hp, blk * 512:blk * 512 + n],
                        in_=PB[bk][hp * 64:(hp + 1) * 64, 0:n], func=AF.Identity, scale=0.125),
                        reads=[PBR[bk]], writes=[QKTR])
                return
            P.add("act", lambda e: e.activation(out=QKT[:, mc, blk * 512:blk * 512 + n], in_=PB[bk][:, 0:n],
                                                func=AF.Identity, scale=1.0), reads=[PBR[bk]], writes=[QKTR])
        proj_fm(pa, "qk", hT, hTR, 1024, w_qkv, 0, 16, ev_qk, wt=w_qkv_t)
        cnt = [0]

        def ev_kv(n, t, bk):
            k = cnt[0] % 2; cnt[0] += 1
            kv, kvR = KV[k], KVR[k]
            P.add("act", lambda e: e.activation(out=kv[:], in_=PB[bk][:], func=AF.Identity), reads=[PBR[bk]],
                  writes=[kvR])
            dst = nk if n < 2 else nv
            cc = (n % 2) * 512
            P.add("act", lambda e: e.dma_start(out=dst[t * 128:(t + 1) * 128, cc:cc + 512], in_=kv[:]), reads=[kvR],
                  dma=True)
            if n >= 2:
                P.add("dve", lambda e: e.tensor_copy(out=VB[:, t, cc:cc + 512], in_=kv[:]), reads=[kvR], writes=[VBR])
        proj_tm(pa, "kv", hT, hTR, 8, w_qkv, 1024, 4, ev_kv)
        mod_chunks(4)
        endph(pa)
        pb_ = P.phase()
        WOH = P.psb(pb_, "woh", [128, NH // 2, D], BF16); WOHR = Res()
        P.add("pool", lambda e: e.dma_start(out=WOH[:], in_=w_o.rearrange("(h p) n -> p h n", p=128)), writes=[WOHR],
              dma=True)
        ETs = [P.psb(pb_, "et%d" % i, [128, 512], BF16) for i in range(2)]; ETRs = [Res() for _ in range(2)]
        OT = P.psb(pb_, "ot", [128, NH // 2, 256], BF16); OTR = Res()
        RECs = [P.psb(pb_, "rec%d" % i, [128, 512], F32) for i in range(2)]; RECRs = [Res() for _ in range(2)]
        XT2 = [P.psb(pb_, "bxt%d" % i, [128, D], F32) for i in range(2)]; XT2R = [Res() for _ in range(2)]
        for s in range(4):
            def head_args(h, s=s):
                p0 = (h % 2) * 64
                qT_ap = QZ[:, h, s * 256:(s + 1) * 256]
                kT_fn = lambda kt, h=h, s=s: QKT[:, 8 + h // 2, s * 256 + kt * 128:s * 256 + (kt + 1) * 128]
                v_fn = lambda kt, h=h, s=s: VB[:, s * 2 + kt, (h // 2) * 128:(h // 2 + 1) * 128]
                return (qT_ap, kT_fn, 2, 256, v_fn, lambda kt: None, ETs[h % 2], ETRs[h % 2],
                        OT[p0:p0 + 64, h // 2, :], OTR, RECs[h % 2], RECRs[h % 2], ([QKTR], [VBR]), p0)
            def run_head(h, part):
                a = head_args(h)
                attn_head(*a[:-1], part=part, p0=a[-1])
            run_head(0, "s")
            for h in range(NH):
                if h + 1 < NH:
                    run_head(h + 1, "s")
                run_head(h, "p")
                if h % 2 == 1:
                    mod_chunks(1)
            for tt in range(2):
                t = s * 2 + tt
                ldx(XD, XDR, t, XT2[tt], XT2R[tt])
                out_proj_heads(XT2[tt], XT2R[tt], OT, OTR, WOH, WOHR, tt)
                stx(XD, XDR, t, XT2[tt], XT2R[tt])
        endph(pb_)
        ph.close()

    def attn_lat():
        XD, XDR = XDs, XDsR
        pa = P.phase()
        hT = P.psb(pa, "hT", [128, 8, 2048], BF16); hTR = Res()
        XT = [P.psb(pa, "axt%d" % i, [128, D], F32) for i in range(2)]; XTR = [Res() for _ in range(2)]
        KS = [P.psb(pa, "ks%d" % i, [128, 512], BF16) for i in range(2)]; KSR = [Res() for _ in range(2)]
        load_mod(0, 1, 1)
        for t in range(16):
            ldx(XD, XDR, t, XT[t % 2], XTR[t % 2])
            mod_transpose(XT[t % 2], XTR[t % 2], hT[:, :, t * 128:(t + 1) * 128], hTR)
        cnt = [0]

        def ev_k(mc, blk, n, bk):
            k = cnt[0] % 2; cnt[0] += 1
            ks, ksR = KS[k], KSR[k]
            P.add("act", lambda e: e.activation(out=ks[:, 0:n], in_=PB[bk][:, 0:n], func=AF.Identity),
                  reads=[PBR[bk]], writes=[ksR])
            P.add("act", lambda e: e.dma_start(out=KTD[:, mc, blk * 512:blk * 512 + n], in_=ks[:, 0:n]), reads=[ksR],
                  writes=[KTDR], dma=True)
        proj_fm(pa, "k", hT, hTR, 2048, w_qkv, 1024, 8, ev_k, wt=w_qkv_t)

        def ev_v(n, t, bk):
            k = cnt[0] % 2; cnt[0] += 1
            ks, ksR = KS[k], KSR[k]
            P.add("act", lambda e: e.activation(out=ks[:], in_=PB[bk][:], func=AF.Identity), reads=[PBR[bk]],
                  writes=[ksR])
            P.add("act", lambda e: e.dma_start(out=VD[t * 128:(t + 1) * 128, n * 512:(n + 1) * 512], in_=ks[:]),
                  reads=[ksR], writes=[VDR], dma=True)
        proj_tm(pa, "v", hT, hTR, 16, w_qkv, 2048, 2, ev_v)
        for kt in range(2):
            xb, xbR = XB[kt], XBR[kt]
            P.add("pool", lambda e, kt=kt, xb=xb: e.dma_start(out=xb[:], in_=ck[kt * 128:(kt + 1) * 128, :]),
                  writes=[xbR], dma=True)
            kc_t = P.psb(pa, "kct%d" % kt, [128, 8, 128], BF16); kcR = Res()
            transpose_tile(xb, xbR, kc_t[:], kcR)
            P.add("act", lambda e, kt=kt, kc_t=kc_t: e.dma_start(out=KTD[:, :, 2048 + kt * 128:2048 + (kt + 1) * 128],
                                                                in_=kc_t[:]), reads=[kcR], writes=[KTDR], dma=True)
            vv = P.psb(pa, "cvt%d" % kt, [128, D], BF16); vvR = Res()
            P.add("pool", lambda e, kt=kt, vv=vv: e.dma_start(out=vv[:], in_=cv[kt * 128:(kt + 1) * 128, :]),
                  writes=[vvR], dma=True)
            P.add("sp", lambda e, kt=kt, vv=vv: e.dma_start(out=VD[2048 + kt * 128:2048 + (kt + 1) * 128, :], in_=vv[:]),
                  reads=[vvR], writes=[VDR], dma=True)
        endph(pa)
        pb_ = P.phase()
        WOH = P.psb(pb_, "woh", [128, NH // 2, D], BF16); WOHR = Res()
        P.add("pool", lambda e: e.dma_start(out=WOH[:], in_=w_o.rearrange("(h p) n -> p h n", p=128)), writes=[WOHR],
              dma=True)
        hTb = P.psb(pb_, "hTb", [128, 8, 512], BF16); hTbR = Res()
        QT = P.psb(pb_, "qt", [128, NH, 512], BF16); QTR = Res()
        P.add("pool", lambda e: e.memset(QT[:], 0.0), writes=[QTR])
        KTh = P.psb(pb_, "kth", [128, 8, 1280], BF16); KThR = Res()
        VBh = P.psb(pb_, "vbh", [128, 10, D], BF16); VBhR = Res()
        BT = [P.psb(pb_, "bt%d" % i, [128, 8, 512], BF16) for i in range(2)]; BTR = [Res() for _ in range(2)]
        ETs = [P.psb(pb_, "et%d" % i, [128, 10 * 512], BF16) for i in range(2)]; ETRs = [Res() for _ in range(2)]
        OT = P.psb(pb_, "ot", [128, NH // 2, 512], BF16); OTR = Res()
        RECs = [P.psb(pb_, "rec%d" % i, [128, 512], F32) for i in range(2)]; RECRs = [Res() for _ in range(2)]
        XT2 = [P.psb(pb_, "bxt%d" % i, [128, D], F32) for i in range(2)]; XT2R = [Res() for _ in range(2)]
        QW = ([P.psb(pb_, "qw%d" % i, [128, 8, 128], BF16) for i in range(3)], [Res() for _ in range(3)])

        def prep(j):
            hs = min(max(8 * j - 4, 0), 16)
            tk0 = hs * 64
            for ti in range(4):
                t = j * 4 + ti
                ldx(XD, XDR, t, XT2[ti % 2], XT2R[ti % 2])
                mod_transpose(XT2[ti % 2], XT2R[ti % 2], hTb[:, :, ti * 128:(ti + 1) * 128], hTbR)

            def ev_q(mc, blk, n, bk):
                for hp in range(2):
                    P.add("act", lambda e, hp=hp: e.activation(
                        out=QT[hp * 64:(hp + 1) * 64, 2 * mc + hp, :], in_=PB[bk][hp * 64:(hp + 1) * 64, :],
                        func=AF.Identity, scale=0.125), reads=[PBR[bk]], writes=[QTR])
            proj_fm(pb_, "q", hTb, hTbR, 512, w_qkv, 0, 8, ev_q, wt=w_qkv_t, wbuf=QW)
            P.add("sp", lambda e, tk0=tk0: e.dma_start(out=KTh[:, :, 0:1024], in_=KTD[:, :, tk0:tk0 + 1024]),
                  reads=[KTDR], writes=[KThR], dma=True)
            P.add("sp", lambda e, tk0=tk0: e.dma_start(
                out=VBh[:, 0:8, :], in_=VD[tk0:tk0 + 1024, :].rearrange("(t p) n -> p t n", p=128)),
                reads=[VDR], writes=[VBhR], dma=True)
            if j == 0:
                P.add("sp", lambda e: e.dma_start(out=KTh[:, :, 1024:1280], in_=KTD[:, :, 2048:2304]),
                      reads=[KTDR], writes=[KThR], dma=True)
                P.add("sp", lambda e: e.dma_start(
                    out=VBh[:, 8:10, :], in_=VD[2048:2304, :].rearrange("(t p) n -> p t n", p=128)),
                    reads=[VDR], writes=[VBhR], dma=True)

        prep(0)
        for j in range(4):
            hs = min(max(8 * j - 4, 0), 16)
            act_t = [t_ for t_ in range(8) if any(
                min(max(8 * j + qr - 4, 0), 24) <= hs + 2 * t_ + a_ < min(max(8 * j + qr - 4, 0), 24) + 8
                for qr in range(8) for a_ in range(2))] + [8, 9]

            def head_args(h, j=j, act_t=act_t):
                bt, btR = BT[h % 2], BTR[h % 2]
                p0 = (h % 2) * 64
                qT_ap = QT[:, h, :]
                kT_fn2 = lambda i, h=h: KTh[:, h // 2, act_t[i] * 128:(act_t[i] + 1) * 128]
                v_fn2 = lambda i, h=h: VBh[:, act_t[i], (h // 2) * 128:(h // 2 + 1) * 128]
                bias_fn2 = lambda i, bt=bt: (bt[:, act_t[i], :] if act_t[i] < 8 else None)
                return (qT_ap, kT_fn2, len(act_t), 512, v_fn2, bias_fn2, ETs[h % 2], ETRs[h % 2],
                        OT[p0:p0 + 64, h // 2, :], OTR, RECs[h % 2], RECRs[h % 2], ([QTR, KThR, btR], [VBhR]), p0)

            def load_bt(h, j=j):
                bt, btR = BT[h % 2], BTR[h % 2]
                P.add("pool", lambda e, bt=bt, h=h, j=j: e.dma_start(out=bt[:], in_=btab[h, j]), writes=[btR], dma=True)
            def run_head(h, part):
                a = head_args(h)
                attn_head(*a[:-1], part=part, p0=a[-1])
            load_bt(0)
            run_head(0, "s")
            for h in range(NH):
                if h + 1 < NH:
                    load_bt(h + 1)
                    run_head(h + 1, "s")
                run_head(h, "p")
            if j + 1 < 4:
                prep(j + 1)
            for ti in range(4):
                t = j * 4 + ti
                ldx(XD, XDR, t, XT2[ti % 2], XT2R[ti % 2])
                out_proj_heads(XT2[ti % 2], XT2R[ti % 2], OT, OTR, WOH, WOHR, ti)
                stx(XD, XDR, t, XT2[ti % 2], XT2R[ti % 2])
        endph(pb_)

    def fwd_dft(pc, L, rhs_fn, rdR_fn, consume):
        ntc = L // 128; nfc = FP[L] // 128
        FB = pc["FB"]; FBR = pc["FBR"]
        res_f = pc.get("res", False)
        for i in range(nfc):
            if res_f:
                fb_, fbR = FB[i], FBR[i]
            else:
                fb_, fbR = FB[i % 2], FBR[i % 2]
                P.add("sp", lambda e, fb_=fb_, i=i: e.dma_start(out=fb_[:].rearrange("p a b c -> p (a b c)"),
                                                                in_=fwd[L][i]), writes=[fbR], dma=True)
            br, bi = nb(), nb()
            for part, bk in ((0, br), (1, bi)):
                for tc in range(ntc):
                    P.add("pe", lambda e, fb_=fb_, part=part, tc=tc, bk=bk: e.matmul(
                        PB[bk][:], lhsT=fb_[:, tc, part, :], rhs=rhs_fn(part, tc),
                        start=(tc == 0), stop=(tc == ntc - 1)), reads=[fbR, rdR_fn(part)], writes=[PBR[bk]])
            consume(i, br, bi)

    def hyena(XD, XDR, L, nseq, row, own=False):
        ntc = L // 128
        ntok = nseq * L; ntiles = ntok // 128
        Fp = FP[L]; nfc = Fp // 128
        TB = TBL[L]; ntb = L // TB
        pf = P.phase()
        zt = P.psb(pf, "zt", [33, L], F32); ztR = Res()
        w1 = P.psb(pf, "w1", [33, 64], F32); w2 = P.psb(pf, "w2", [64, 64], F32); wR = Res()
        w3 = P.psb(pf, "w3", [65, 4096], F32)
        fb = P.psb(pf, "fb", [64, 6], F32); fbR = Res()
        h1 = P.psb(pf, "h1", [64, L], F32); h1R = Res()
        h2 = P.psb(pf, "h2", [65, L], F32); h2R = Res()
        sc1 = P.psb(pf, "sc1", [64, 512], F32); sc2 = P.psb(pf, "sc2", [64, 512], F32); scR = Res()
        HS = P.psb(pf, "hs", [128, ntc, 2, D], BF16); HSR = [Res(), Res()]
        NBF = 2 if L <= 256 else 1
        DEC = [P.psb(pf, "dec%d" % i, [128, 2, D], F32) for i in range(NBF)]; DECR = [Res() for _ in range(NBF)]
        HF = [P.psb(pf, "hf%d" % i, [128, 2, D], F32) for i in range(NBF)]; HFR = [Res() for _ in range(NBF)]
        KO = [P.psb(pf, "ko%d" % i, [128, 512], F32) for i in range(4)]; KOR = [Res() for _ in range(4)]
        pcf = {"FB": [P.psb(pf, "ffb%d" % i, [128, ntc, 2, 128], BF16) for i in range(2)], "FBR": [Res(), Res()]}
        P.add("sp", lambda e: e.dma_start(out=zt[:], in_=zT[L]), writes=[ztR], dma=True)
        P.add("sp", lambda e: e.dma_start(out=w1[:], in_=hy_w1), writes=[wR], dma=True)
        P.add("sp", lambda e: e.dma_start(out=w2[:], in_=hy_w2), writes=[wR], dma=True)
        P.add("sp", lambda e: e.dma_start(out=w3[:], in_=hy_w3a), writes=[wR], dma=True)
        P.add("sp", lambda e: e.dma_start(out=fb[:, 0:4], in_=hy_fb), writes=[fbR], dma=True)
        P.add("dve", lambda e: e.tensor_tensor(out=fb[:, 4:5], in0=fb[:, 0:1], in1=fb[:, 1:2], op=ALU.mult),
              reads=[fbR], writes=[fbR])
        P.add("dve", lambda e: e.tensor_tensor(out=fb[:, 5:6], in0=fb[:, 2:3], in1=fb[:, 3:4], op=ALU.mult),
              reads=[fbR], writes=[fbR])
        P.add("dve", lambda e: e.memset(h2[64:65, :], 1.0), writes=[h2R])
        TS = min(L, 512)
        for (wt, kdim, src, srcR, dstt, dstR, fcol, bcol) in (
                (w1, 33, zt, ztR, h1, h1R, 1, 4), (w2, 64, h1, h1R, h2, h2R, 3, 5)):
            for b in range(L // TS):
                bk = nb()
                P.add("pe", lambda e, wt=wt, kdim=kdim, src=src, b=b, bk=bk: e.matmul(
                    PB[bk][0:64, 0:TS], lhsT=wt[0:kdim, :], rhs=src[0:kdim, b * TS:(b + 1) * TS], start=True, stop=True),
                    reads=[wR, srcR], writes=[PBR[bk]])
                P.add("act", lambda e, bk=bk, fcol=fcol, bcol=bcol: e.activation(
                    out=sc1[:, 0:TS], in_=PB[bk][0:64, 0:TS], func=AF.Identity, scale=fb[:, fcol:fcol + 1],
                    bias=fb[:, bcol:bcol + 1]), reads=[PBR[bk], fbR], writes=[scR])
                for _ in range(2):
                    P.add("dve", lambda e: e.tensor_scalar(out=sc2[:, 0:TS], in0=sc1[:, 0:TS], scalar1=-PI,
                                                           scalar2=2 * PI, op0=ALU.is_lt, op1=ALU.mult),
                          reads=[scR], writes=[scR])
                    P.add("dve", lambda e: e.tensor_tensor(out=sc1[:, 0:TS], in0=sc1[:, 0:TS], in1=sc2[:, 0:TS],
                                                           op=ALU.add), reads=[scR], writes=[scR])
                    P.add("dve", lambda e: e.tensor_scalar(out=sc2[:, 0:TS], in0=sc1[:, 0:TS], scalar1=PI,
                                                           scalar2=-2 * PI, op0=ALU.is_gt, op1=ALU.mult),
                          reads=[scR], writes=[scR])
                    P.add("dve", lambda e: e.tensor_tensor(out=sc1[:, 0:TS], in0=sc1[:, 0:TS], in1=sc2[:, 0:TS],
                                                           op=ALU.add), reads=[scR], writes=[scR])
                P.add("act", lambda e, dstt=dstt, b=b: e.activation(out=dstt[0:64, b * TS:(b + 1) * TS], in_=sc1[:, 0:TS],
                                                                    func=AF.Sin), reads=[scR], writes=[dstR])
        kcnt = [0]
        for o in range(2):
            for tc in range(ntc):
                dec, decR, hf, hfR = DEC[tc % NBF], DECR[tc % NBF], HF[tc % NBF], HFR[tc % NBF]
                P.add("sp", lambda e, tc=tc, dec=dec: e.dma_start(out=dec[:], in_=decay[L][tc * 128:(tc + 1) * 128]),
                      writes=[decR], dma=True)
                for dr_ in range(2):
                    for dh in range(2):
                        c0 = o * 2048 + dr_ * 1024 + dh * 512
                        bk = nb()
                        P.add("pe", lambda e, tc=tc, c0=c0, bk=bk: e.matmul(
                            PB[bk][:], lhsT=h2[0:65, tc * 128:(tc + 1) * 128], rhs=w3[0:65, c0:c0 + 512],
                            start=True, stop=True), reads=[h2R, wR], writes=[PBR[bk]])
                        P.add("dve", lambda e, bk=bk, dr_=dr_, dh=dh, hf=hf, dec=dec: e.tensor_tensor(
                            out=hf[:, dr_, dh * 512:(dh + 1) * 512], in0=PB[bk][:], in1=dec[:, dr_, dh * 512:(dh + 1) * 512],
                            op=ALU.mult), reads=[PBR[bk], decR], writes=[hfR])
                P.add("pool", lambda e, tc=tc, hf=hf: e.tensor_tensor(out=HS[:, tc, 0, :], in0=hf[:, 0, :], in1=hf[:, 1, :],
                                                                     op=ALU.add), reads=[hfR], writes=[HSR[0]])
                P.add("pool", lambda e, tc=tc, hf=hf: e.tensor_tensor(out=HS[:, tc, 1, :], in0=hf[:, 0, :], in1=hf[:, 1, :],
                                                                     op=ALU.subtract), reads=[hfR], writes=[HSR[1]])
            for hh in range(2):
                def cons(i, br, bi, o=o, hh=hh):
                    for part, bk in ((0, br), (1, bi)):
                        k = kcnt[0] % 4; kcnt[0] += 1
                        ko, koR = KO[k], KOR[k]
                        if part == 0:
                            P.add("act", lambda e, ko=ko, bk=bk: e.activation(out=ko[:], in_=PB[bk][:], func=AF.Identity),
                                  reads=[PBR[bk]], writes=[koR])
                        else:
                            P.add("dve", lambda e, ko=ko, bk=bk: e.tensor_copy(out=ko[:], in_=PB[bk][:]),
                                  reads=[PBR[bk]], writes=[koR])
                        r0 = part * Fp + i * 128
                        P.add("act", lambda e, ko=ko, r0=r0: e.dma_start(
                            out=KD[L][o, r0:r0 + 128, hh * 512:(hh + 1) * 512], in_=ko[:]), reads=[koR], writes=[KDR],
                            dma=True)
                fwd_dft(pcf, L, lambda part, tc, hh=hh: HS[:, tc, part, hh * 512:(hh + 1) * 512],
                        lambda part: HSR[part], cons)
        endph(pf)
        pv = P.phase()
        VT = P.psb(pv, "vt", [128, ntiles, D], BF16)
        if own:
            ZFo = P.psb(pv, "zfo", [128, 8, 512], BF16); ZFoR = Res()
        VTR = {(s_, hh): Res() for s_ in range(nseq) for hh in range(2)}
        pi_ = P.phase()
        hT = P.psb(pi_, "hT", [128, 8, ntok], BF16); hTR = Res()
        XT = [P.psb(pi_, "hxt%d" % i, [128, D], F32) for i in range(2)]; XTR = [Res(), Res()]
        UP = [P.psb(pi_, "upad%d" % i, [128, nseq, L + 2], F32) for i in range(2)]; UPR = [Res(), Res()]
        UC = [P.psb(pi_, "uc%d" % i, [128, nseq, L], F32) for i in range(2)]; UCR = [Res(), Res()]
        UCb = [P.psb(pi_, "ucb%d" % i, [128, ntok], BF16) for i in range(2)]; UCbR = [Res(), Res()]
        CW = P.psb(pi_, "cw", [128, 24, 3], F32); CB = P.psb(pi_, "cb", [128, 24], F32); CWR = Res()
        P.add("sp", lambda e: e.dma_start(out=CW[:], in_=hy_cw), writes=[CWR], dma=True)
        P.add("sp", lambda e: e.dma_start(out=CB[:], in_=hy_cb), writes=[CWR], dma=True)
        for k in range(2):
            P.add("dve", lambda e, k=k: e.memset(UP[k][:, :, 0:1], 0.0), writes=[UPR[k]])
            P.add("dve", lambda e, k=k: e.memset(UP[k][:, :, L + 1:L + 2], 0.0), writes=[UPR[k]])
        load_mod(1, 1, row)
        for ti in range(ntiles):
            ldx(XD, XDR, ti, XT[ti % 2], XTR[ti % 2])
            mod_transpose(XT[ti % 2], XTR[ti % 2], hT[:, :, ti * 128:(ti + 1) * 128], hTR)
        nblk = ntok // 512

        def ev_u(mc, blk, n, bk):
            k = mc % 2
            up, upR, uc, ucR, ucb, ucbR = UP[k], UPR[k], UC[k], UCR[k], UCb[k], UCbR[k]
            if L >= 512:
                s_ = (blk * 512) // L; off = (blk * 512) % L
                P.add("act", lambda e: e.activation(out=up[:, s_, 1 + off:1 + off + 512], in_=PB[bk][:],
                                                    func=AF.Identity), reads=[PBR[bk]], writes=[upR])
            else:
                ns = 512 // L
                P.add("act", lambda e: e.activation(out=up[:, blk * ns:(blk + 1) * ns, 1:L + 1],
                                                    in_=PB[bk][:].rearrange("p (s t) -> p s t", s=ns),
                                                    func=AF.Identity), reads=[PBR[bk]], writes=[upR])
            if blk != nblk - 1:
                return
            P.add("dve", lambda e: e.tensor_scalar(out=uc[:], in0=up[:, :, 0:L], scalar1=CW[:, mc, 0:1],
                                                   scalar2=CB[:, mc:mc + 1], op0=ALU.mult, op1=ALU.add),
                  reads=[upR, CWR], writes=[ucR])
            P.add("dve", lambda e: e.scalar_tensor_tensor(out=uc[:], in0=up[:, :, 1:L + 1], scalar=CW[:, mc, 1:2],
                                                          in1=uc[:], op0=ALU.mult, op1=ALU.add),
                  reads=[upR, CWR, ucR], writes=[ucR])
            P.add("dve", lambda e: e.scalar_tensor_tensor(out=uc[:], in0=up[:, :, 2:L + 2], scalar=CW[:, mc, 2:3],
                                                          in1=uc[:], op0=ALU.mult, op1=ALU.add),
                  reads=[upR, CWR, ucR], writes=[ucR])
            P.add("act", lambda e: e.dma_start(out=UD[mc // 8, (mc % 8) * 128:(mc % 8 + 1) * 128, 0:ntok],
                                              in_=uc[:].rearrange("p s t -> p (s t)")),
                  reads=[ucR], writes=[UDR], dma=True)
            if own and mc >= 16:
                for tb_ in range(4):
                    P.add("act", lambda e, tb_=tb_: e.dma_start(
                        out=U2b[tb_ * D + (mc - 16) * 128:tb_ * D + (mc - 15) * 128, :],
                        in_=uc[:, 0, tb_ * 512:(tb_ + 1) * 512]), reads=[ucR], writes=[U2bR], dma=True)
            for fn_ in pend:
                fn_()
            del pend[:]
            if mc < 8:
                P.add("act", lambda e: e.activation(out=ucb[:], in_=uc[:].rearrange("p s t -> p (s t)"),
                                                    func=AF.Identity), reads=[ucR], writes=[ucbR])

                def tp(mc=mc, ucb=ucb, ucbR=ucbR):
                    for g in range(ntiles // 8):
                        seqs = sorted(set((g * 8 + q) // ntc for q in range(8)))
                        transpose_tile(ucb[:, g * 1024:(g + 1) * 1024], ucbR,
                                       VT[:, g * 8:(g + 1) * 8, mc * 128:(mc + 1) * 128],
                                       [VTR[(s_, mc // 4)] for s_ in seqs], nchunk=8)
                pend.append(tp)
        pend = []
        proj_fm(pi_, "hin", hT, hTR, ntok, hy_w_in, 0, 24, ev_u, wt=hy_w_in_t)
        for fn_ in pend:
            fn_()
        endph(pi_)
        pc_ = P.phase()
        nys = 2 if L <= 256 else 1
        YSs = [P.psb(pc_, "ys%d" % i, [128, 2 * nfc, 512], BF16) for i in range(nys)]
        YSRs = [Res() for _ in range(nys)]
        gcnt = [0]
        small = (L <= 256)
        nfb = nfc if small else 2
        pcd = {"FB": [P.psb(pc_, "dfb%d" % i, [128, ntc, 2, 128], BF16) for i in range(nfb)],
               "FBR": [Res() for _ in range(nfb)], "res": small}
        GB = [P.psb(pc_, "gb%d" % i, [128, nfc, TB], BF16) for i in range(2)]; GBR = [Res(), Res()]
        if small:
            for i in range(nfc):
                P.add("sp", lambda e, i=i: e.dma_start(out=pcd["FB"][i][:].rearrange("p a b c -> p (a b c)"),
                                                       in_=fwd[L][i]), writes=[pcd["FBR"][i]], dma=True)
            for gh in range(2):
                P.add("sp", lambda e, gh=gh: e.dma_start(
                    out=GB[gh][:].rearrange("p a b -> p (a b)"),
                    in_=inv[L][0, :, gh * nfc * TB:(gh + 1) * nfc * TB]), writes=[GBR[gh]], dma=True)
            KS = P.psb(pc_, "ksb", [128, 2, 2 * nfc, D], F32); KSR = Res()
            for o in range(2):
                P.add("sp", lambda e, o=o: e.dma_start(out=KS[:, o, :, :],
                                                       in_=KD[L][o].rearrange("(c p) n -> p c n", p=128)),
                      reads=[KDR], writes=[KSR], dma=True)
        KB = [P.psb(pc_, "kb%d" % i, [128, 2, 512], F32) for i in range(2)]; KBR = [Res(), Res()]
        T4 = [P.psb(pc_, "t4%d" % i, [128, 512], F32) for i in range(4)]; T4R = [Res(), Res()]
        EP = [P.psb(pc_, "ep%d" % i, [128, 2, TB], F32) for i in range(2)]; EPR = [Res(), Res()]
        ZT = [P.psb(pc_, "zt%d" % i, [128, TB], F32) for i in range(2)]; ZTR = [Res(), Res()]
        ZB = [P.psb(pc_, "zb%d" % i, [128, TB], BF16) for i in range(2)]; ZBR = [Res(), Res()]
        HB = P.psb(pc_, "hb", [128, 8, 2], F32); HBR = Res()
        P.add("sp", lambda e: e.dma_start(out=HB[:], in_=hy_biasT), writes=[HBR], dma=True)
        cnt = [0]
        if own:
            IDC = P.psb(pc_, "idc", [128, 8], I32); IDCR = Res()
            P.add("sp", lambda e: e.dma_start(out=IDC[:], in_=idx_ch_d), writes=[IDCR], dma=True)
        groups = [(o, s_, hh) for o in range(2) for s_ in range(nseq) for hh in range(2)]
        def group_body(gi, o, s_, hh):
            for _once in (0,):
                for _once2 in (0,):
                    YS, YSR = YSs[gi % nys], YSRs[gi % nys]

                    def consY(i, br, bi, o=o, hh=hh, YS=YS, YSR=YSR):
                        if small:
                            kre = KS[:, o, i, hh * 512:(hh + 1) * 512]
                            kim = KS[:, o, nfc + i, hh * 512:(hh + 1) * 512]
                            kbR = KSR
                        else:
                            kb, kbR = KB[i % 2], KBR[i % 2]
                            kre, kim = kb[:, 0, :], kb[:, 1, :]
                            for part in range(2):
                                r0 = part * Fp + i * 128
                                P.add("sp", lambda e, kb=kb, part=part, r0=r0: e.dma_start(
                                    out=kb[:, part, :], in_=KD[L][o, r0:r0 + 128, hh * 512:(hh + 1) * 512]),
                                    reads=[KDR], writes=[kbR], dma=True)
                        P.add("dve", lambda e: e.tensor_tensor(out=T4[0][:], in0=PB[br][:], in1=kre, op=ALU.mult),
                              reads=[PBR[br], kbR], writes=[T4R[0]])
                        P.add("dve", lambda e: e.tensor_tensor(out=T4[1][:], in0=PB[bi][:], in1=kim, op=ALU.mult),
                              reads=[PBR[bi], kbR], writes=[T4R[0]])
                        P.add("pool", lambda e: e.tensor_tensor(out=YS[:, i, :], in0=T4[0][:], in1=T4[1][:],
                                                                op=ALU.subtract), reads=[T4R[0]], writes=[YSR])
                        P.add("dve", lambda e: e.tensor_tensor(out=T4[2][:], in0=PB[br][:], in1=kim, op=ALU.mult),
                              reads=[PBR[br], kbR], writes=[T4R[1]])
                        P.add("dve", lambda e: e.tensor_tensor(out=T4[3][:], in0=PB[bi][:], in1=kre, op=ALU.mult),
                              reads=[PBR[bi], kbR], writes=[T4R[1]])
                        P.add("pool", lambda e: e.tensor_tensor(out=YS[:, nfc + i, :], in0=T4[2][:], in1=T4[3][:],
                                                                op=ALU.add), reads=[T4R[1]], writes=[YSR])
                    fwd_dft(pcd, L, lambda part, tc, s_=s_, hh=hh: VT[:, s_ * ntc + tc, hh * 512:(hh + 1) * 512],
                            lambda part, s_=s_, hh=hh: VTR[(s_, hh)], consY)
                    yield
                    own2 = own and o == 1
                    for tb in range(1 if own2 else ntb):
                        for gh in range(2):
                            if small or (own2 and hh == 1):
                                break
                            gsrc = inv_own if own2 else inv[L][tb]
                            P.add("sp", lambda e, gsrc=gsrc, gh=gh: e.dma_start(
                                out=GB[gh][:].rearrange("p a b -> p (a b)"),
                                in_=gsrc[:, gh * nfc * TB:(gh + 1) * nfc * TB]), writes=[GBR[gh]], dma=True)
                        ibk = [nb() for _ in range(4)]
                        for gh in range(2):
                            for cc in range(4):
                                for f_ in range(nfc):
                                    fc = gh * nfc + f_
                                    P.add("pe", lambda e, fc=fc, f_=f_, gh=gh, cc=cc, bk=ibk[cc], YS=YS: e.matmul(
                                        PB[bk][:, 0:TB], lhsT=YS[:, fc, cc * 128:(cc + 1) * 128], rhs=GB[gh][:, f_, :],
                                        start=(fc == 0), stop=(fc == 2 * nfc - 1)), reads=[YSR, GBR[gh]],
                                        writes=[PBR[ibk[cc]]])
                        for cc in range(4):
                            c = hh * 4 + cc
                            bk = ibk[cc]
                            k = cnt[0] % 2; cnt[0] += 1
                            ep, epR, ztt, zttR, zb, zbR = EP[k], EPR[k], ZT[k], ZTR[k], ZB[k], ZBR[k]
                            vsrc = UD[0] if o == 0 else ZD
                            vsrcR = UDR if o == 0 else ZDR
                            col = s_ * L + tb * TB
                            if own2:
                                P.add("pool", lambda e, ep=ep, c=c: e.indirect_dma_start(
                                    out=ep[:, 0, :], out_offset=None, in_=ZDb[:, :],
                                    in_offset=bass.IndirectOffsetOnAxis(ap=IDC[:, c:c + 1], axis=0)),
                                    reads=[ZDbR, IDCR], writes=[epR], dma=True)
                                P.add("pool", lambda e, ep=ep, c=c: e.indirect_dma_start(
                                    out=ep[:, 1, :], out_offset=None, in_=U2b[:, :],
                                    in_offset=bass.IndirectOffsetOnAxis(ap=IDC[:, c:c + 1], axis=0)),
                                    reads=[U2bR, IDCR], writes=[epR], dma=True)
                            else:
                                P.add("sp", lambda e, ep=ep, vsrc=vsrc, c=c, col=col: e.dma_start(
                                    out=ep[:, 0, :], in_=vsrc[c * 128:(c + 1) * 128, col:col + TB]),
                                    reads=[vsrcR], writes=[epR], dma=True)
                                P.add("sp", lambda e, ep=ep, o=o, c=c, col=col: e.dma_start(
                                    out=ep[:, 1, :], in_=UD[1 + o, c * 128:(c + 1) * 128, col:col + TB]),
                                    reads=[UDR], writes=[epR], dma=True)
                            P.add("dve", lambda e, ep=ep, ztt=ztt, c=c, o=o, bk=bk: e.scalar_tensor_tensor(
                                out=ztt[:], in0=ep[:, 0, :], scalar=HB[:, c, o:o + 1], in1=PB[bk][:, 0:TB],
                                op0=ALU.mult, op1=ALU.add), reads=[epR, HBR, PBR[bk]], writes=[zttR])
                            P.add("pool", lambda e, ep=ep, ztt=ztt: e.tensor_tensor(out=ztt[:], in0=ztt[:], in1=ep[:, 1, :],
                                                                                   op=ALU.mult),
                                  reads=[epR, zttR], writes=[zttR])
                            P.add("act", lambda e, ztt=ztt, zb=zb: e.activation(out=zb[:], in_=ztt[:], func=AF.Identity),
                                  reads=[zttR], writes=[zbR])
                            if o == 0:
                                if own:
                                    P.add("pool", lambda e, ztt=ztt, c=c, tb=tb: e.dma_start(
                                        out=ZDb[tb * D + c * 128:tb * D + (c + 1) * 128, :], in_=ztt[:]),
                                        reads=[zttR], writes=[ZDbR], dma=True)
                                else:
                                    P.add("pool", lambda e, ztt=ztt, c=c, col=col: e.dma_start(
                                        out=ZD[c * 128:(c + 1) * 128, col:col + TB], in_=ztt[:]),
                                        reads=[zttR], writes=[ZDR], dma=True)
                                nch = TB // 128
                                transpose_tile(zb, zbR,
                                               VT[:, s_ * ntc + tb * nch:s_ * ntc + (tb + 1) * nch, c * 128:(c + 1) * 128],
                                               VTR[(s_, hh)], nchunk=nch)
                            elif own:
                                P.add("dve", lambda e, zb=zb, c=c: e.tensor_copy(out=ZFo[:, c, :], in_=zb[:]),
                                      reads=[zbR], writes=[ZFoR])
                            else:
                                P.add("act", lambda e, zb=zb, c=c, col=col: e.dma_start(
                                    out=ZFD[c * 128:(c + 1) * 128, col:col + TB], in_=zb[:]),
                                    reads=[zbR], writes=[ZFDR], dma=True)
        gens = [group_body(gi, *g_) for gi, g_ in enumerate(groups)]
        if small:
            next(gens[0])
            for gi in range(len(gens)):
                if gi + 1 < len(gens):
                    next(gens[gi + 1])
                for _ in gens[gi]:
                    pass
        else:
            for g_ in gens:
                for _ in g_:
                    pass
        endph(pc_)
        if own:
            po = P.phase()
            WOU = P.psb(po, "wou", [128, 8, D], BF16); WOUR = Res()
            XT3 = [P.psb(po, "oxt%d" % i, [128, D], F32) for i in range(2)]; XT3R = [Res(), Res()]
            IDT = P.psb(po, "idt", [128, 4], I32); IDTR = Res()
            P.add("sp", lambda e: e.dma_start(out=IDT[:], in_=idx_tok_d), writes=[IDTR], dma=True)
            P.add("pool", lambda e: e.dma_start(out=WOU[:], in_=hy_w_out.rearrange("(c p) n -> p c n", p=128)),
                  writes=[WOUR], dma=True)
            for ti in range(4):
                xt, xtR = XT3[ti % 2], XT3R[ti % 2]
                P.add("pool", lambda e, xt=xt, ti=ti: e.indirect_dma_start(
                    out=xt[:, :], out_offset=None, in_=XD[:, :],
                    in_offset=bass.IndirectOffsetOnAxis(ap=IDT[:, ti:ti + 1], axis=0)),
                    reads=list(XDR) + [IDTR], writes=[xtR], dma=True)
                for n in range(2):
                    bk = nb()
                    for c in range(8):
                        P.add("pe", lambda e, c=c, ti=ti, n=n, bk=bk: e.matmul(
                            PB[bk][:], lhsT=ZFo[:, c, ti * 128:(ti + 1) * 128], rhs=WOU[:, c, n * 512:(n + 1) * 512],
                            start=(c == 0), stop=(c == 7)), reads=[ZFoR, WOUR], writes=[PBR[bk]])
                    residual_ln(xt, xtR, bk, n, 1.0)
                layer_norm(xt, xtR)
                stx(XDo, XDoR, ti, xt, xtR)
            endph(po)
            pv.close()
            return
        endph(pv)
        po = P.phase()
        ZF = P.psb(po, "zf", [128, 8, ntok], BF16); ZFR = Res()
        WOU = P.psb(po, "wou", [128, 8, D], BF16); WOUR = Res()
        XT3 = [P.psb(po, "oxt%d" % i, [128, D], F32) for i in range(2)]; XT3R = [Res(), Res()]
        P.add("sp", lambda e: e.dma_start(out=ZF[:], in_=ZFD[:, 0:ntok].rearrange("(c p) t -> p c t", p=128)),
              reads=[ZFDR], writes=[ZFR], dma=True)
        P.add("pool", lambda e: e.dma_start(out=WOU[:], in_=hy_w_out.rearrange("(c p) n -> p c n", p=128)),
              writes=[WOUR], dma=True)
        for ti in range(ntiles):
            xt, xtR = XT3[ti % 2], XT3R[ti % 2]
            ldx(XD, XDR, ti, xt, xtR)
            for n in range(2):
                bk = nb()
                for c in range(8):
                    P.add("pe", lambda e, c=c, ti=ti, n=n, bk=bk: e.matmul(
                        PB[bk][:], lhsT=ZF[:, c, ti * 128:(ti + 1) * 128], rhs=WOU[:, c, n * 512:(n + 1) * 512],
                        start=(c == 0), stop=(c == 7)), reads=[ZFR, WOUR], writes=[PBR[bk]])
                residual_ln(xt, xtR, bk, n, 1.0)
            layer_norm(xt, xtR)
            stx(XD, XDR, ti, xt, xtR)
        endph(po)

    def copy_in(src, XD, XDR, ntiles):
        for t in range(ntiles):
            P.add("sp", lambda e, t=t: e.dma_start(out=XD[t * 128:(t + 1) * 128, :], in_=src[t * 128:(t + 1) * 128, :]),
                  writes=[XDR[t]], dma=True)

    mod_chunks(12)

    def out_copy(XD, XDR, dst, ntiles):
        for t in range(ntiles):
            P.add("sp", lambda e, t=t: e.dma_start(out=dst[t * 128:(t + 1) * 128, :], in_=XD[t * 128:(t + 1) * 128, :]),
                  reads=[XDR[t]], dma=True)

    do_p = stage in (2, 4, 99)
    do_s = stage in (3, 5, 99)
    if do_p:
        ffn(XDp, XDpR, 8, 0, 0, 0, 0, modn=1, src=xp)
        mod_upto(24)
        attn_ctx()
        if stage != 2:
            mod_upto(48)
            ffn(XDp, XDpR, 8, 0, 1, 2, 0, modn=1)
            mod_upto(60)
            ffn(XDp, XDpR, 8, 1, 0, 0, 0, modn=1)
            mod_upto(60)
            hyena(XDp, XDpR, 256, 4, 0)
            mod_upto(72)
            ffn(XDp, XDpR, 8, 1, 1, 2, 0, dst=yp)
        else:
            out_copy(XDp, XDpR, yp, 8)
    if do_s:
        mod_upto(72)
        ffn(XDs, XDsR, 16, 0, 0, 0, 1, src=xs)
        attn_lat()
        if stage != 3:
            ffn(XDs, XDsR, 16, 0, 1, 2, 1)
            ffn(XDs, XDsR, 16, 1, 0, 0, 1)
            hyena(XDs, XDsR, 2048, 1, 1, own=True)
            ffn(XDo, XDoR, 4, 1, 1, 2, 1, dst=yso)
        else:
            out_copy(XDs, XDsR, ys, 16)
    P.emit()
    return nc

def _bf(a):
    return np.asarray(a, dtype=np.float32).astype(ml_dtypes.bfloat16)


def prep_inputs(inp):
    f = lambda k: np.ascontiguousarray(np.asarray(inp[k], dtype=np.float32))
    shared = {
        "mod_w": np.ascontiguousarray(f("mod_w").reshape(2, 8, 128, 36, 256).transpose(0, 3, 2, 1, 4)
                                      ).reshape(2, 36, 128, 8 * 256),
        "mod_b": f("mod_b"), "ln_g": f("ln_g"), "ln_b": f("ln_b"),
        "ffn_w_in": np.ascontiguousarray(f("ffn_w_in").reshape(2, 2, 8, 128, 2, 11, 256).transpose(0, 1, 5, 3, 2, 4, 6)
                                         ).reshape(2, 2, 11, 128, 8 * 2 * 256),
        "ffn_w_out": f("ffn_w_out"),
        "attn_w_qkv_t": np.ascontiguousarray(f("attn_w_qkv")[0].reshape(8, 128, 24, 128).transpose(2, 1, 0, 3)
                                             ).reshape(24, 128, 8 * 128),
        "hy_w_in_t": np.ascontiguousarray(f("hy_w_in")[0].reshape(8, 128, 24, 128).transpose(2, 1, 0, 3)
                                          ).reshape(24, 128, 8 * 128),
        "ident": _bf(np.eye(128)),
        "attn_w_qkv": f("attn_w_qkv")[0], "attn_w_o": f("attn_w_o")[0],
        "btab": make_btab(inp["attn_rpb"][0]),
    }
    shared.update(hyena_consts(inp))
    xpa = f("x_prompt"); xsa = f("x_sample"); cka = f("cache_k"); cva = f("cache_v")
    ca = f("c"); cctx = f("c_ctx")
    maps = []
    for c in range(8):
        b = c // 4
        cond = np.stack([cctx, ca[b]], 0)
        condT = np.ascontiguousarray(cond.reshape(2, 8, 128).transpose(2, 0, 1))
        m = dict(shared)
        m.update({
            "xp": np.ascontiguousarray(xpa[4 * c:4 * c + 4].reshape(1024, 1024)),
            "xs": np.ascontiguousarray(xsa[b]),
            "ck": np.ascontiguousarray(cka[b, 0].reshape(256, 1024)),
            "cv": np.ascontiguousarray(cva[b, 0].reshape(256, 1024)),
            "condT": condT,
            "idx_tok": np.ascontiguousarray(((c % 4) * 512 + np.arange(4)[None, :] * 128
                                             + np.arange(128)[:, None]).astype(np.int32)),
            "idx_ch": np.ascontiguousarray(((c % 4) * 1024 + np.arange(8)[None, :] * 128
                                            + np.arange(128)[:, None]).astype(np.int32)),
            "inv_own": np.ascontiguousarray(shared["inv2048"][c % 4]),
        })
        maps.append(m)
    return maps


def make_btab(rpb):
    rpb = np.asarray(rpb, dtype=np.float32)
    out = np.empty((16, 4, 2, 64, 8, 8, 64), np.float32)
    a = np.arange(2)[:, None, None]; t = np.arange(8)[None, :, None]; qr = np.arange(8)[None, None, :]
    kcol = np.arange(64)[:, None]; qcol = np.arange(64)[None, :]
    cstart = np.clip(qcol - 8, 0, 48)
    vcol = (kcol >= cstart) & (kcol < cstart + 16)
    dc = np.clip(kcol - qcol + 15, 0, 30)
    for j in range(4):
        hs = min(max(8 * j - 4, 0), 16)
        kr = hs + 2 * t + a
        r = 8 * j + qr
        rs = np.clip(r - 4, 0, 24)
        vrow = (kr >= rs) & (kr < rs + 8)
        dr = np.clip(kr - r + 7, 0, 14)
        val = rpb[:, dr[:, None, :, :, None], dc[None, :, None, None, :]]
        ok = vrow[:, None, :, :, None] & vcol[None, :, None, None, :]
        out[:, j] = np.where(ok[None], val, np.float32(-30000.0))
    return np.ascontiguousarray(out.reshape(16, 4, 128, 8, 512))


def hyena_consts(inp):
    f = lambda k: np.asarray(inp[k], dtype=np.float32)[0]
    cw = f("hy_conv_w"); cb = f("hy_conv_b"); hb = f("hy_bias")
    out = {
        "hy_w_in": np.ascontiguousarray(f("hy_w_in")), "hy_w_out": np.ascontiguousarray(f("hy_w_out")),
        "hy_cw": np.ascontiguousarray(cw.T.reshape(24, 128, 3).transpose(1, 0, 2)),
        "hy_cb": np.ascontiguousarray(cb.reshape(24, 128).T),
        "hy_biasT": np.ascontiguousarray(hb.reshape(2, 8, 128).transpose(2, 1, 0)),
        "hy_w1": np.ascontiguousarray(f("hy_f_w1")), "hy_w2": np.ascontiguousarray(f("hy_f_w2")),
        "hy_w3a": np.ascontiguousarray(np.concatenate([f("hy_f_w3"), f("hy_f_b3")[None, :]], 0)),
        "hy_fb": np.ascontiguousarray(np.stack([f("hy_f_b1"), f("hy_f_freq1"), f("hy_f_b2"), f("hy_f_freq2")], 1)),
    }
    deltas = np.abs(np.linspace(np.log(1e-2) / 1.5, np.log(1e-2) / 0.3, 1024, dtype=np.float32)).astype(np.float64)
    for L in (256, 2048):
        pos = np.arange(L, dtype=np.float64)[:, None]
        t = pos / max(L - 1, 1)
        bands = np.linspace(1e-4, 15, 16, dtype=np.float32).astype(np.float64)
        ang = 2.0 * np.pi * pos / L * bands
        z = np.concatenate([t, np.cos(ang), -np.sin(ang)], -1)
        out["zT%d" % L] = np.ascontiguousarray(z.T.astype(np.float32))
        dec = np.exp(-t * deltas[None, :])
        dec2 = np.stack([dec, dec], 1)
        dec2[0, 1, :] = 0.0
        out["decay%d" % L] = np.ascontiguousarray(dec2.astype(np.float32))
        Fp = 384 if L == 256 else 2176
        N = 2 * L
        fidx = np.arange(Fp, dtype=np.float64)[None, :]
        valid = (fidx <= L)
        th = 2.0 * np.pi * pos * fidx / N
        fw = np.concatenate([np.cos(th) * valid, -np.sin(th) * valid], 1)
        ntc = L // 128; nfc = Fp // 128
        fwt = fw.reshape(ntc, 128, 2, nfc, 128).transpose(3, 1, 0, 2, 4)
        out["fwd%d" % L] = _bf(np.ascontiguousarray(fwt).reshape(nfc, 128, ntc * 2 * 128))
        wf = np.where((fidx == 0) | (fidx == L), 1.0, 2.0) * valid / N
        iv = np.concatenate([(np.cos(th) * wf).T, (-np.sin(th) * wf).T], 0)
        TB = 256 if L == 256 else 512
        ivt = iv.reshape(2 * nfc, 128, L // TB, TB).transpose(2, 1, 0, 3)
        out["inv%d" % L] = _bf(np.ascontiguousarray(ivt).reshape(L // TB, 128, 2 * nfc * TB))
    return out


def kernel(**inputs):
    maps = prep_inputs(inputs)
    nc = build(99)
    res = run_bass_kernel_spmd(nc, maps, core_ids=list(range(8)))
    r = res.results
    yp = np.concatenate([np.asarray(r[c]["yp"], np.float32).reshape(4, 256, 1024) for c in range(8)], 0)
    ys = np.stack([np.concatenate([np.asarray(r[4 * b + q]["yso"], np.float32) for q in range(4)], 0)
                   for b in range(2)], 0)
    nk = np.concatenate([np.asarray(r[c]["nk"], np.float32).reshape(4, 1, 256, 16, 64) for c in range(8)], 0)
    nv = np.concatenate([np.asarray(r[c]["nv"], np.float32).reshape(4, 1, 256, 16, 64) for c in range(8)], 0)
    return (yp, ys, nk, nv)
```
